# Optimizing a Trainium2 kernel written in Bass

```python
import math
import jax
import jax.numpy as jnp
from jax import lax
import numpy as np

D_MODEL = 1024
BATCH = 16
SEQ = 2048
DEPTH = 2

GRID_W = 64
CTX_LEN = 256
EPS = 1e-6
CHUNK = 64
F32 = jnp.float32

NA_HEAD_DIM = 64
NA_WIDTH = D_MODEL // 2
NA_HEADS = NA_WIDTH // NA_HEAD_DIM
WIN_ROWS = 8
WIN_COLS = 16
COL_BLOCK = 16

HG_WIDTH = D_MODEL // 2
HG_HEAD_DIM = 128
HG_HEADS = HG_WIDTH // HG_HEAD_DIM

EVEN_IN = 3 * NA_WIDTH + 5 * HG_WIDTH
EVEN_MIX = NA_WIDTH + HG_WIDTH

SSD_INNER = 2 * D_MODEL
SSD_HEAD_DIM = 64
SSD_HEADS = SSD_INNER // SSD_HEAD_DIM
SSD_GROUPS = 4
SSD_STATE = 128
SSD_CONV = 3
SSD_XBC = SSD_INNER + 2 * SSD_GROUPS * SSD_STATE
SSD_IN = SSD_INNER + SSD_XBC + 2 * SSD_HEADS

D_FF = ((8 * D_MODEL // 3 + 127) // 128) * 128
FFN_CONV = 3

N_EVEN = (DEPTH + 1) // 2
N_ODD = DEPTH // 2

kernel_name = 'hybrid_natten_hgrn2_ssd_prefix_dit'


def rms_norm(x, gain):
    xf = x.astype(F32)
    y = xf * lax.rsqrt(jnp.mean(xf * xf, axis=-1, keepdims=True) + EPS)
    return (y * gain.astype(F32)).astype(x.dtype)


def modulate(x, gain, shift, scale):
    return rms_norm(x, gain) * (1 + scale[:, None]) + shift[:, None]


def dwconv_centred(x, w, b):
    k = w.shape[0]
    y = lax.conv_general_dilated(x, w[:, None, :].astype(x.dtype), (1,), [(k // 2, k // 2)],
                                 dimension_numbers=('NWC', 'WIO', 'NWC'),
                                 feature_group_count=x.shape[-1])
    return y + b


def conv_ffn(u, w_up, conv_w, conv_b, w_down):
    a, v = jnp.split(u @ w_up, 2, axis=-1)
    return (jax.nn.gelu(dwconv_centred(a, conv_w, conv_b)) * v) @ w_down


def _chunks(a):
    b, t = a.shape[:2]
    return jnp.moveaxis(a.reshape(b, t // CHUNK, CHUNK, *a.shape[2:]), 1, 0)


def _unchunk(a):
    nc, b, cl = a.shape[:3]
    return jnp.moveaxis(a, 0, 1).reshape(b, nc * cl, *a.shape[3:])


def _causal_mask():
    return jnp.tril(jnp.ones((CHUNK, CHUNK), bool))[None, :, :, None, None]


def gla_scan(q, k, v, log_f, s0):
    mask = _causal_mask()

    def step(state, inp):
        qc, kc, vc, gc = inp
        g = jnp.cumsum(gc, axis=1)
        decay = jnp.exp(jnp.where(mask, g[:, :, None] - g[:, None], -jnp.inf))
        att = jnp.einsum('bthd,btshd,bshd->bhts', qc, decay, kc)
        o = (jnp.einsum('bhts,bshv->bthv', att, vc)
             + jnp.einsum('bthd,bhdv->bthv', qc * jnp.exp(g), state))
        g_last = g[:, -1]
        state = (jnp.exp(g_last)[..., None] * state
                 + jnp.einsum('bshd,bshv->bhdv', kc * jnp.exp(g_last[:, None] - g), vc))
        return state, o

    state, o = lax.scan(step, s0, (_chunks(q), _chunks(k), _chunks(v), _chunks(log_f)))
    return _unchunk(o), state


def ssd_scan(x, dt, la, bm, cm, s0):
    mask = _causal_mask()

    def step(state, inp):
        xc, dtc, lac, bc, cc = inp
        cum = jnp.cumsum(lac, axis=1)
        lmat = jnp.exp(jnp.where(mask, cum[:, :, None] - cum[:, None], -jnp.inf))
        cb = jnp.einsum('btgn,bsgn->btsg', cc, bc)
        y = jnp.einsum('btsgh,btsg,bsghp->btghp', lmat, cb, dtc[..., None] * xc)
        y = y + jnp.einsum('btgn,bghpn->btghp', cc, state) * jnp.exp(cum)[..., None]
        w = jnp.exp(cum[:, -1:] - cum) * dtc
        state = (jnp.exp(cum[:, -1])[..., None, None] * state
                 + jnp.einsum('bsgh,bsghp,bsgn->bghpn', w, xc, bc))
        return state, y

    state, y = lax.scan(step, s0, (_chunks(x), _chunks(dt), _chunks(la), _chunks(bm), _chunks(cm)))
    return _unchunk(y), state


def bidir_prefix(scan_fn, lat_f, lat_b, ctx_f, ctx_b, s0):
    flip = lambda t: tuple(jnp.flip(a, axis=1) for a in t)
    yc_f, sc_f = scan_fn(*ctx_f, s0)
    yc_b, sc_b = scan_fn(*flip(ctx_b), s0)
    y_f, _ = scan_fn(*lat_f, sc_f)
    y_b, _ = scan_fn(*flip(lat_b), sc_b)
    return y_f + jnp.flip(y_b, axis=1), yc_f + jnp.flip(yc_b, axis=1)


def neighbourhood_attention(q, k, v, kc, vc, rpb):
    b, n, h, dh = q.shape
    rows = n // GRID_W
    kr = min(WIN_ROWS, rows)
    n_cb = GRID_W // COL_BLOCK
    band = 2 * WIN_COLS
    scale = dh ** -0.5
    grid = lambda a: a.reshape(b, rows, GRID_W, h, dh)
    q, k, v = grid(q), grid(k), grid(v)
    col = jnp.arange(GRID_W)
    band_start = jnp.clip(jnp.arange(n_cb) * COL_BLOCK - WIN_COLS // 2, 0, GRID_W - band)
    band_cols = band_start[:, None] + jnp.arange(band)
    q_cols = col.reshape(n_cb, COL_BLOCK)
    win_start = jnp.clip(q_cols - WIN_COLS // 2, 0, GRID_W - WIN_COLS)
    kcol = band_cols[:, None, :]
    in_win = (kcol >= win_start[..., None]) & (kcol < win_start[..., None] + WIN_COLS)
    dc_idx = jnp.clip(kcol - q_cols[..., None] + WIN_COLS - 1, 0, 2 * WIN_COLS - 2)
    rpb = rpb.astype(F32)

    def row_block(r):
        r0 = jnp.clip(r - kr // 2, 0, rows - kr)
        qb = lax.dynamic_index_in_dim(q, r, axis=1, keepdims=False).reshape(b, n_cb, COL_BLOCK, h, dh)
        kb = lax.dynamic_slice_in_dim(k, r0, kr, axis=1)[:, :, band_cols]
        vb = lax.dynamic_slice_in_dim(v, r0, kr, axis=1)[:, :, band_cols]
        dr_idx = r0 + jnp.arange(kr) - r + WIN_ROWS - 1
        bias = rpb[:, dr_idx][:, :, dc_idx].transpose(0, 2, 3, 1, 4)
        s_loc = jnp.einsum('bjqhd,bijkhd->bhjqik', qb, kb).astype(F32) * scale + bias
        s_loc = jnp.where(in_win[:, :, None, :], s_loc, -jnp.inf)
        s_ctx = jnp.einsum('bjqhd,blhd->bhjql', qb, kc).astype(F32) * scale
        s = jnp.concatenate([s_loc.reshape(*s_loc.shape[:4], kr * band), s_ctx], axis=-1)
        p = jax.nn.softmax(s, axis=-1).astype(v.dtype)
        p_loc = p[..., :kr * band].reshape(s_loc.shape)
        p_ctx = p[..., kr * band:]
        o = (jnp.einsum('bhjqik,bijkhd->bjqhd', p_loc, vb)
             + jnp.einsum('bhjql,blhd->bjqhd', p_ctx, vc))
        return o.reshape(b, GRID_W, h, dh)

    out = lax.map(row_block, jnp.arange(rows))
    return jnp.moveaxis(out, 0, 1).reshape(b, n, h * dh)


def ctx_attention(q, k, v):
    s = jnp.einsum('blhd,bmhd->bhlm', q, k).astype(F32) * q.shape[-1] ** -0.5
    p = jax.nn.softmax(s, axis=-1).astype(v.dtype)
    o = jnp.einsum('bhlm,bmhd->blhd', p, v)
    return o.reshape(o.shape[0], o.shape[1], -1)


def hgrn2_prep(q, f_f, f_b, i, lb_f, lb_b):
    heads = lambda a: a.astype(F32).reshape(a.shape[0], a.shape[1], HG_HEADS, HG_HEAD_DIM)

    def forget(f, lb):
        log_f = jnp.logaddexp(jnp.log(lb), jnp.log1p(-lb) + jax.nn.log_sigmoid(f.astype(F32)))
        return heads(-jnp.expm1(log_f)), heads(log_f)

    k_f, g_f = forget(f_f, lb_f)
    k_b, g_b = forget(f_b, lb_b)
    return heads(jax.nn.silu(q)), k_f, g_f, k_b, g_b, heads(i)


def hybrid_mixer(u, uc, w_in, w_out, q_gain, k_gain, rpb, hg_gain, lb_f, lb_b, need_ctx):
    split_at = [NA_WIDTH, 2 * NA_WIDTH, 3 * NA_WIDTH] + [3 * NA_WIDTH + m * HG_WIDTH for m in range(1, 5)]
    na_heads = lambda a: a.reshape(a.shape[0], a.shape[1], NA_HEADS, NA_HEAD_DIM)

    def project(v_in):
        qa, ka, va, hq, hff, hfb, hi, hg = jnp.split(v_in @ w_in, split_at, axis=-1)
        qa = rms_norm(na_heads(qa), q_gain)
        ka = rms_norm(na_heads(ka), k_gain)
        return (qa, ka, na_heads(va)), hgrn2_prep(hq, hff, hfb, hi, lb_f, lb_b), hg

    (qa, ka, va), (q, kf, gf, kb, gb, vi), g = project(u)
    (qac, kac, vac), (qc, kfc, gfc, kbc, gbc, vic), gc = project(uc)

    o_na = neighbourhood_attention(qa, ka, va, kac, vac, rpb)
    s0 = jnp.zeros((u.shape[0], HG_HEADS, HG_HEAD_DIM, HG_HEAD_DIM), F32)
    o_hg, oc_hg = bidir_prefix(gla_scan, (q, kf, vi, gf), (q, kb, vi, gb),
                               (qc, kfc, vic, gfc), (qc, kbc, vic, gbc), s0)

    def hg_out(o, gate):
        y = rms_norm(o, hg_gain) * jax.nn.silu(gate.astype(F32)).reshape(o.shape)
        return y.reshape(o.shape[0], o.shape[1], HG_WIDTH)

    y = jnp.concatenate([o_na, hg_out(o_hg, g)], axis=-1) @ w_out
    if not need_ctx:
        return y, None
    yc = jnp.concatenate([ctx_attention(qac, kac, vac), hg_out(oc_hg, gc)], axis=-1) @ w_out
    return y, yc


def ssd_mixer(u, uc, w_in, conv_w, conv_b, dt_bias_f, dt_bias_b, a_log_f, a_log_b, d_skip,
              norm_gain, w_out, need_ctx):
    n_hg = SSD_HEADS // SSD_GROUPS
    a_f = -jnp.exp(a_log_f.astype(F32)).reshape(SSD_GROUPS, n_hg)
    a_b = -jnp.exp(a_log_b.astype(F32)).reshape(SSD_GROUPS, n_hg)

    def project(v_in):
        b, t = v_in.shape[:2]
        z, xbc, dt_f, dt_b = jnp.split(v_in @ w_in, [SSD_INNER, SSD_INNER + SSD_XBC,
                                                     SSD_INNER + SSD_XBC + SSD_HEADS], axis=-1)
        xbc = jax.nn.silu(dwconv_centred(xbc, conv_w, conv_b)).astype(F32)
        xs, bm, cm = jnp.split(xbc, [SSD_INNER, SSD_INNER + SSD_GROUPS * SSD_STATE], axis=-1)
        xs = xs.reshape(b, t, SSD_GROUPS, n_hg, SSD_HEAD_DIM)
        bm = bm.reshape(b, t, SSD_GROUPS, SSD_STATE)
        cm = cm.reshape(b, t, SSD_GROUPS, SSD_STATE)

        def direction(dt_raw, dt_bias, a):
            dt = jax.nn.softplus(dt_raw.astype(F32) + dt_bias.astype(F32)).reshape(b, t, SSD_GROUPS, n_hg)
            return (xs, dt, dt * a, bm, cm)

        return z, xs, direction(dt_f, dt_bias_f, a_f), direction(dt_b, dt_bias_b, a_b)

    z, xs, lat_f, lat_b = project(u)
    zc, xsc, ctx_f, ctx_b = project(uc)
    s0 = jnp.zeros((u.shape[0], SSD_GROUPS, n_hg, SSD_HEAD_DIM, SSD_STATE), F32)
    y, yc = bidir_prefix(ssd_scan, lat_f, lat_b, ctx_f, ctx_b, s0)
    d = d_skip.astype(F32).reshape(SSD_GROUPS, n_hg, 1)

    def finish(yy, xx, zz):
        b, t = zz.shape[:2]
        yy = (yy + d * xx).reshape(b, t, SSD_INNER) * jax.nn.silu(zz.astype(F32))
        yy = rms_norm(yy.reshape(b, t, SSD_GROUPS, -1), norm_gain.reshape(SSD_GROUPS, -1))
        return yy.reshape(b, t, SSD_INNER) @ w_out

    y_lat = finish(y, xs, z)
    if not need_ctx:
        return y_lat, None
    return y_lat, finish(yc, xsc, zc)


def setup_inputs(seed: int = 0) -> dict:
    key = jax.random.key(seed)
    keys = iter(jax.random.split(key, 40))
    D = D_MODEL

    def nrm(shape, std):
        return jax.random.normal(next(keys), shape, F32) * std

    def gain(shape):
        return 1.0 + nrm(shape, 0.02)

    def dt_bias(shape):
        dt = jnp.exp(jax.random.uniform(next(keys), shape, F32, math.log(1e-3), math.log(1e-1)))
        return dt + jnp.log(-jnp.expm1(-dt))

    def a_log(shape):
        return jnp.log(jax.random.uniform(next(keys), shape, F32, 1.0, 16.0))

    return {
        'x': nrm((BATCH, SEQ, D), 1.0),
        'c': nrm((BATCH, D), 1.0),
        'ctx': nrm((BATCH, CTX_LEN, D), 1.0),
        'c_ctx': nrm((D,), 1.0),
        'w_mod': nrm((DEPTH, D, 6 * D), 0.5 * D ** -0.5),
        'b_mod': nrm((DEPTH, 6 * D), 0.02),
        'norm_mix': gain((DEPTH, D)),
        'norm_ffn': gain((DEPTH, D)),
        'ffn_w_up': nrm((DEPTH, D, 2 * D_FF), D ** -0.5),
        'ffn_conv_w': nrm((DEPTH, FFN_CONV, D_FF), FFN_CONV ** -0.5),
        'ffn_conv_b': nrm((DEPTH, D_FF), 0.02),
        'ffn_w_down': nrm((DEPTH, D_FF, D), D_FF ** -0.5),
        'hy_w_in': nrm((N_EVEN, D, EVEN_IN), D ** -0.5),
        'hy_w_out': nrm((N_EVEN, EVEN_MIX, D), EVEN_MIX ** -0.5),
        'na_q_gain': gain((N_EVEN, NA_HEAD_DIM)),
        'na_k_gain': gain((N_EVEN, NA_HEAD_DIM)),
        'na_rpb': nrm((N_EVEN, NA_HEADS, 2 * WIN_ROWS - 1, 2 * WIN_COLS - 1), 0.1),
        'hg_out_gain': gain((N_EVEN, HG_HEAD_DIM)),
        'hg_lb_fwd': nrm((DEPTH + 1, HG_WIDTH), 0.5),
        'hg_lb_bwd': nrm((DEPTH + 1, HG_WIDTH), 0.5),
        'ssd_w_in': nrm((N_ODD, D, SSD_IN), D ** -0.5),
        'ssd_conv_w': nrm((N_ODD, SSD_CONV, SSD_XBC), SSD_CONV ** -0.5),
        'ssd_conv_b': nrm((N_ODD, SSD_XBC), 0.02),
        'ssd_dt_bias_fwd': dt_bias((N_ODD, SSD_HEADS)),
        'ssd_dt_bias_bwd': dt_bias((N_ODD, SSD_HEADS)),
        'ssd_a_log_fwd': a_log((N_ODD, SSD_HEADS)),
        'ssd_a_log_bwd': a_log((N_ODD, SSD_HEADS)),
        'ssd_d': 1.0 + nrm((N_ODD, SSD_HEADS), 0.1),
        'ssd_norm_gain': gain((N_ODD, SSD_INNER)),
        'ssd_w_out': nrm((N_ODD, SSD_INNER, D), SSD_INNER ** -0.5),
    }


def reference(x, c, ctx, c_ctx, w_mod, b_mod, norm_mix, norm_ffn, ffn_w_up, ffn_conv_w, ffn_conv_b,
              ffn_w_down, hy_w_in, hy_w_out, na_q_gain, na_k_gain, na_rpb, hg_out_gain, hg_lb_fwd,
              hg_lb_bwd, ssd_w_in, ssd_conv_w, ssd_conv_b, ssd_dt_bias_fwd, ssd_dt_bias_bwd,
              ssd_a_log_fwd, ssd_a_log_bwd, ssd_d, ssd_norm_gain, ssd_w_out):
    lb_f_all = jnp.cumsum(jax.nn.softmax(hg_lb_fwd.astype(F32), axis=0), axis=0)
    lb_b_all = jnp.cumsum(jax.nn.softmax(hg_lb_bwd.astype(F32), axis=0), axis=0)
    s_lat = jax.nn.silu(c)
    s_ctx = jax.nn.silu(c_ctx)[None]
    h, hc = x, ctx
    for l in range(DEPTH):
        need_ctx = l < DEPTH - 1
        j = l // 2
        sh_m, sc_m, g_m, sh_f, sc_f, g_f = jnp.split(s_lat @ w_mod[l] + b_mod[l], 6, axis=-1)
        csh_m, csc_m, cg_m, csh_f, csc_f, cg_f = jnp.split(s_ctx @ w_mod[l] + b_mod[l], 6, axis=-1)
        u = modulate(h, norm_mix[l], sh_m, sc_m)
        uc = modulate(hc, norm_mix[l], csh_m, csc_m)
        if l % 2 == 0:
            y, yc = hybrid_mixer(u, uc, hy_w_in[j], hy_w_out[j], na_q_gain[j], na_k_gain[j], na_rpb[j],
                                 hg_out_gain[j], lb_f_all[l], lb_b_all[l], need_ctx)
        else:
            y, yc = ssd_mixer(u, uc, ssd_w_in[j], ssd_conv_w[j], ssd_conv_b[j], ssd_dt_bias_fwd[j],
                              ssd_dt_bias_bwd[j], ssd_a_log_fwd[j], ssd_a_log_bwd[j], ssd_d[j],
                              ssd_norm_gain[j], ssd_w_out[j], need_ctx)
        h = h + g_m[:, None] * y
        h = h + g_f[:, None] * conv_ffn(modulate(h, norm_ffn[l], sh_f, sc_f), ffn_w_up[l],
                                        ffn_conv_w[l], ffn_conv_b[l], ffn_w_down[l])
        if need_ctx:
            hc = hc + cg_m[:, None] * yc
            hc = hc + cg_f[:, None] * conv_ffn(modulate(hc, norm_ffn[l], csh_f, csc_f), ffn_w_up[l],
                                              ffn_conv_w[l], ffn_conv_b[l], ffn_w_down[l])
    return h
```

```python
import contextlib
import numpy as np
import concourse.bass as bass
import concourse.mybir as mybir
from concourse.bass_utils import run_bass_kernel_spmd

F32 = mybir.dt.float32
BF16 = mybir.dt.bfloat16
AF = mybir.ActivationFunctionType
ALU = mybir.AluOpType

COMPUTE = ("tensor", "vector", "scalar", "gpsimd")
ALLENG = ("tensor", "vector", "scalar", "gpsimd", "sync")
NDSEM = 8

D = 1024
KC = 8
LCTX = 256
LLAT = 2048
SEQT = LCTX + LLAT
NT = 2 * SEQT
NROW = NT // 64
DFF = 2816
NFF = DFF // 128
EPS = 1e-6


class Buf:
    __slots__ = ("name", "writer", "readers", "dma_readers", "excl")

    def __init__(self, name=""):
        self.name = name
        self.excl = False
        self.writer = None
        self.readers = {}
        self.dma_readers = []

    def reset(self):
        self.writer = None
        self.readers = {}
        self.dma_readers = []


class Prog:
    def __init__(self, nc, stack):
        self.nc = nc
        self.ops = []
        self.bufs = []
        self.csem = {e: stack.enter_context(nc.semaphore("s_" + e)) for e in COMPUTE}
        self.dsem = {e: [stack.enter_context(nc.semaphore("d_%s_%d" % (e, j))) for j in range(NDSEM)]
                     for e in ("sync", "gpsimd", "scalar")}
        self.bar = stack.enter_context(nc.semaphore("bar"))
        self.cnt = {e: 0 for e in COMPUTE}
        self.dcnt = {e: 0 for e in self.dsem}
        self.nphase = 0
        self.total_ops = 0

    def buf(self, name=""):
        b = Buf(name)
        self.bufs.append(b)
        return b

    def op(self, eng, fn, reads=(), writes=(), dma=False):
        ex = [b for b in reads if b.excl and b not in writes]
        if ex:
            writes = list(writes) + ex
            reads = [b for b in reads if not b.excl]
        idx = len(self.ops)
        deps = set()
        for b in reads:
            if b.writer is not None:
                deps.add(b.writer)
        for b in writes:
            if b.writer is not None:
                deps.add(b.writer)
            for r in b.readers.values():
                deps.add(r)
            for r in b.dma_readers:
                deps.add(r)
        self.ops.append(dict(eng=eng, fn=fn, deps=deps, dma=dma, signal=False))
        for b in writes:
            b.writer = idx
            b.readers = {}
            b.dma_readers = []
        for b in reads:
            if b in writes:
                continue
            if dma:
                b.dma_readers.append(idx)
            else:
                b.readers[eng] = idx
        return idx

    def dma(self, q, out, in_, reads=(), writes=(), **kw):
        return self.op(q, lambda e: e.dma_start(out=out, in_=in_, **kw), reads, writes, dma=True)

    def emit_phase(self):
        nc = self.nc
        ops = self.ops
        if not ops:
            return
        for o in ops:
            nd = set()
            for d in o["deps"]:
                od = ops[d]
                if (not od["dma"]) and (not o["dma"]) and od["eng"] == o["eng"] == "tensor":
                    continue
                nd.add(d)
            o["deps"] = nd
            for d in nd:
                ops[d]["signal"] = True
        last_c = {}
        for o in ops:
            e = o["eng"]
            if o["dma"]:
                k = self.dcnt[e]
                self.dcnt[e] = k + 1
                o["dslot"] = k % NDSEM
                o["dval"] = 16 * (k // NDSEM + 1)
            else:
                last_c[e] = o
        for e, o in last_c.items():
            o["signal"] = True
        for o in ops:
            if (not o["dma"]) and o["signal"]:
                e = o["eng"]
                self.cnt[e] += 1
                o["seq"] = self.cnt[e]
        csem, dsem, bar = self.csem, self.dsem, self.bar
        phase = self.nphase
        cnt_end = dict(self.cnt)
        dcnt_end = dict(self.dcnt)

        def body_for(ename):
            def body(eng):
                if phase > 0:
                    eng.wait_ge(bar, len(ALLENG) * phase)
                seen = {}
                for o in ops:
                    if o["eng"] != ename:
                        continue
                    waits = []
                    for d in o["deps"]:
                        od = ops[d]
                        if od["dma"]:
                            waits.append((("d", od["eng"], od["dslot"]), od["dval"]))
                        else:
                            waits.append((("c", od["eng"]), od["seq"]))
                    if o["dma"] and o["dval"] > 16:
                        waits.append((("d", ename, o["dslot"]), o["dval"] - 16))
                    best = {}
                    for k, v in waits:
                        if v > best.get(k, 0):
                            best[k] = v
                    for k, v in best.items():
                        if seen.get(k, 0) >= v:
                            continue
                        seen[k] = v
                        s = csem[k[1]] if k[0] == "c" else dsem[k[1]][k[2]]
                        eng.wait_ge(s, v)
                    ins = o["fn"](eng)
                    if o["dma"]:
                        ins.then_inc(dsem[ename][o["dslot"]], 16)
                    elif o["signal"]:
                        ins.then_inc(csem[ename], 1)
                if ename in COMPUTE and cnt_end[ename] > 0:
                    eng.wait_ge(csem[ename], cnt_end[ename])
                if ename in dsem:
                    k = dcnt_end[ename]
                    for sl in range(NDSEM):
                        n = (k - sl + NDSEM - 1) // NDSEM if k > sl else 0
                        if n > 0:
                            eng.wait_ge(dsem[ename][sl], 16 * n)
                eng.sem_inc(bar, 1)
            return body

        with nc.Block() as block:
            for e in ALLENG:
                getattr(block, e)(body_for(e))
        self.nphase += 1
        self.total_ops += len(ops)
        self.ops = []
        for b in self.bufs:
            b.reset()
        self.bufs = []


class Ring:
    def __init__(self, kb, shape, dt, n, psum=False):
        self.items = []
        for _ in range(n):
            if psum:
                ap = kb.PS([128, 512], F32) if dt == F32 else kb.PS([128, 1024], BF16)
            else:
                ap = kb.T(shape, dt)
            b = kb.P.buf()
            b.excl = psum
            self.items.append((ap, b))
        self.i = 0

    def next(self):
        it = self.items[self.i % len(self.items)]
        self.i += 1
        return it


def all_blocks(with_ctx=True):
    out = []
    for s in range(2):
        if with_ctx:
            out.append((s, "c", s * SEQT, LCTX, 2))
        for b in range(4):
            out.append((s, "l", s * SEQT + LCTX + 512 * b, 512, s))
    return out


def all_segments(with_ctx=True):
    out = []
    for s in range(2):
        if with_ctx:
            out.append((s, "c", s * SEQT, LCTX, 2))
        out.append((s, "l", s * SEQT + LCTX, LLAT, s))
    return out


class KB:
    def __init__(self, dbg=(), stop_after=None):
        self.nc = bass.Bass("TRN2", target_bir_lowering=False)
        self.dbg = set(dbg)
        self.stop_after = stop_after
        self.uid = 0
        self.inputs = {}

    def din(self, name, shape):
        order = ["setup", "l0mod", "l0proj", "na", "gla", "l0out", "l0ffn", "l1proj", "ssd", "l1out"]
        first = {"hy_in": 2, "hy_out": 5, "ffn_up": 6, "ffn_dn": 6, "ssd_in": 7, "ssd_out": 9}
        if self.stop_after is not None and name in first and order.index(self.stop_after) < first[name]:
            return self.nc.dram_tensor(name, list(shape), F32, kind="Internal").ap()
        ap = self.nc.dram_tensor(name, list(shape), F32, kind="ExternalInput").ap()
        self.inputs[name] = ap
        return ap

    def dscr(self, name, shape, dt):
        kind = "ExternalOutput" if name in self.dbg else "Internal"
        return self.nc.dram_tensor(name, list(shape), dt, kind=kind).ap()

    def T(self, shape, dt, persistent=False, stack=None):
        self.uid += 1
        st = stack if stack is not None else (self.gst if persistent else self.st)
        return st.enter_context(self.nc.sbuf_tensor("t%d" % self.uid, list(shape), dt)).ap()

    def PS(self, shape=(128, 512), dt=F32):
        self.uid += 1
        return self.st.enter_context(self.nc.psum_tensor("p%d" % self.uid, list(shape), dt)).ap()

    @contextlib.contextmanager
    def phase(self, name):
        with contextlib.ExitStack() as st:
            self.st = st
            yield
            self.P.emit_phase()

    def mm(self, out, lhsT, rhs, start, stop, reads, writes):
        self.P.op("tensor", lambda e: e.matmul(out, lhsT, rhs, start=start, stop=stop), reads, writes)

    def tr(self, out, in_, ident, reads, writes):
        self.P.op("tensor", lambda e: e.transpose(out, in_, ident), reads, writes)

    def act(self, out, in_, func, reads, writes, bias=None, scale=None):
        kw = {}
        if bias is not None:
            kw["bias"] = bias
        if scale is not None:
            kw["scale"] = scale
        self.P.op("scalar", lambda e: e.activation(out=out, in_=in_, func=func, **kw), reads, writes)

    def tt(self, eng, out, in0, in1, op, reads, writes):
        self.P.op(eng, lambda e: e.tensor_tensor(out, in0, in1, op), reads, writes)

    def ts(self, eng, out, in0, s1, s2, op0, op1, reads, writes):
        if s2 is None:
            self.P.op(eng, lambda e: e.tensor_scalar(out, in0, s1, None, op0), reads, writes)
        else:
            self.P.op(eng, lambda e: e.tensor_scalar(out, in0, s1, s2, op0, op1), reads, writes)

    def stt(self, eng, out, in0, scalar, in1, op0, op1, reads, writes):
        self.P.op(eng, lambda e: e.scalar_tensor_tensor(out, in0, scalar, in1, op0, op1), reads, writes)

    def cp(self, eng, out, in_, reads, writes):
        if eng == "scalar":
            self.P.op(eng, lambda e: e.copy(out, in_), reads, writes)
        else:
            self.P.op(eng, lambda e: e.tensor_copy(out, in_), reads, writes)

    def rsqrt_from(self, out, in_, scale, reads, writes_buf):
        self.act(out, in_, AF.Ln, reads + [self.b_cst], [writes_buf], bias=self.eps_t[0:out.shape[0], 0:1], scale=scale)
        self.act(out, out, AF.Exp, [writes_buf], [writes_buf], scale=-0.5)

    def build(self):
        nc = self.nc
        hin = self.din("hin", [KC, 128, NT])
        cT = self.din("cT", [128, KC, 4])
        wmod = self.din("wmod", [2, 128, KC, 6 * D])
        bmod = self.din("bmod", [128, 2, 48])
        nmix = self.din("nmix", [128, 2, KC])
        nffn = self.din("nffn", [128, 2, KC])
        ffn_up = self.din("ffn_up", [2, 128, KC, 2 * DFF])
        ffn_cw = self.din("ffn_cw", [128, 2, NFF, 4])
        ffn_dn = self.din("ffn_dn", [2, 128, NFF, D])
        hy_in = self.din("hy_in", [128, KC, 4096])
        hy_out = self.din("hy_out", [128, KC, D])
        na_g = self.din("na_g", [128, 2])
        na_bias = self.din("na_bias", [64, 8 * 15 * 64])
        na_mask = self.din("na_mask", [64, 64])
        hg_gain = self.din("hg_gain", [128, 1])
        hg_lb = self.din("hg_lb", [128, 2, 3, 4])
        ssd_in = self.din("ssd_in", [128, KC, 5184])
        ssd_cw = self.din("ssd_cw", [128, 24, 4])
        ssd_dtb = self.din("ssd_dtb", [128, 64])
        ssd_alog = self.din("ssd_alog", [128, 64])
        ssd_dsk = self.din("ssd_dsk", [128, 16])
        ssd_ng = self.din("ssd_ng", [128, 16])
        ssd_out = self.din("ssd_out", [128, 16, D])
        cst_ident = self.din("cst_ident", [128, 128])
        cst_bones = self.din("cst_bones", [128, 128])
        cst_masks = self.din("cst_masks", [64, 4, 64])
        cst_masks2 = self.din("cst_masks2", [128, 4, 64])
        cst_reset = self.din("cst_reset", [128, 512])
        outT = nc.dram_tensor("outT", [KC, 128, 2 * LLAT], F32, kind="ExternalOutput").ap()

        hT = self.dscr("hT", [KC, 128, NT], F32)
        qT = self.dscr("qT", [4, 128, NT], BF16)
        kT = self.dscr("kT", [4, 128, NT], BF16)
        vtok = self.dscr("vtok", [NROW, 64, 512], BF16)
        hqT = self.dscr("hqT", [4, 128, NT], BF16)
        kfT = self.dscr("kfT", [2, 4, 128, NT], BF16)
        gfT = self.dscr("gfT", [2, 4, 128, NT], F32)
        vitok = self.dscr("vitok", [NROW, 64, 512], BF16)
        gateT = self.dscr("gateT", [4, 128, NT], BF16)
        mixT = self.dscr("mixT", [KC, 128, NT], BF16)
        guT = self.dscr("guT", [NFF, 128, NT], BF16)
        zsT = self.dscr("zsT", [16, 128, NT], BF16)
        xsT = self.dscr("xsT", [16, 128, NT], BF16)
        xstok = self.dscr("xstok", [NROW, 64, 2048], BF16)
        bcT = self.dscr("bcT", [8, 128, NT], BF16)
        dtT = self.dscr("dtT", [64, NT], F32)
        btok = self.dscr("btok", [NROW, 64, 512], BF16)
        dltok = self.dscr("dltok", [NROW, 64, 4, 64], F32)
        yfb = self.dscr("yfb", [2, 64, 64, 2048], BF16)
        mix2T = self.dscr("mix2T", [16, 128, NT], BF16)

        with contextlib.ExitStack() as gst:
            self.gst = gst
            self.P = Prog(nc, gst)
            P = self.P
            self.modT = self.T([128, 2, 48, 4], F32, True)
            self.Am = self.T([128, 2, KC, 4], F32, True)
            self.Af = self.T([128, 2, KC, 4], F32, True)
            self.ident = self.T([128, 128], BF16, True)
            self.ones = self.T([128, 128], BF16, True)
            self.bones = self.T([128, 128], BF16, True)
            self.masks = self.T([64, 4, 64], F32, True)
            self.onesf = self.T([64, 128], F32, True)
            self.masks2 = self.T([128, 4, 64], F32, True)
            self.masks2b = self.T([128, 4, 64], BF16, True)
            self.resetm = self.T([128, 512], F32, True)
            self.eps_t = self.T([128, 1], F32, True)
            self.lbv = self.T([128, 2, 4], F32, True)
            self.omlv = self.T([128, 2, 4], F32, True)
            self.EB = self.T([64, 8 * 15 * 64], BF16, True)
            self.nag = self.T([128, 2], F32, True)
            self.hgg = self.T([128, 1], F32, True)
            self.ffcw = self.T([128, 2, NFF, 4], F32, True)
            self.sscw = self.T([128, 24, 4], F32, True)
            self.dtb = self.T([128, 64], F32, True)
            self.aneg = self.T([128, 64], F32, True)
            self.dsk = self.T([128, 16], F32, True)
            self.sng = self.T([128, 16], F32, True)
            self.one_t = self.T([128, 1], F32, True)
            self.identf = self.T([64, 64], F32, True)

            with self.phase("setup"):
                self.b_cst = P.buf("cst")
                bc = self.b_cst
                for dst, src in ((self.masks, cst_masks), (self.resetm, cst_reset), (self.nag, na_g), (self.hgg, hg_gain),
                                 (self.ffcw, ffn_cw), (self.sscw, ssd_cw), (self.dtb, ssd_dtb), (self.dsk, ssd_dsk),
                                 (self.sng, ssd_ng), (self.identf, cst_ident[0:64, 0:64]), (self.masks2, cst_masks2)):
                    P.dma("sync", dst, src, writes=[P.buf()])
                for dst, src in ((self.ident, cst_ident), (self.bones, cst_bones), (self.masks2b, cst_masks2)):
                    P.dma("gpsimd", dst, src, writes=[P.buf()])
                P.op("vector", lambda e: e.memset(self.ones, 1.0), writes=[P.buf()])
                P.op("vector", lambda e: e.memset(self.eps_t, EPS), writes=[P.buf()])
                P.op("vector", lambda e: e.memset(self.one_t, 1.0), writes=[P.buf()])
                P.op("vector", lambda e: e.memset(self.onesf, 1.0), writes=[P.buf()])
                b_al = P.buf()
                al = self.T([128, 64], F32)
                P.dma("sync", al, ssd_alog, writes=[b_al])
                self.act(al, al, AF.Exp, [b_al], [b_al])
                self.ts("vector", self.aneg, al, -1.0, None, ALU.mult, None, [b_al], [P.buf()])
                lbr = self.T([128, 2, 3, 4], F32)
                b_lb = P.buf()
                P.dma("sync", lbr, hg_lb, writes=[b_lb])
                self.act(lbr, lbr, AF.Exp, [b_lb], [b_lb])
                ssum = self.T([128, 2, 4], F32)
                b_ss = P.buf()
                self.tt("vector", ssum, lbr[:, :, 0, :], lbr[:, :, 1, :], ALU.add, [b_lb], [b_ss])
                self.tt("vector", ssum, ssum, lbr[:, :, 2, :], ALU.add, [b_lb, b_ss], [b_ss])
                P.op("vector", lambda e: e.reciprocal(ssum, ssum), [b_ss], [b_ss])
                b_lbv = P.buf()
                self.tt("vector", self.lbv, lbr[:, :, 0, :], ssum, ALU.mult, [b_lb, b_ss], [b_lbv])
                self.ts("vector", self.omlv, self.lbv, -1.0, 1.0, ALU.mult, ALU.add, [b_lbv], [P.buf()])
                bt = self.T([64, 120, 64], F32)
                mk = self.T([64, 64], F32)
                b_bt, b_mk = P.buf(), P.buf()
                P.dma("sync", bt, na_bias.rearrange("p (j c) -> p j c", c=64), writes=[b_bt])
                P.dma("sync", mk, na_mask, writes=[b_mk])
                self.tt("vector", bt, bt, mk.unsqueeze(1).broadcast_to([64, 120, 64]), ALU.add, [b_bt, b_mk], [b_bt])
                self.act(self.EB.rearrange("p (j c) -> p j c", c=64), bt, AF.Exp, [b_bt], [P.buf()])
                sc = self.T([128, KC, 4], F32)
                b_sc = P.buf()
                P.dma("sync", sc, cT, writes=[b_sc])
                self.act(sc, sc, AF.Silu, [b_sc], [b_sc])
                bm = self.T([128, 2, 48], F32)
                nm = self.T([128, 2, KC], F32)
                nf = self.T([128, 2, KC], F32)
                b_bm, b_nm, b_nf = P.buf(), P.buf(), P.buf()
                P.dma("sync", bm, bmod, writes=[b_bm])
                P.dma("sync", nm, nmix, writes=[b_nm])
                P.dma("sync", nf, nffn, writes=[b_nf])
                wring = Ring(self, [128, KC, 768], F32, 2)
                b_mod = P.buf()
                for l in range(2):
                    ps = self.PS()
                    b_ps = P.buf()
                    for cb in range(8):
                        wt, b_wt = wring.next()
                        P.dma("sync", wt, wmod[l, :, :, cb * 768:(cb + 1) * 768], writes=[b_wt])
                        for jj in range(6):
                            ch = cb * 6 + jj
                            for kc in range(KC):
                                self.mm(ps[:, ch * 4:(ch + 1) * 4], wt[:, kc, jj * 128:(jj + 1) * 128], sc[:, kc, :],
                                        kc == 0, kc == KC - 1, [b_wt, b_sc], [b_ps])
                    self.tt("vector", self.modT[:, l], ps[:, 0:192].rearrange("p (c j) -> p c j", j=4),
                            bm[:, l, :].unsqueeze(2).broadcast_to([128, 48, 4]), ALU.add, [b_ps, b_bm], [b_mod])
                    tmpa = self.T([128, KC, 4], F32)
                    b_ta = P.buf()
                    for (dst, lo, nv, b_nv) in ((self.Am, 8, nm, b_nm), (self.Af, 32, nf, b_nf)):
                        self.ts("vector", tmpa, self.modT[:, l, lo:lo + KC, :], 1.0, None, ALU.add, None, [b_mod], [b_ta])
                        self.tt("vector", dst[:, l], tmpa, nv[:, l, :].unsqueeze(2).broadcast_to([128, KC, 4]), ALU.mult,
                                [b_ta, b_nv], [P.buf()])
            if self.stop_after == "setup":
                return

            self.phase_l0_proj(hin, hy_in, qT, kT, vtok, hqT, kfT, gfT, vitok, gateT)
            if self.stop_after in ("l0proj", "l0mod"):
                return
            self.phase_na(qT, kT, vtok, mixT)
            if self.stop_after == "na":
                return
            self.phase_gla(hqT, kfT, gfT, vitok, gateT, mixT)
            if self.stop_after == "gla":
                return
            self.phase_outproj(0, hy_out, KC, mixT, hin, hT, True, None)
            if self.stop_after == "l0out":
                return
            self.phase_ffn(0, ffn_up, ffn_dn, guT, hT, hT, True, None)
            if self.stop_after == "l0ffn":
                return
            self.phase_l1_proj(hT, ssd_in, zsT, xsT, xstok, bcT, btok, dltok)
            if self.stop_after == "l1proj":
                return
            self.phase_ssd2(xstok, btok, bcT, dltok, yfb)
            if self.stop_after == "ssdscan":
                return
            self.phase_ssd_fin(zsT, xsT, yfb, mix2T)
            if self.stop_after == "ssd":
                return
            self.phase_outproj(1, ssd_out, 16, mix2T, hT, hT, False, None)
            if self.stop_after == "l1out":
                return
            self.phase_ffn(1, ffn_up, ffn_dn, guT, hT, None, False, outT)

    def modulate_all(self, ust, hsrc, l, A, shlo, blocks):
        U = self.T([128, KC, NT], BF16, stack=ust)
        with self.phase("modulate"):
            self._modulate(U, hsrc, l, A, shlo, blocks)
        return U

    def _modulate(self, U, hsrc, l, A, shlo, blocks):
        P = self.P
        hring = Ring(self, [128, KC, 512], F32, 3)
        sqring = Ring(self, [128, KC, 512], BF16, 2)
        rring = Ring(self, [128, 512], F32, 3)
        tring = Ring(self, [128, 512], F32, 3)
        psr = Ring(self, [128, 512], F32, 2, psum=True)
        hv = hsrc.rearrange("c p t -> p c t")

        def front_a(blk):
            (s, kind, off, n, mc) = blk
            hb, b_hb = hring.next()
            P.dma("sync", hb[:, :, 0:n], hv[:, :, off:off + n], writes=[b_hb])
            sq, b_sq = sqring.next()
            self.act(sq[:, :, 0:n], hb[:, :, 0:n], AF.Square, [b_hb], [b_sq])
            return dict(hb=hb, b_hb=b_hb, sq=sq, b_sq=b_sq)

        def front_b(blk, c_):
            (s, kind, off, n, mc) = blk
            sq, b_sq = c_["sq"], c_["b_sq"]
            ps, b_ps = psr.next()
            for c in range(KC):
                self.mm(ps[:, 0:n], self.ones, sq[:, c, 0:n], c == 0, c == KC - 1, [b_sq], [b_ps])
            rs, b_rs = rring.next()
            self.rsqrt_from(rs[:, 0:n], ps[:, 0:n], 1.0 / D, [b_ps], b_rs)
            c_["rs"], c_["b_rs"] = rs, b_rs

        def back(blk, c_):
            (s, kind, off, n, mc) = blk
            hb, b_hb, rs, b_rs = c_["hb"], c_["b_hb"], c_["rs"], c_["b_rs"]
            b_u = P.buf()
            for c in range(KC):
                tm, b_tm = tring.next()
                self.stt("vector", tm[:, 0:n], hb[:, c, 0:n], A[:, l, c, mc:mc + 1], rs[:, 0:n], ALU.mult, ALU.mult,
                         [b_hb, b_rs], [b_tm])
                self.act(U[:, c, off:off + n], tm[:, 0:n], AF.Identity, [b_tm], [b_u],
                         bias=self.modT[:, l, shlo + c, mc:mc + 1], scale=1.0)

        nb = len(blocks)
        ctxs = {}
        for i in range(min(2, nb)):
            ctxs[i] = front_a(blocks[i])
        front_b(blocks[0], ctxs[0])
        for i in range(nb):
            back(blocks[i], ctxs[i])
            if i + 1 < nb:
                front_b(blocks[i + 1], ctxs[i + 1])
            if i + 2 < nb:
                ctxs[i + 2] = front_a(blocks[i + 2])
            ctxs.pop(i)

    def load_w_chunk(self, dst, b_dst, src_ap):
        self.P.dma("gpsimd", dst, src_ap, writes=[b_dst])

    def phase_l0_proj(self, hsrc, w_in, qT, kT, vtok, hqT, kfT, gfT, vitok, gateT):
        P = self.P
        blocks = all_blocks(True)
        with contextlib.ExitStack() as ust:
          U = self.modulate_all(ust, hsrc, 0, self.Am, 0, blocks)
          if self.stop_after == "l0mod":
              return
          with self.phase("l0proj"):
            wring = Ring(self, [128, KC, 128], BF16, 2)
            psr = Ring(self, [128, 512], F32, 2, psum=True)
            ps2 = Ring(self, [128, 512], F32, 1, psum=True)
            oring = Ring(self, [128, 512], BF16, 3)
            o32 = Ring(self, [128, 512], F32, 2)
            f32r = Ring(self, [128, 512], F32, 2)
            sqr = Ring(self, [128, 512], BF16, 2)
            rr = Ring(self, [128, 512], F32, 2)
            fm = list(range(0, 8)) + list(range(12, 24)) + list(range(28, 32))
            import os as _os
            _skip = _os.environ.get("KSKIP", "")
            if "fm" in _skip:
                fm = []
            if "q" in _skip:
                fm = [j for j in fm if j >= 8]
            if "g" in _skip:
                fm = [j for j in fm if not (16 <= j < 24)]
            if "s" in _skip:
                fm = [j for j in fm if not (12 <= j < 16 or j >= 28)]
            for j in fm:
                wt, b_wt = wring.next()
                self.load_w_chunk(wt, b_wt, w_in[:, :, j * 128:(j + 1) * 128])
                for (s, kind, off, n, mc) in blocks:
                    ps, b_ps = psr.next()
                    for kc in range(KC):
                        self.mm(ps[:, 0:n], wt[:, kc, :], U[:, kc, off:off + n], kc == 0, kc == KC - 1, [b_wt], [b_ps])
                    if j < 8:
                        sq, b_sq = sqr.next()
                        self.act(sq[:, 0:n], ps[:, 0:n], AF.Square, [b_ps], [b_sq])
                        raw, b_raw = f32r.next()
                        self.cp("vector", raw[:, 0:n], ps[:, 0:n], [b_ps], [b_raw])
                        p2, b_p2 = ps2.next()
                        self.mm(p2[:, 0:n], self.bones, sq[:, 0:n], True, True, [b_sq], [b_p2])
                        r, b_r = rr.next()
                        self.rsqrt_from(r[:, 0:n], p2[:, 0:n], 1.0 / 64, [b_p2], b_r)
                        ot, b_ot = oring.next()
                        gi = 0 if j < 4 else 1
                        self.stt("vector", ot[:, 0:n], raw[:, 0:n], self.nag[:, gi:gi + 1], r[:, 0:n], ALU.mult, ALU.mult,
                                 [b_raw, b_r], [b_ot])
                        dst = qT if j < 4 else kT
                        P.dma("sync", dst[j % 4, :, off:off + n], ot[:, 0:n], reads=[b_ot])
                    elif 12 <= j < 16 or j >= 28:
                        ot, b_ot = oring.next()
                        self.act(ot[:, 0:n], ps[:, 0:n], AF.Silu, [b_ps], [b_ot])
                        dst = hqT if j < 16 else gateT
                        P.dma("sync", dst[j % 4, :, off:off + n], ot[:, 0:n], reads=[b_ot])
                    else:
                        dr = 0 if j < 20 else 1
                        jj = j % 4
                        f, b_f = f32r.next()
                        self.act(f[:, 0:n], ps[:, 0:n], AF.Sigmoid, [b_ps], [b_f])
                        self.ts("vector", f[:, 0:n], f[:, 0:n], self.omlv[:, dr, jj:jj + 1], self.lbv[:, dr, jj:jj + 1],
                                ALU.mult, ALU.add, [b_f], [b_f])
                        g, b_g = o32.next()
                        self.act(g[:, 0:n], f[:, 0:n], AF.Ln, [b_f], [b_g])
                        P.dma("sync", gfT[dr, jj, :, off:off + n], g[:, 0:n], reads=[b_g])
                        ot, b_ot = oring.next()
                        self.ts("vector", ot[:, 0:n], f[:, 0:n], -1.0, 1.0, ALU.mult, ALU.add, [b_f], [b_ot])
                        P.dma("sync", kfT[dr, jj, :, off:off + n], ot[:, 0:n], reads=[b_ot])
            if "tm" not in _skip:
                self.proj_tokmajor(U, blocks, [(w_in[:, :, 1024:1536], vtok, 0, 512), (w_in[:, :, 3072:3584], vitok, 0, 512)], psr)

    def proj_tokmajor(self, U, blocks, specs, psr):
        P = self.P
        st_ring = Ring(self, [64, 8, 512], BF16, 2)
        for (wsrc, dst, c0, ncols) in specs:
            wt = self.T([128, KC, 512], BF16)
            b_wt = P.buf()
            self.load_w_chunk(wt, b_wt, wsrc)
            for (s, kind, off, n, mc) in blocks:
                stg, b_stg = st_ring.next()
                nr = n // 64
                for r in range(nr):
                    ps, b_ps = psr.next()
                    for kc in range(KC):
                        self.mm(ps[0:64, 0:512], U[:, kc, off + r * 64:off + (r + 1) * 64], wt[:, kc, :], kc == 0, kc == KC - 1,
                                [b_wt], [b_ps])
                    self.cp("scalar" if r % 2 == 0 else "vector", stg[:, r, :], ps[0:64, 0:512], [b_ps], [b_stg])
                r0 = off // 64
                P.dma("sync", dst[r0:r0 + nr, :, c0:c0 + ncols].rearrange("r p f -> p r f"), stg[:, 0:nr, :], reads=[b_stg])

    def phase_na(self, qT, kT, vtok, mixT):
        P = self.P
        with self.phase("na"):
            qring = Ring(self, [128, SEQT], BF16, 2)
            kring = Ring(self, [128, SEQT], BF16, 2)
            vring = Ring(self, [64, 36, 128], BF16, 2)
            sps = Ring(self, [128, 512], F32, 4, psum=True)
            ops_ = Ring(self, [128, 512], F32, 2, psum=True)
            dps = Ring(self, [128, 512], F32, 2, psum=True)
            ering = Ring(self, [64, 512], BF16, 5)
            pring = Ring(self, [64, 512], BF16, 7)
            rdr = Ring(self, [64, 512], F32, 2)
            ostr = Ring(self, [64, 512], BF16, 2)
            EBv = self.EB.rearrange("p (h j c) -> p h j c", h=8, j=15)

            def r0(r):
                return min(max(r - 4, 0), 24)

            for s in range(2):
                for c in range(4):
                    q2, b_q = qring.next()
                    k2, b_k = kring.next()
                    v2, b_v = vring.next()
                    P.dma("sync", q2, qT[c, :, s * SEQT:(s + 1) * SEQT], writes=[b_q])
                    P.dma("sync", k2, kT[c, :, s * SEQT:(s + 1) * SEQT], writes=[b_k])
                    P.dma("sync", v2, vtok[s * 36:(s + 1) * 36, :, c * 128:(c + 1) * 128].rearrange("r p f -> p r f"), writes=[b_v])
                    items = []
                    for hh in range(2):
                        h = 2 * c + hh
                        pb = 64 * hh
                        groups = [("c", 0, LCTX, None)] + [("l", LCTX + 512 * g, 512, g) for g in range(4)]
                        for (gk, qoff, nq_g, g) in groups:
                            chunks = [("c", m, 0, nq_g, 0) for m in range(4)]
                            if gk == "l":
                                rows = list(range(8 * g, 8 * g + 8))
                                for rp in range(32):
                                    rin = [r for r in rows if r0(r) <= rp < r0(r) + 8]
                                    if rin:
                                        qlo, qhi = rin[0], rin[-1] + 1
                                        chunks.append(("l", rp, (qlo - 8 * g) * 64, (qhi - qlo) * 64, 7 - rp + qlo))
                            G = dict(hh=hh, h=h, pb=pb, qoff=qoff, nq_g=nq_g)
                            for ci_, ch_ in enumerate(chunks):
                                items.append(dict(G=G, chunk=ch_, first=(ci_ == 0), last=(ci_ == len(chunks) - 1)))

                    def front(it):
                        G = it["G"]
                        (ck, idx, c0, nq, jlo) = it["chunk"]
                        pb, h, qoff = G["pb"], G["h"], G["qoff"]
                        if it["first"]:
                            G["o"] = ops_.next()
                            G["d"] = dps.next()
                        koff = idx * 64 if ck == "c" else LCTX + idx * 64
                        s_ps, b_s = sps.next()
                        self.mm(s_ps[0:64, 0:nq], k2[pb:pb + 64, koff:koff + 64], q2[pb:pb + 64, qoff + c0:qoff + c0 + nq],
                                True, True, [b_k, b_q], [b_s])
                        pT, b_p = pring.next()
                        if ck == "c":
                            self.act(pT[:, 0:nq], s_ps[0:64, 0:nq], AF.Exp, [b_s], [b_p], scale=0.125)
                        else:
                            e_t, b_e = ering.next()
                            self.act(e_t[:, 0:nq], s_ps[0:64, 0:nq], AF.Exp, [b_s], [b_e], scale=0.125)
                            nj = nq // 64
                            self.tt("vector", pT[:, 0:nq], e_t[:, 0:nq],
                                    EBv[:, h, jlo:jlo + nj, :].rearrange("p j c -> p (j c)"), ALU.mult,
                                    [b_e, self.b_cst], [b_p])
                        it["pT"], it["b_p"] = pT, b_p

                    def back(it):
                        G = it["G"]
                        (ck, idx, c0, nq, jlo) = it["chunk"]
                        hh, pb, qoff, nq_g = G["hh"], G["pb"], G["qoff"], G["nq_g"]
                        o_ps, b_o = G["o"]
                        d_ps, b_d = G["d"]
                        pT, b_p = it["pT"], it["b_p"]
                        vrow = idx if ck == "c" else 4 + idx
                        self.mm(o_ps[0:64, c0:c0 + nq], v2[:, vrow, hh * 64:(hh + 1) * 64], pT[:, 0:nq], it["first"], it["last"],
                                [b_v, b_p], [b_o])
                        self.mm(d_ps[0:64, c0:c0 + nq], self.ones[0:64, 0:64], pT[:, 0:nq], it["first"], it["last"], [b_p], [b_d])
                        if it["last"]:
                            rd, b_rd = rdr.next()
                            P.op("vector", (lambda rd=rd, d_ps=d_ps, n=nq_g: lambda e: e.reciprocal(rd[:, 0:n], d_ps[0:64, 0:n]))(),
                                 [b_d], [b_rd])
                            ot, b_ot = ostr.next()
                            self.tt("vector", ot[:, 0:nq_g], o_ps[0:64, 0:nq_g], rd[:, 0:nq_g], ALU.mult, [b_o, b_rd], [b_ot])
                            P.dma("gpsimd", mixT[c, pb:pb + 64, s * SEQT + qoff:s * SEQT + qoff + nq_g], ot[:, 0:nq_g], reads=[b_ot])

                    LAG = 3
                    for t_ in range(len(items) + LAG):
                        if t_ < len(items):
                            front(items[t_])
                        if t_ >= LAG:
                            back(items[t_ - LAG])

    def scan_setup(self, nchains, dv):
        P = self.P
        W = {}
        W["G"] = Ring(self, [128, 512], F32, 2)
        W["X"] = Ring(self, [128, 512], F32, 2)
        W["Xc"] = Ring(self, [128, 512], F32, 2)
        W["qe"] = Ring(self, [128, 512], F32, 2)
        W["ke"] = Ring(self, [128, 512], F32, 2)
        W["qt"] = Ring(self, [128, 512], BF16, nchains + 1)
        W["kt"] = Ring(self, [128, 512], BF16, nchains + 1)
        W["q2"] = Ring(self, [128, 512], BF16, nchains + 1)
        W["sc3"] = Ring(self, [128, 3, 8], F32, nchains + 1)
        W["esc"] = Ring(self, [128, 3, 8], F32, nchains + 1)
        W["aT"] = Ring(self, [64, 64], BF16, 4)
        W["kT"] = Ring(self, [64, 128], BF16, 4)
        W["tmp"] = Ring(self, [128, dv], F32, 4)
        W["a_ps"] = Ring(self, [128, 512], F32, 2, psum=True)
        W["t_ps"] = Ring(self, [128, 512], BF16, 2, psum=True)
        W["kv_ps"] = Ring(self, [128, 512], F32, 2, psum=True)
        W["o_ps"] = Ring(self, [128, 512], F32, nchains, psum=True)
        W["LT"] = Ring(self, [64, 8, 64], BF16, nchains + 1)
        W["Dm"] = Ring(self, [64, 8, 64], F32, 2)
        W["xcol"] = Ring(self, [64, 8], F32, 2)
        W["chains"] = []
        for i in range(nchains):
            S = self.T([128, dv], F32)
            Sbf = self.T([128, dv], BF16)
            W["chains"].append(dict(S=S, Sbf=Sbf, bS=P.buf(), bSbf=P.buf()))
        return W

    def chain_reset(self, ch):
        S, Sbf = ch["S"], ch["Sbf"]
        self.P.op("vector", lambda e: e.memset(S, 0.0), writes=[ch["bS"]])
        ch["has_state"] = False

    def scan_prep(self, W, ch, q_ap, k_ap, g_ap, rd, vrows, n, dv, sigma, o_out, pbo=0, scalar=False):
        P = self.P
        nch = n // 64
        G, bG = W["G"].next()
        P.op("vector", lambda e: e.tensor_tensor_scan(G[:, 0:n], self.resetm[:, 0:n], g_ap, 0.0, ALU.mult, ALU.add),
             rd + [self.b_cst], [bG])
        Gv = G[:, 0:n].rearrange("p (c t) -> p c t", t=64)
        if sigma > 0:
            X, bX = G, bG
        else:
            X, bX = W["X"].next()
            self.tt("vector", X[:, 0:n], G[:, 0:n], g_ap, ALU.subtract, [bG] + rd, [bX])
        Xv = X[:, 0:n].rearrange("p (c t) -> p c t", t=64)
        sc3, b_sc3 = W["sc3"].next()
        esc, b_esc = W["esc"].next()
        qt, b_qt = W["qt"].next()
        kt, b_kt = W["kt"].next()
        q2, b_q2 = W["q2"].next()
        LT = b_LT = None
        if scalar:
            totb = sc3[:, 2, 0:nch].unsqueeze(2).broadcast_to([128, nch, 64])
            self.cp("vector", sc3[:, 2, 0:nch], Gv[:, :, 63], [bG], [b_sc3])
            self.act(esc[:, 2, 0:nch], sc3[:, 2, 0:nch], AF.Exp, [b_sc3], [b_esc])
            cD = esc[:, 2, :]
            cK = None
            Xc, bXc = W["Xc"].next()
            self.tt("vector", Xc[:, 0:n].rearrange("p (c t) -> p c t", t=64), totb, Xv, ALU.subtract, [bX, b_sc3], [bXc])
            qe, b_qe = W["qe"].next()
            ke, b_ke = W["ke"].next()
            if sigma > 0:
                self.act(qe[:, 0:n], X[:, 0:n], AF.Exp, [bX], [b_qe])
                self.act(ke[:, 0:n], Xc[:, 0:n], AF.Exp, [bXc], [b_ke])
            else:
                self.act(qe[:, 0:n], Xc[:, 0:n], AF.Exp, [bXc], [b_qe])
                self.act(ke[:, 0:n], X[:, 0:n], AF.Exp, [bX], [b_ke])
            self.tt("vector", q2[:, 0:n], q_ap, qe[:, 0:n], ALU.mult, rd + [b_qe], [b_q2])
            self.tt("vector", kt[:, 0:n], k_ap, ke[:, 0:n], ALU.mult, rd + [b_ke], [b_kt])
            self.cp("vector", qt[:, 0:n], k_ap, rd, [b_qt])
            Dm, b_Dm = W["Dm"].next()
            xcol, b_xc = W["xcol"].next()
            X64 = X[0:64, 0:n].rearrange("p (c t) -> p c t", t=64)
            self.tt("vector", Dm[:, 0:nch, :], X64, self.identf.unsqueeze(1).broadcast_to([64, nch, 64]), ALU.mult,
                    [bX, self.b_cst], [b_Dm])
            P.op("vector", (lambda xcol=xcol, Dm=Dm, nch=nch: lambda e: e.tensor_reduce(
                xcol[:, 0:nch], Dm[:, 0:nch, :], mybir.AxisListType.X, ALU.add))(), [b_Dm], [b_xc])
            self.tt("vector", Dm[:, 0:nch, :], X64, xcol[:, 0:nch].unsqueeze(2).broadcast_to([64, nch, 64]), ALU.subtract,
                    [bX, b_xc, b_Dm], [b_Dm])
            self.ts("vector", Dm[:, 0:nch, :], Dm[:, 0:nch, :], float(sigma), 0.0, ALU.mult, ALU.min, [b_Dm], [b_Dm])
            self.act(Dm[:, 0:nch, :], Dm[:, 0:nch, :], AF.Exp, [b_Dm], [b_Dm])
            LT, b_LT = W["LT"].next()
            mk = self.masks[:, 0, :] if sigma > 0 else self.masks[:, 1, :]
            self.tt("vector", LT[:, 0:nch, :], Dm[:, 0:nch, :], mk.unsqueeze(1).broadcast_to([64, nch, 64]), ALU.mult,
                    [b_Dm, self.b_cst], [b_LT])
        else:
            self.cp("vector", sc3[:, 0, 0:nch], Xv[:, :, 32], [bX], [b_sc3])
            self.cp("vector", sc3[:, 2, 0:nch], Gv[:, :, 63], [bG, b_sc3], [b_sc3])
            self.tt("vector", sc3[:, 1, 0:nch], sc3[:, 2, 0:nch], sc3[:, 0, 0:nch], ALU.subtract, [b_sc3], [b_sc3])
            self.act(esc[:, :, 0:nch], sc3[:, :, 0:nch], AF.Exp, [b_sc3], [b_esc])
            if sigma > 0:
                cS, cK = esc[:, 0, :], esc[:, 1, :]
            else:
                cS, cK = esc[:, 1, :], esc[:, 0, :]
            cD = esc[:, 2, :]
            Xc, bXc = W["Xc"].next()
            self.tt("vector", Xc[:, 0:n].rearrange("p (c t) -> p c t", t=64), Xv,
                    sc3[:, 0, 0:nch].unsqueeze(2).broadcast_to([128, nch, 64]), ALU.subtract, [bX, b_sc3], [bXc])
            qe, b_qe = W["qe"].next()
            ke, b_ke = W["ke"].next()
            self.act(qe[:, 0:n], Xc[:, 0:n], AF.Exp, [bXc], [b_qe], scale=float(sigma))
            self.act(ke[:, 0:n], Xc[:, 0:n], AF.Exp, [bXc], [b_ke], scale=float(-sigma))
            self.tt("vector", qt[:, 0:n], q_ap, qe[:, 0:n], ALU.mult, rd + [b_qe], [b_qt])
            self.tt("vector", kt[:, 0:n], k_ap, ke[:, 0:n], ALU.mult, rd + [b_ke], [b_kt])
            self.tt("vector", q2[:, 0:n].rearrange("p (c t) -> p c t", t=64), qt[:, 0:n].rearrange("p (c t) -> p c t", t=64),
                    cS[:, 0:nch].unsqueeze(2).broadcast_to([128, nch, 64]), ALU.mult, [b_qt, b_esc], [b_q2])
        o_ps, b_o = W["o_ps"].next()
        order = list(range(nch)) if sigma > 0 else list(range(nch - 1, -1, -1))
        mask = self.masks[:, 0, :] if sigma > 0 else self.masks[:, 1, :]
        return dict(W=W, ch=ch, n=n, dv=dv, pbo=pbo, vrows=vrows, o_out=o_out, qt=qt, b_qt=b_qt, kt=kt, b_kt=b_kt, q2=q2,
                    b_q2=b_q2, esc=esc, b_esc=b_esc, cK=cK, cD=cD, o_ps=o_ps, b_o=b_o, order=order, mask=mask,
                    scalar=scalar, LT=LT, b_LT=b_LT, q_ap=q_ap, rd=rd)

    def scan_s1(self, C, k):
        W = C["W"]
        i = C["order"][k]
        cs = slice(i * 64, (i + 1) * 64)
        qt, kt = C["qt"], C["kt"]
        b_qt, b_kt = C["b_qt"], C["b_kt"]
        U = {}
        if C["o_out"] is not None:
            a_ps, b_a = W["a_ps"].next()
            if C["scalar"]:
                self.mm(a_ps[0:64, 0:64], qt[:, cs], C["q_ap"][:, cs], True, True, [b_qt] + C["rd"], [b_a])
            else:
                self.mm(a_ps[0:64, 0:64], kt[:, cs], qt[:, cs], True, True, [b_kt, b_qt], [b_a])
            U["a"] = (a_ps, b_a)
        t_ps, b_t = W["t_ps"].next()
        self.tr(t_ps[0:64, 0:128], kt[:, cs], self.ident, [b_kt, self.b_cst], [b_t])
        U["t"] = (t_ps, b_t)
        C.setdefault("units", {})[k] = U

    def scan_s2(self, C, k):
        W = C["W"]
        i = C["order"][k]
        U = C["units"][k]
        if C["o_out"] is not None:
            a_ps, b_a = U["a"]
            aT, b_aT = W["aT"].next()
            if C["scalar"]:
                self.tt("vector", aT, a_ps[0:64, 0:64], C["LT"][:, i, :], ALU.mult, [b_a, C["b_LT"]], [b_aT])
            else:
                self.tt("vector", aT, a_ps[0:64, 0:64], C["mask"], ALU.mult, [b_a, self.b_cst], [b_aT])
            U["aT"] = (aT, b_aT)
        t_ps, b_t = U["t"]
        kT, b_kT = W["kT"].next()
        self.cp("scalar", kT, t_ps[0:64, 0:128], [b_t], [b_kT])
        U["kT"] = (kT, b_kT)

    def scan_s3(self, C, k):
        W, ch = C["W"], C["ch"]
        dv, pbo = C["dv"], C["pbo"]
        i = C["order"][k]
        cs = slice(i * 64, (i + 1) * 64)
        U = C["units"].pop(k)
        q2, b_q2, b_esc = C["q2"], C["b_q2"], C["b_esc"]
        S, Sbf, bS, bSbf = ch["S"], ch["Sbf"], ch["bS"], ch["bSbf"]
        o_ps, b_o = C["o_ps"], C["b_o"]
        vr, b_vr = C["vrows"](i)
        if C["o_out"] is not None:
            aT, b_aT = U["aT"]
            self.mm(o_ps[pbo:pbo + dv, cs], vr, aT, True, not ch["has_state"], [b_vr, b_aT], [b_o])
            if ch["has_state"]:
                self.mm(o_ps[pbo:pbo + dv, cs], Sbf, q2[:, cs], False, True, [bSbf, b_q2], [b_o])
        kT, b_kT = U["kT"]
        kv_ps, b_kv = W["kv_ps"].next()
        self.mm(kv_ps[:, 0:dv], kT, vr, True, True, [b_kT, b_vr], [b_kv])
        if C["scalar"]:
            self.stt("vector", S, S, C["cD"][:, i:i + 1], kv_ps[:, 0:dv], ALU.mult, ALU.add, [bS, b_esc, b_kv], [bS])
        else:
            tmp, b_tmp = W["tmp"].next()
            self.ts("vector", tmp, kv_ps[:, 0:dv], C["cK"][:, i:i + 1], None, ALU.mult, None, [b_kv, b_esc], [b_tmp])
            self.stt("vector", S, S, C["cD"][:, i:i + 1], tmp, ALU.mult, ALU.add, [bS, b_esc, b_tmp], [bS])
        self.cp("scalar", Sbf, S, [bS], [bSbf])
        ch["has_state"] = True

    def scan_finish(self, C):
        if C["o_out"] is not None:
            oa, b_oa = C["o_out"]
            pbo, dv, n = C["pbo"], C["dv"], C["n"]
            self.tt("vector", oa, oa, C["o_ps"][pbo:pbo + dv, 0:n], ALU.add, [C["b_o"], b_oa], [b_oa])

    def scan_step(self, preps):
        nch = preps[0]["n"] // 64
        units = [(C, k) for k in range(nch) for C in preps]
        for idx, (C, k) in enumerate(units):
            if idx == 0:
                self.scan_s1(C, k)
            self.scan_s2(C, k)
            if idx + 1 < len(units):
                self.scan_s1(*units[idx + 1])
            self.scan_s3(C, k)
        for C in preps:
            self.scan_finish(C)

    def phase_gla(self, hqT, kfT, gfT, vitok, gateT, mixT):
        P = self.P
        with self.phase("gla"):
            NCH = 2
            W = self.scan_setup(NCH, 128)
            qb_r = Ring(self, [128, 512], BF16, 2 * NCH)
            kb_r = Ring(self, [128, 512], BF16, 2 * NCH)
            gb_r = Ring(self, [128, 512], F32, 2 * NCH)
            vb_r = Ring(self, [64, 8, 128], BF16, 2 * NCH)
            gtr = Ring(self, [128, 512], BF16, 3)
            oacc_r = Ring(self, [128, SEQT], F32, 4)
            sqr = Ring(self, [128, 512], BF16, 2)
            rr = Ring(self, [128, 512], F32, 2)
            t32 = Ring(self, [128, 512], F32, 2)
            yo = Ring(self, [128, 512], BF16, 2)
            nps = W["a_ps"]
            fwd = [(0, LCTX)] + [(LCTX + 512 * b, 512) for b in range(4)]
            bwd = [(0, LCTX)] + [(LCTX + 512 * b, 512) for b in (3, 2, 1, 0)]
            for s in range(2):
                for hp in range(4):
                    heads = (hp,)
                    oaccs = {}
                    for hd in heads:
                        oacc, b_oa = oacc_r.next()
                        P.op("gpsimd", (lambda oacc=oacc: lambda e: e.memset(oacc, 0.0))(), writes=[b_oa])
                        oaccs[hd] = (oacc, b_oa)
                    specs = []
                    for hd in heads:
                        specs.append((hd, 0, +1))
                        specs.append((hd, 1, -1))
                    for ci in range(NCH):
                        self.chain_reset(W["chains"][ci])
                    for step in range(5):
                        preps = []
                        for ci, (hd, dr, sg) in enumerate(specs):
                            ch = W["chains"][ci]
                            to, n = (fwd if sg > 0 else bwd)[step]
                            off = s * SEQT + to
                            nr = n // 64
                            q, b_q = qb_r.next()
                            k, b_k = kb_r.next()
                            g, b_g = gb_r.next()
                            v, b_v = vb_r.next()
                            P.dma("sync", q[:, 0:n], hqT[hd, :, off:off + n], writes=[b_q])
                            P.dma("sync", k[:, 0:n], kfT[dr, hd, :, off:off + n], writes=[b_k])
                            P.dma("sync", g[:, 0:n], gfT[dr, hd, :, off:off + n], writes=[b_g])
                            P.dma("sync", v[:, 0:nr, :], vitok[off // 64:off // 64 + nr, :, hd * 128:(hd + 1) * 128].rearrange("r p f -> p r f"),
                                  writes=[b_v])
                            oacc, b_oa = oaccs[hd]
                            preps.append(self.scan_prep(W, ch, q[:, 0:n], k[:, 0:n], g[:, 0:n], [b_q, b_k, b_g],
                                                        (lambda i, v=v, b_v=b_v: (v[:, i, :], b_v)), n, 128, sg,
                                                        (oacc[:, to:to + n], b_oa)))
                        self.scan_step(preps)
                    for hd in heads:
                        oacc, b_oa = oaccs[hd]
                        for (to, n) in fwd:
                            gt, b_gt = gtr.next()
                            P.dma("sync", gt[:, 0:n], gateT[hd, :, s * SEQT + to:s * SEQT + to + n], writes=[b_gt])
                            sq, b_sq = sqr.next()
                            self.act(sq[:, 0:n], oacc[:, to:to + n], AF.Square, [b_oa], [b_sq])
                            ps, b_ps = nps.next()
                            self.mm(ps[:, 0:n], self.ones, sq[:, 0:n], True, True, [b_sq], [b_ps])
                            r, b_r = rr.next()
                            self.rsqrt_from(r[:, 0:n], ps[:, 0:n], 1.0 / 128, [b_ps], b_r)
                            t, b_t = t32.next()
                            self.stt("vector", t[:, 0:n], oacc[:, to:to + n], self.hgg[:, 0:1], r[:, 0:n], ALU.mult, ALU.mult,
                                     [b_oa, b_r, self.b_cst], [b_t])
                            y, b_y = yo.next()
                            self.tt("vector", y[:, 0:n], t[:, 0:n], gt[:, 0:n], ALU.mult, [b_t, b_gt], [b_y])
                            P.dma("gpsimd", mixT[4 + hd, :, s * SEQT + to:s * SEQT + to + n], y[:, 0:n], reads=[b_y])

    def phase_outproj(self, l, w_out, nk, mixsrc, hsrc, hdst, with_ctx, outT):
        P = self.P
        with self.phase("outproj%d" % l):
            wt = self.T([128, nk, D], BF16)
            b_wt = P.buf()
            for kc in range(nk):
                P.dma("gpsimd", wt[:, kc, :], w_out[:, kc, :], writes=[b_wt])
            self.linear_residual(l, wt, b_wt, nk, mixsrc, hsrc, hdst, 16, with_ctx, outT)

    def linear_residual(self, l, wt, b_wt, nk, xsrc, hsrc, hdst, glo, with_ctx, outT):
        P = self.P
        xr = Ring(self, [128, nk, 512], BF16, 2)
        hr = Ring(self, [128, KC, 512], F32, 2)
        orr = Ring(self, [128, KC, 512], F32, 2)
        psr = Ring(self, [128, 512], F32, 2, psum=True)
        xv = xsrc.rearrange("c p t -> p c t")
        hv = hsrc.rearrange("c p t -> p c t")
        for (s, kind, off, n, mc) in all_blocks(with_ctx):
            xb, b_x = xr.next()
            P.dma("sync", xb[:, :, 0:n], xv[:, :, off:off + n], writes=[b_x])
            hb, b_h = hr.next()
            P.dma("sync", hb[:, :, 0:n], hv[:, :, off:off + n], writes=[b_h])
            ob, b_ob = orr.next()
            for i in range(KC):
                ps, b_ps = psr.next()
                for kc in range(nk):
                    self.mm(ps[:, 0:n], wt[:, kc, i * 128:(i + 1) * 128], xb[:, kc, 0:n], kc == 0, kc == nk - 1, [b_wt, b_x], [b_ps])
                self.stt("vector", ob[:, i, 0:n], ps[:, 0:n], self.modT[:, l, glo + i, mc:mc + 1], hb[:, i, 0:n], ALU.mult, ALU.add,
                         [b_ps, b_h], [b_ob])
            if outT is None:
                P.dma("gpsimd", hdst.rearrange("c p t -> p c t")[:, :, off:off + n], ob[:, :, 0:n], reads=[b_ob])
            else:
                oo = s * LLAT + (off - s * SEQT - LCTX)
                P.dma("gpsimd", outT.rearrange("c p t -> p c t")[:, :, oo:oo + n], ob[:, :, 0:n], reads=[b_ob])

    def conv3(self, strip, b_strip, ln, wv, out, b_out):
        self.ts("vector", out, strip[:, 1:ln + 1], wv[:, 1:2], None, ALU.mult, None, [b_strip, self.b_cst], [b_out])
        self.stt("vector", out, strip[:, 0:ln], wv[:, 0:1], out, ALU.mult, ALU.add, [b_strip, b_out, self.b_cst], [b_out])
        self.stt("vector", out, strip[:, 2:ln + 2], wv[:, 2:3], out, ALU.mult, ALU.add, [b_strip, b_out, self.b_cst], [b_out])

    def phase_ffn(self, l, ffn_up, ffn_dn, guT, hsrc, hdst, with_ctx, outT):
        P = self.P
        blocks = all_blocks(with_ctx)
        with contextlib.ExitStack() as ust:
          U = self.modulate_all(ust, hsrc, l, self.Af, 24, blocks)
          with self.phase("ffn_up%d" % l):
            war = Ring(self, [128, KC, 128], BF16, 2)
            wvr = Ring(self, [128, KC, 128], BF16, 2)
            psr = Ring(self, [128, 512], F32, 4, psum=True)
            a_l = Ring(self, [128, LLAT + 2], F32, 2)
            a_c = Ring(self, [128, LCTX + 2], F32, 2)
            v_r = Ring(self, [128, LLAT], F32, 2)
            c_r = Ring(self, [128, LLAT], F32, 2)
            g_r = Ring(self, [128, LLAT], F32, 2)
            gu_r = Ring(self, [128, LLAT], BF16, 2)
            for rg in (a_l, a_c):
                for (ap, b) in rg.items:
                    P.op("vector", (lambda ap=ap: lambda e: e.memset(ap, 0.0))(), writes=[b])
            for j in range(NFF):
                wa, b_wa = war.next()
                wv, b_wv = wvr.next()
                self.load_w_chunk(wa, b_wa, ffn_up[l, :, :, j * 128:(j + 1) * 128])
                self.load_w_chunk(wv, b_wv, ffn_up[l, :, :, DFF + j * 128:DFF + (j + 1) * 128])
                for (s, kind, soff, ln, mc) in all_segments(with_ctx):
                    strip, b_st = (a_l if kind == "l" else a_c).next()
                    vv, b_vv = v_r.next()
                    for bo in range(0, ln, 512):
                        n = min(512, ln - bo)
                        off = soff + bo
                        ps, b_ps = psr.next()
                        for kc in range(KC):
                            self.mm(ps[:, 0:n], wa[:, kc, :], U[:, kc, off:off + n], kc == 0, kc == KC - 1, [b_wa], [b_ps])
                        self.cp("scalar", strip[:, 1 + bo:1 + bo + n], ps[:, 0:n], [b_ps], [b_st])
                        ps2, b_ps2 = psr.next()
                        for kc in range(KC):
                            self.mm(ps2[:, 0:n], wv[:, kc, :], U[:, kc, off:off + n], kc == 0, kc == KC - 1, [b_wv], [b_ps2])
                        self.cp("vector", vv[:, bo:bo + n], ps2[:, 0:n], [b_ps2], [b_vv])
                    cv, b_cv = c_r.next()
                    self.conv3(strip, b_st, ln, self.ffcw[:, l, j, :], cv[:, 0:ln], b_cv)
                    gl, b_gl = g_r.next()
                    self.act(gl[:, 0:ln], cv[:, 0:ln], AF.Gelu_apprx_tanh, [b_cv, self.b_cst], [b_gl], bias=self.ffcw[:, l, j, 3:4], scale=1.0)
                    gu, b_gu = gu_r.next()
                    self.tt("vector", gu[:, 0:ln], gl[:, 0:ln], vv[:, 0:ln], ALU.mult, [b_gl, b_vv], [b_gu])
                    P.dma("sync", guT[j, :, soff:soff + ln], gu[:, 0:ln], reads=[b_gu])
        with self.phase("ffn_dn%d" % l):
            wt = self.T([128, NFF, D], BF16)
            b_wt = P.buf()
            for kc in range(NFF):
                P.dma("gpsimd", wt[:, kc, :], ffn_dn[l, :, kc, :], writes=[b_wt])
            self.linear_residual(l, wt, b_wt, NFF, guT, hsrc, hdst, 40, with_ctx, outT)

    def phase_l1_proj(self, hsrc, w_in, zsT, xsT, xstok, bcT, btok, dltok):
        P = self.P
        blocks = all_blocks(True)
        with contextlib.ExitStack() as ust:
          U = self.modulate_all(ust, hsrc, 1, self.Am, 0, blocks)
          with self.phase("l1proj"):
            wring = Ring(self, [128, KC, 128], BF16, 2)
            psr = Ring(self, [128, 512], F32, 3, psum=True)
            tpr = Ring(self, [128, 512], BF16, 2, psum=True)
            oring = Ring(self, [128, 512], BF16, 3)
            o32 = Ring(self, [128, 512], F32, 2)
            a_l = Ring(self, [128, LLAT + 2], F32, 2)
            a_c = Ring(self, [128, LCTX + 2], F32, 2)
            c_r = Ring(self, [128, LLAT], F32, 2)
            s_r = Ring(self, [128, LLAT], BF16, 2)
            stg_r = Ring(self, [64, 32, 128], BF16, 2)
            for rg in (a_l, a_c):
                for (ap, b) in rg.items:
                    P.op("vector", (lambda ap=ap: lambda e: e.memset(ap, 0.0))(), writes=[b])
            for j in range(16):
                wt, b_wt = wring.next()
                self.load_w_chunk(wt, b_wt, w_in[:, :, j * 128:(j + 1) * 128])
                for (s, kind, off, n, mc) in all_blocks(False):
                    ps, b_ps = psr.next()
                    for kc in range(KC):
                        self.mm(ps[:, 0:n], wt[:, kc, :], U[:, kc, off:off + n], kc == 0, kc == KC - 1, [b_wt], [b_ps])
                    ot, b_ot = oring.next()
                    self.act(ot[:, 0:n], ps[:, 0:n], AF.Silu, [b_ps], [b_ot])
                    P.dma("sync", zsT[j, :, off:off + n], ot[:, 0:n], reads=[b_ot])
            wd = self.T([128, KC, 64], BF16)
            b_wd = P.buf()
            self.load_w_chunk(wd, b_wd, w_in[:, :, 5120:5184])
            dstg_r = Ring(self, [64, 8, 4, 64], F32, 2)
            dhb_r = Ring(self, [64, 64], BF16, 2)
            dtmp_r = Ring(self, [64, 64], F32, 2)
            for (s, kind, off, n, mc) in blocks:
                dstg, b_dstg = dstg_r.next()
                nr = n // 64
                for r in range(nr):
                    ps, b_ps = psr.next()
                    for kc in range(KC):
                        self.mm(ps[0:64, 0:64], U[:, kc, off + r * 64:off + (r + 1) * 64], wd[:, kc, :], kc == 0, kc == KC - 1,
                                [b_wd], [b_ps])
                    dt_, b_dt_ = dtmp_r.next()
                    self.tt("vector", dt_, ps[0:64, 0:64], self.dtb[0:64, :], ALU.add, [b_ps, self.b_cst], [b_dt_])
                    self.act(dt_, dt_, AF.Exp, [b_dt_], [b_dt_])
                    self.act(dstg[:, r, 0, :], dt_, AF.Ln, [b_dt_, self.b_cst], [b_dstg], bias=self.one_t[0:64, 0:1], scale=1.0)
                    self.tt("gpsimd", dstg[:, r, 1, :], dstg[:, r, 0, :], self.aneg[0:64, :], ALU.mult, [b_dstg, self.b_cst], [b_dstg])
                    hb, b_hb = dhb_r.next()
                    self.cp("gpsimd", hb, dstg[:, r, 1, :], [b_dstg], [b_hb])
                    self.cp("gpsimd", dstg[:, r, 2, :], hb, [b_hb], [b_dstg])
                    self.tt("gpsimd", dstg[:, r, 3, :], dstg[:, r, 1, :], dstg[:, r, 2, :], ALU.subtract, [b_dstg], [b_dstg])
                r0 = off // 64
                P.dma("sync", dltok[r0:r0 + nr].rearrange("r p a f -> p r a f"), dstg[:, 0:nr], reads=[b_dstg])
            pending = []
            for j in range(24):
                wt, b_wt = wring.next()
                self.load_w_chunk(wt, b_wt, w_in[:, :, 2048 + j * 128:2048 + (j + 1) * 128])
                for (s, kind, soff, ln, mc) in all_segments(True):
                    strip, b_st = (a_l if kind == "l" else a_c).next()
                    for bo in range(0, ln, 512):
                        n = min(512, ln - bo)
                        off = soff + bo
                        ps, b_ps = psr.next()
                        for kc in range(KC):
                            self.mm(ps[:, 0:n], wt[:, kc, :], U[:, kc, off:off + n], kc == 0, kc == KC - 1, [b_wt], [b_ps])
                        self.cp("scalar", strip[:, 1 + bo:1 + bo + n], ps[:, 0:n], [b_ps], [b_st])
                    while pending:
                        pending.pop(0)()
                    cv, b_cv = c_r.next()
                    self.conv3(strip, b_st, ln, self.sscw[:, j, :], cv[:, 0:ln], b_cv)
                    so, b_so = s_r.next()
                    self.act(so[:, 0:ln], cv[:, 0:ln], AF.Silu, [b_cv, self.b_cst], [b_so], bias=self.sscw[:, j, 3:4], scale=1.0)
                    if j < 16:
                        P.dma("sync", xsT[j, :, soff:soff + ln], so[:, 0:ln], reads=[b_so])
                    else:
                        P.dma("sync", bcT[j - 16, :, soff:soff + ln], so[:, 0:ln], reads=[b_so])
                    if j < 20:
                        def tok_copy(so=so, b_so=b_so, ln=ln, soff=soff, j=j):
                            stg, b_stg = stg_r.next()
                            nr = ln // 64
                            for r in range(nr):
                                tp, b_tp = tpr.next()
                                self.tr(tp[0:64, 0:128], so[:, r * 64:(r + 1) * 64], self.ident, [b_so, self.b_cst], [b_tp])
                                self.cp("scalar" if r % 2 else "vector", stg[:, r, :], tp[0:64, 0:128], [b_tp], [b_stg])
                            r0 = soff // 64
                            tdst = xstok[r0:r0 + nr, :, j * 128:(j + 1) * 128] if j < 16 else btok[r0:r0 + nr, :, (j - 16) * 128:(j - 15) * 128]
                            P.dma("gpsimd", tdst.rearrange("r p f -> p r f"), stg[:, 0:nr, :], reads=[b_stg])
                        pending.append(tok_copy)
            while pending:
                pending.pop(0)()

    def phase_ssd2(self, xstok, btok, bcT, dltok, yfb):
        P = self.P
        with self.phase("ssd2"):
            LE, GE, GT, LT = [self.masks[:, i, :] for i in range(4)]
            bc_r = Ring(self, [128, 8, SEQT], BF16, 1)
            x_r = Ring(self, [64, 2048], BF16, 3)
            bt_r = Ring(self, [64, 512], BF16, 4)
            dl_r = Ring(self, [64, 4, 64], F32, 3)
            lahl_r = Ring(self, [128, 32], F32, 3)
            ula_r = Ring(self, [128, 2048], BF16, 2)
            mtall_r = Ring(self, [64, 2048], BF16, 3)
            e12_r = Ring(self, [64, 64], F32, 4)
            etot_r = Ring(self, [128, 32], F32, 4)
            w_r = Ring(self, [64, 32], F32, 3)
            cbm_r = Ring(self, [64, 256], F32, 4)
            ed_r = Ring(self, [64, 512], F32, 2)
            mt_r = Ring(self, [64, 512], BF16, 3)
            xdt_r = Ring(self, [64, 2048], BF16, 3)
            xw_r = Ring(self, [64, 2048], BF16, 4)
            tmp_r = Ring(self, [64, 512], F32, 2)
            yrow_r = Ring(self, [64, 2048], BF16, 2)
            sm_ps = Ring(self, [128, 512], F32, 1, psum=True)
            D_ps = Ring(self, [128, 512], F32, 2, psum=True)
            y_ps = Ring(self, [128, 512], F32, 2, psum=True)
            in_ps = Ring(self, [128, 512], F32, 1, psum=True)
            kv_ps = Ring(self, [128, 512], F32, 2, psum=True)
            ST = [self.T([128, 2048], F32) for _ in range(2)]
            STbf = [self.T([128, 2048], BF16) for _ in range(2)]
            for s in range(2):
                sl = slice(s * SEQT, (s + 1) * SEQT)
                bc, b_bc = bc_r.next()
                P.dma("sync", bc, bcT[:, :, sl].rearrange("c p t -> p c t"), writes=[b_bc])
                bST = [[P.buf() for _ in range(4)] for _ in range(2)]
                bSTbf = [[P.buf() for _ in range(4)] for _ in range(2)]
                has_state = [False, False]
                order = [list(range(36)), [3, 2, 1, 0] + list(range(35, 3, -1))]
                seq_items = []
                for k in range(36):
                    for d in range(2):
                        seq_items.append((d, order[d][k]))

                def stageA(d, row):
                    A1 = LE if d == 0 else GE
                    A2 = GT if d == 0 else LT
                    need_out = row >= 4
                    grow = s * 36 + row
                    tok0 = row * 64
                    x, b_x = x_r.next()
                    bt, b_bt = bt_r.next()
                    dl, b_dl = dl_r.next()
                    P.dma("sync", x, xstok[grow], writes=[b_x])
                    P.dma("sync", bt, btok[grow], writes=[b_bt])
                    P.dma("sync", dl, dltok[grow], writes=[b_dl])
                    dt_d = dl[:, 0, d * 32:(d + 1) * 32]
                    la_d = dl[:, 1, d * 32:(d + 1) * 32]
                    sp, b_sp = sm_ps.next()
                    self.mm(sp[0:64, 0:32], A1, la_d, True, True, [b_dl, self.b_cst], [b_sp])
                    self.mm(sp[0:64, 32:64], A2, la_d, True, True, [b_dl, self.b_cst], [b_sp])
                    self.mm(sp[0:128, 64:96], self.onesf, la_d, True, True, [b_dl, self.b_cst], [b_sp])
                    e12, b_e12 = e12_r.next()
                    etot, b_et = etot_r.next()
                    self.act(e12, sp[0:64, 0:64], AF.Exp, [b_sp], [b_e12])
                    self.act(etot, sp[0:128, 64:96], AF.Exp, [b_sp], [b_et])
                    w, b_w = w_r.next()
                    self.tt("vector", w, e12[:, 32:64], dt_d, ALU.mult, [b_e12, b_dl], [b_w])
                    xv = x.rearrange("p (h q) -> p h q", q=64)
                    xw, b_xw = xw_r.next()
                    self.tt("gpsimd", xw.rearrange("p (h q) -> p h q", q=64), xv, w.unsqueeze(2).broadcast_to([64, 32, 64]),
                            ALU.mult, [b_x, b_w], [b_xw])
                    C = dict(d=d, row=row, A2=A2, need_out=need_out, tok0=tok0, bt=bt, b_bt=b_bt, e12=e12, b_e12=b_e12,
                             etot=etot, b_et=b_et, xw=xw, b_xw=b_xw)
                    if need_out:
                        lahl, b_lahl = lahl_r.next()
                        P.dma("sync", lahl[0:64, :], dltok[grow, :, 2, d * 32:(d + 1) * 32], writes=[b_lahl])
                        P.dma("sync", lahl[64:128, :], dltok[grow, :, 3, d * 32:(d + 1) * 32], writes=[b_lahl])
                        ula, b_ula = ula_r.next()
                        self.tt("gpsimd", ula.rearrange("p (h t) -> p h t", t=64), lahl.unsqueeze(2).broadcast_to([128, 32, 64]),
                                self.masks2[:, d, :].unsqueeze(1).broadcast_to([128, 32, 64]), ALU.mult, [b_lahl, self.b_cst], [b_ula])
                        xdt, b_xdt = xdt_r.next()
                        self.tt("vector", xdt.rearrange("p (h q) -> p h q", q=64), xv, dt_d.unsqueeze(2).broadcast_to([64, 32, 64]),
                                ALU.mult, [b_x, b_dl], [b_xdt])
                        cp_, b_cp = sp[:, 128:384], b_sp
                        for g in range(4):
                            self.mm(cp_[0:64, g * 64:(g + 1) * 64], bc[:, g, tok0:tok0 + 64], bc[:, 4 + g, tok0:tok0 + 64], True, True,
                                    [b_bc], [b_cp])
                        cbm, b_cbm = cbm_r.next()
                        self.tt("vector", cbm.rearrange("p (g t) -> p g t", t=64), cp_[0:64, 0:256].rearrange("p (g t) -> p g t", t=64),
                                A1.unsqueeze(1).broadcast_to([64, 4, 64]), ALU.mult, [b_cp, self.b_cst], [b_cbm])
                        mt, b_mt = mtall_r.next()
                        for g in range(4):
                            gs = slice(g * 512, (g + 1) * 512)
                            Dp, b_Dp = D_ps.next()
                            self.mm(Dp[0:64, 0:512], self.masks2b[:, 2 + d, :], ula[:, gs], True, True, [b_ula, self.b_cst], [b_Dp])
                            ed, b_ed = ed_r.next()
                            self.act(ed, Dp[0:64, 0:512], AF.Exp, [b_Dp], [b_ed])
                            self.tt("vector", mt[:, gs].rearrange("p (h t) -> p h t", t=64), ed.rearrange("p (h t) -> p h t", t=64),
                                    cbm[:, g * 64:(g + 1) * 64].unsqueeze(1).broadcast_to([64, 8, 64]), ALU.mult, [b_ed, b_cbm], [b_mt])
                        C.update(xdt=xdt, b_xdt=b_xdt, mt=mt, b_mt=b_mt)
                    return C

                def stageB(C):
                    d, row, A2, tok0 = C["d"], C["row"], C["A2"], C["tok0"]
                    store = None
                    bt, b_bt, e12, b_e12, etot, b_et, xw, b_xw = (C["bt"], C["b_bt"], C["e12"], C["b_e12"], C["etot"], C["b_et"],
                                                                  C["xw"], C["b_xw"])
                    if C["need_out"]:
                        xdt, b_xdt, mt, b_mt = C["xdt"], C["b_xdt"], C["mt"], C["b_mt"]
                        yrow, b_yr = yrow_r.next()
                        for g in range(4):
                            gs = slice(g * 512, (g + 1) * 512)
                            yp, b_yp = y_ps.next()
                            for hh in range(8):
                                h = g * 8 + hh
                                self.mm(yp[0:64, hh * 64:(hh + 1) * 64], mt[:, h * 64:(h + 1) * 64], xdt[:, h * 64:(h + 1) * 64],
                                        True, True, [b_mt, b_xdt], [b_yp])
                            if has_state[d]:
                                ip, b_ip = in_ps.next()
                                self.mm(ip[0:64, 0:512], bc[:, 4 + g, tok0:tok0 + 64], STbf[d][:, gs], True, True,
                                        [b_bc, bSTbf[d][g]], [b_ip])
                                tmp, b_tmp = tmp_r.next()
                                self.tt("vector", tmp.rearrange("p (h q) -> p h q", q=64), ip[0:64, 0:512].rearrange("p (h q) -> p h q", q=64),
                                        e12[:, g * 8:(g + 1) * 8].unsqueeze(2).broadcast_to([64, 8, 64]), ALU.mult, [b_ip, b_e12], [b_tmp])
                                self.tt("vector", yrow[:, gs], tmp, yp[0:64, 0:512], ALU.add, [b_tmp, b_yp], [b_yr])
                            else:
                                self.cp("scalar", yrow[:, gs], yp[0:64, 0:512], [b_yp], [b_yr])
                        lrow = s * 32 + (row - 4)
                        store = (yfb[d, lrow], yrow, b_yr)
                    for g in range(4):
                        gs = slice(g * 512, (g + 1) * 512)
                        kp, b_kp = kv_ps.next()
                        self.mm(kp[:, 0:512], bt[:, g * 128:(g + 1) * 128], xw[:, gs], True, True, [b_bt, b_xw], [b_kp])
                        if has_state[d]:
                            stv = ST[d][:, gs].rearrange("p (h q) -> p h q", q=64)
                            self.tt("gpsimd", stv, stv, etot[:, g * 8:(g + 1) * 8].unsqueeze(2).broadcast_to([128, 8, 64]), ALU.mult,
                                    [bST[d][g], b_et], [bST[d][g]])
                            self.tt("vector", ST[d][:, gs], ST[d][:, gs], kp[:, 0:512], ALU.add, [bST[d][g], b_kp], [bST[d][g]])
                        else:
                            self.cp("vector", ST[d][:, gs], kp[:, 0:512], [b_kp], [bST[d][g]])
                        self.cp("scalar", STbf[d][:, gs], ST[d][:, gs], [bST[d][g]], [bSTbf[d][g]])
                    if store is not None:
                        P.dma("gpsimd", store[0], store[1], reads=[store[2]])
                    has_state[d] = True

                LAG = 2
                ctxs = {}
                for t_ in range(len(seq_items) + LAG):
                    if t_ < len(seq_items):
                        ctxs[t_] = stageA(*seq_items[t_])
                    if t_ >= LAG:
                        stageB(ctxs.pop(t_ - LAG))

    def phase_ssd_fin(self, zsT, xsT, yfb, mix2T):
        P = self.P
        with self.phase("ssdfin"):
            yf_r = Ring(self, [64, 2048], BF16, 3)
            yb_r = Ring(self, [64, 2048], BF16, 3)
            yT_r = Ring(self, [128, 16, 512], F32, 2)
            tp_ps = Ring(self, [128, 512], F32, 3, psum=True)
            n_ps = Ring(self, [128, 512], F32, 1, psum=True)
            xf_r = Ring(self, [128, 4, 512], BF16, 2)
            zf_r = Ring(self, [128, 4, 512], BF16, 2)
            y_r = Ring(self, [128, 4, 512], F32, 1)
            sq_r = Ring(self, [128, 4, 512], BF16, 1)
            r_r = Ring(self, [128, 512], F32, 2)
            yo_r = Ring(self, [128, 4, 512], BF16, 2)
            def gather(s, b):
                yT, b_yT = yT_r.next()
                for rr in range(8):
                    lrow = s * 32 + b * 8 + rr
                    yf, b_yf = yf_r.next()
                    yb, b_yb = yb_r.next()
                    P.dma("sync", yf, yfb[0, lrow], writes=[b_yf])
                    P.dma("sync", yb, yfb[1, lrow], writes=[b_yb])
                    for half in range(2):
                        tp, b_tp = tp_ps.next()
                        for c8 in range(8):
                            c = half * 8 + c8
                            self.mm(tp[:, c8 * 64:(c8 + 1) * 64], yf[:, c * 128:(c + 1) * 128], self.ident[0:64, 0:64], True, False,
                                    [b_yf, self.b_cst], [b_tp])
                            self.mm(tp[:, c8 * 64:(c8 + 1) * 64], yb[:, c * 128:(c + 1) * 128], self.ident[0:64, 0:64], False, True,
                                    [b_yb, self.b_cst], [b_tp])
                        self.cp("scalar" if half == 0 else "vector", yT[:, half * 8:(half + 1) * 8, rr * 64:(rr + 1) * 64],
                                tp[:, 0:512].rearrange("p (c t) -> p c t", t=64), [b_tp], [b_yT])
                return (yT, b_yT)

            def finish(s, b, ctx):
                yT, b_yT = ctx
                to = LCTX + 512 * b
                off = s * SEQT + to
                for g in range(4):
                    xf, b_xf = xf_r.next()
                    zf, b_zf = zf_r.next()
                    P.dma("sync", xf, xsT[4 * g:4 * g + 4, :, off:off + 512].rearrange("c p t -> p c t"), writes=[b_xf])
                    P.dma("sync", zf, zsT[4 * g:4 * g + 4, :, off:off + 512].rearrange("c p t -> p c t"), writes=[b_zf])
                    y, b_y = y_r.next()
                    sq, b_sq = sq_r.next()
                    ps, b_ps = n_ps.next()
                    for pr in range(4):
                        cix = 4 * g + pr
                        self.stt("vector", y[:, pr, :], xf[:, pr, :], self.dsk[:, cix:cix + 1], yT[:, cix, :],
                                 ALU.mult, ALU.add, [b_xf, b_yT, self.b_cst], [b_y])
                    self.tt("gpsimd", y, y, zf, ALU.mult, [b_y, b_zf], [b_y])
                    self.act(sq, y, AF.Square, [b_y], [b_sq])
                    for pr in range(4):
                        self.mm(ps[:, 0:512], self.ones, sq[:, pr, :], pr == 0, pr == 3, [b_sq], [b_ps])
                    r, b_r = r_r.next()
                    self.rsqrt_from(r, ps[:, 0:512], 1.0 / 512, [b_ps], b_r)
                    yo, b_yo = yo_r.next()
                    for pr in range(4):
                        cix = 4 * g + pr
                        self.stt("vector", yo[:, pr, :], y[:, pr, :], self.sng[:, cix:cix + 1], r, ALU.mult, ALU.mult,
                                 [b_y, b_r, self.b_cst], [b_yo])
                    P.dma("gpsimd", mix2T[4 * g:4 * g + 4, :, off:off + 512].rearrange("c p t -> p c t"), yo, reads=[b_yo])

            blks = [(s, b) for s in range(2) for b in range(4)]
            ctxs = {0: gather(*blks[0])}
            for i in range(len(blks)):
                if i + 1 < len(blks):
                    ctxs[i + 1] = gather(*blks[i + 1])
                finish(blks[i][0], blks[i][1], ctxs.pop(i))

    def phase_ssd(self, zsT, xsT, xstok, bcT, dtT, mix2T):
        P = self.P
        with self.phase("ssd"):
            NCH = 4
            W = self.scan_setup(NCH, 64)
            Br = Ring(self, [128, SEQT], BF16, 1)
            Cr = Ring(self, [128, SEQT], BF16, 1)
            dtr = Ring(self, [64, SEQT], F32, 1)
            xtr = Ring(self, [64, 36, 512], BF16, 1)
            oaccs = [(self.T([128, SEQT], F32), None) for _ in range(4)]
            dps = W["a_ps"]
            dtb_r = Ring(self, [128, 512], F32, 2)
            la_r = Ring(self, [128, 512], F32, 2)
            kk_r = Ring(self, [128, 512], F32, 2)
            xf_r = Ring(self, [128, 4, 512], BF16, 1)
            zf_r = Ring(self, [128, 4, 512], BF16, 1)
            y_r = Ring(self, [128, 4, 512], F32, 1)
            sq_r = Ring(self, [128, 4, 512], BF16, 1)
            r_r = Ring(self, [128, 512], F32, 1)
            yo_r = Ring(self, [128, 4, 512], BF16, 1)
            fwd = [(0, LCTX)] + [(LCTX + 512 * b, 512) for b in range(4)]
            bwd = [(0, LCTX)] + [(LCTX + 512 * b, 512) for b in (3, 2, 1, 0)]
            for s in range(2):
                sl = slice(s * SEQT, (s + 1) * SEQT)
                dt, b_dt = dtr.next()
                P.dma("sync", dt, dtT[:, sl], writes=[b_dt])
                for g in range(4):
                    Bt, b_B = Br.next()
                    Ct, b_C = Cr.next()
                    xt, b_xt = xtr.next()
                    P.dma("sync", Bt, bcT[g, :, sl], writes=[b_B])
                    P.dma("sync", Ct, bcT[4 + g, :, sl], writes=[b_C])
                    P.dma("sync", xt, xstok[s * 36:(s + 1) * 36, :, g * 512:(g + 1) * 512].rearrange("r p f -> p r f"), writes=[b_xt])
                    b_oas = [P.buf() for _ in range(4)]
                    for pr in range(4):
                        P.op("gpsimd", (lambda t=oaccs[pr][0]: lambda e: e.memset(t, 0.0))(), writes=[b_oas[pr]])
                    jobs = []
                    for hh in range(8):
                        for (dr, sg) in ((0, +1), (1, -1)):
                            jobs.append((hh, dr, sg))
                    for j0 in range(0, len(jobs), NCH):
                        batch = jobs[j0:j0 + NCH]
                        for ci in range(len(batch)):
                            self.chain_reset(W["chains"][ci])
                        for step in range(5):
                            preps = []
                            for ci, (hh, dr, sg) in enumerate(batch):
                                ch = W["chains"][ci]
                                to, n = (fwd if sg > 0 else bwd)[step]
                                hidx = g * 8 + hh
                                col = dr * 32 + hidx
                                dtv, b_dtv = dtb_r.next()
                                P.dma("sync", dtv[:, 0:n], dtT[col:col + 1, s * SEQT + to:s * SEQT + to + n].partition_broadcast(128),
                                      writes=[b_dtv])
                                self.act(dtv[:, 0:n], dtv[:, 0:n], AF.Exp, [b_dtv, self.b_cst], [b_dtv], bias=self.dtb[:, col:col + 1], scale=1.0)
                                self.act(dtv[:, 0:n], dtv[:, 0:n], AF.Ln, [b_dtv, self.b_cst], [b_dtv], bias=self.one_t[:, 0:1], scale=1.0)
                                la, b_la = la_r.next()
                                self.ts("vector", la[:, 0:n], dtv[:, 0:n], self.aneg[:, col:col + 1], None, ALU.mult, None,
                                        [b_dtv, self.b_cst], [b_la])
                                kk, b_kk = kk_r.next()
                                self.tt("vector", kk[:, 0:n], Bt[:, to:to + n], dtv[:, 0:n], ALU.mult, [b_B, b_dtv], [b_kk])
                                pair = hh // 2
                                pbo = 64 * (hh % 2)
                                oacc = oaccs[pair][0]
                                o_out = None
                                if to >= LCTX:
                                    o_out = (oacc[pbo:pbo + 64, to:to + n], b_oas[pair])
                                preps.append(self.scan_prep(W, ch, Ct[:, to:to + n], kk[:, 0:n], la[:, 0:n], [b_C, b_kk, b_la],
                                                            (lambda i, to=to, xt=xt, b_xt=b_xt, hh=hh: (xt[:, to // 64 + i, hh * 64:(hh + 1) * 64], b_xt)),
                                                            n, 64, sg, o_out, pbo, scalar=True))
                            self.scan_step(preps)
                    for b in range(4):
                        to = LCTX + 512 * b
                        off = s * SEQT + to
                        xf, b_xf = xf_r.next()
                        zf, b_zf = zf_r.next()
                        P.dma("sync", xf, xsT[4 * g:4 * g + 4, :, off:off + 512].rearrange("c p t -> p c t"), writes=[b_xf])
                        P.dma("sync", zf, zsT[4 * g:4 * g + 4, :, off:off + 512].rearrange("c p t -> p c t"), writes=[b_zf])
                        y, b_y = y_r.next()
                        sq, b_sq = sq_r.next()
                        ps, b_ps = dps.next()
                        for pr in range(4):
                            cix = 4 * g + pr
                            self.stt("vector", y[:, pr, :], xf[:, pr, :], self.dsk[:, cix:cix + 1], oaccs[pr][0][:, to:to + 512],
                                     ALU.mult, ALU.add, [b_xf, b_oas[pr], self.b_cst], [b_y])
                        self.tt("vector", y, y, zf, ALU.mult, [b_y, b_zf], [b_y])
                        self.act(sq, y, AF.Square, [b_y], [b_sq])
                        for pr in range(4):
                            self.mm(ps[:, 0:512], self.ones, sq[:, pr, :], pr == 0, pr == 3, [b_sq], [b_ps])
                        r, b_r = r_r.next()
                        self.rsqrt_from(r, ps[:, 0:512], 1.0 / 512, [b_ps], b_r)
                        yo, b_yo = yo_r.next()
                        for pr in range(4):
                            cix = 4 * g + pr
                            self.stt("vector", yo[:, pr, :], y[:, pr, :], self.sng[:, cix:cix + 1], r, ALU.mult, ALU.mult,
                                     [b_y, b_r, self.b_cst], [b_yo])
                        P.dma("gpsimd", mix2T[4 * g:4 * g + 4, :, off:off + 512].rearrange("c p t -> p c t"), yo, reads=[b_yo])


def _fm(v, nchunk):
    return np.ascontiguousarray(np.asarray(v, np.float32).reshape(nchunk, 128).T)


def _wt(w):
    K, N = w.shape
    return np.ascontiguousarray(np.asarray(w, np.float32).reshape(K // 128, 128, N).transpose(1, 0, 2))


def host_inputs(I):
    f32 = np.float32
    sh = {}
    sh["wmod"] = np.stack([_wt(I["w_mod"][l]) for l in range(2)])
    sh["bmod"] = np.ascontiguousarray(np.asarray(I["b_mod"], f32).reshape(2, 48, 128).transpose(2, 0, 1))
    sh["nmix"] = np.ascontiguousarray(np.asarray(I["norm_mix"], f32).reshape(2, KC, 128).transpose(2, 0, 1))
    sh["nffn"] = np.ascontiguousarray(np.asarray(I["norm_ffn"], f32).reshape(2, KC, 128).transpose(2, 0, 1))
    sh["ffn_up"] = np.stack([_wt(I["ffn_w_up"][l]) for l in range(2)])
    cw = np.concatenate([np.asarray(I["ffn_conv_w"], f32), np.asarray(I["ffn_conv_b"], f32)[:, None, :]], axis=1)
    sh["ffn_cw"] = np.ascontiguousarray(cw.reshape(2, 4, NFF, 128).transpose(3, 0, 2, 1))
    sh["ffn_dn"] = np.stack([_wt(I["ffn_w_down"][l]) for l in range(2)])
    sh["hy_in"] = _wt(I["hy_w_in"][0])
    sh["hy_out"] = _wt(I["hy_w_out"][0])
    sh["na_g"] = np.ascontiguousarray(np.stack([np.tile(np.asarray(I["na_q_gain"][0], f32), 2),
                                                np.tile(np.asarray(I["na_k_gain"][0], f32), 2)], axis=1))
    rpb = np.asarray(I["na_rpb"][0], f32)
    cp = np.arange(64)[:, None]
    cq = np.arange(64)[None, :]
    dc = np.clip(cp - cq + 15, 0, 30)
    tab = rpb[:, ::-1, :][:, :, dc]
    sh["na_bias"] = np.ascontiguousarray(tab.transpose(2, 0, 1, 3).reshape(64, 8 * 15 * 64))
    ws = np.clip(np.arange(64) - 8, 0, 48)[None, :]
    inwin = (cp >= ws) & (cp < ws + 16)
    sh["na_mask"] = np.where(inwin, 0.0, -30000.0).astype(f32)
    sh["hg_gain"] = np.asarray(I["hg_out_gain"][0], f32).reshape(128, 1).copy()
    lb = np.stack([np.asarray(I["hg_lb_fwd"], f32), np.asarray(I["hg_lb_bwd"], f32)])
    sh["hg_lb"] = np.ascontiguousarray(lb.reshape(2, 3, 4, 128).transpose(3, 0, 1, 2))
    sh["ssd_in"] = _wt(I["ssd_w_in"][0])
    scw = np.concatenate([np.asarray(I["ssd_conv_w"][0], f32), np.asarray(I["ssd_conv_b"][0], f32)[None, :]], axis=0)
    sh["ssd_cw"] = np.ascontiguousarray(scw.reshape(4, 24, 128).transpose(2, 1, 0))
    dtb = np.concatenate([np.asarray(I["ssd_dt_bias_fwd"][0], f32), np.asarray(I["ssd_dt_bias_bwd"][0], f32)])
    sh["ssd_dtb"] = np.ascontiguousarray(np.broadcast_to(dtb[None, :], (128, 64)))
    al = np.concatenate([np.asarray(I["ssd_a_log_fwd"][0], f32), np.asarray(I["ssd_a_log_bwd"][0], f32)])
    sh["ssd_alog"] = np.ascontiguousarray(np.broadcast_to(al[None, :], (128, 64)))
    dsk = np.repeat(np.asarray(I["ssd_d"][0], f32), 64)
    sh["ssd_dsk"] = _fm(dsk, 16)
    sh["ssd_ng"] = _fm(I["ssd_norm_gain"][0], 16)
    sh["ssd_out"] = _wt(I["ssd_w_out"][0])
    sh["cst_ident"] = np.eye(128, dtype=f32)
    bo = np.zeros((128, 128), f32)
    bo[0:64, 0:64] = 1.0
    bo[64:128, 64:128] = 1.0
    sh["cst_bones"] = bo
    si = np.arange(64)[:, None]
    ti = np.arange(64)[None, :]
    sh["cst_masks"] = np.ascontiguousarray(np.stack([(si <= ti), (si >= ti), (si > ti), (si < ti)], axis=1).astype(f32))
    sh["cst_masks2"] = np.ascontiguousarray(np.concatenate([sh["cst_masks"], sh["cst_masks"]], axis=0))
    rm = np.ones((128, 512), f32)
    rm[:, ::64] = 0.0
    sh["cst_reset"] = rm
    per_core = []
    x = np.asarray(I["x"], f32)
    ctx = np.asarray(I["ctx"], f32)
    c = np.asarray(I["c"], f32)
    cc = np.asarray(I["c_ctx"], f32)
    for i in range(8):
        toks = np.concatenate([ctx[2 * i], x[2 * i], ctx[2 * i + 1], x[2 * i + 1]], axis=0)
        hin = np.ascontiguousarray(toks.T.reshape(KC, 128, NT))
        cm = np.stack([c[2 * i], c[2 * i + 1], cc, cc], axis=1)
        cTt = np.ascontiguousarray(cm.reshape(KC, 128, 4).transpose(1, 0, 2))
        d = dict(sh)
        d["hin"] = hin
        d["cT"] = cTt
        per_core.append(d)
    return per_core


_CACHE = {}


def build_program(dbg=(), stop_after=None):
    key = (tuple(sorted(dbg)), stop_after)
    if key not in _CACHE:
        kb = KB(dbg, stop_after)
        kb.build()
        _CACHE[key] = kb
    return _CACHE[key]


def kernel(**inputs):
    kb = build_program()
    in_maps = host_inputs(inputs)
    names = set(kb.inputs.keys())
    in_maps = [{k: v for k, v in m.items() if k in names} for m in in_maps]
    res = run_bass_kernel_spmd(kb.nc, in_maps, core_ids=list(range(8)))
    out = np.empty((16, LLAT, D), np.float32)
    for i in range(8):
        o = np.asarray(res.results[i]["outT"], np.float32).reshape(D, 2 * LLAT)
        out[2 * i] = o[:, 0:LLAT].T
        out[2 * i + 1] = o[:, LLAT:2 * LLAT].T
    return out
```

```python
import contextlib
import numpy as np
import concourse.bass as bass
import concourse.mybir as mybir
from concourse.bass_utils import run_bass_kernel_spmd

F32 = mybir.dt.float32
BF16 = mybir.dt.bfloat16
AF = mybir.ActivationFunctionType
ALU = mybir.AluOpType

COMPUTE = ("tensor", "vector", "scalar", "gpsimd")
ALLENG = ("tensor", "vector", "scalar", "gpsimd", "sync")
NDSEM = 8

D = 1024
KC = 8
LCTX = 256
LLAT = 2048
SEQT = LCTX + LLAT
NT = 2 * SEQT
NROW = NT // 64
DFF = 2816
NFF = DFF // 128
EPS = 1e-6


class Buf:
    __slots__ = ("name", "writer", "readers", "dma_readers", "excl")

    def __init__(self, name=""):
        self.name = name
        self.excl = False
        self.writer = None
        self.readers = {}
        self.dma_readers = []

    def reset(self):
        self.writer = None
        self.readers = {}
        self.dma_readers = []


class Prog:
    def __init__(self, nc, stack):
        self.nc = nc
        self.ops = []
        self.bufs = []
        self.csem = {e: stack.enter_context(nc.semaphore("s_" + e)) for e in COMPUTE}
        self.dsem = {e: [stack.enter_context(nc.semaphore("d_%s_%d" % (e, j))) for j in range(NDSEM)]
                     for e in ("sync", "gpsimd", "scalar")}
        self.bar = stack.enter_context(nc.semaphore("bar"))
        self.cnt = {e: 0 for e in COMPUTE}
        self.dcnt = {e: 0 for e in self.dsem}
        self.nphase = 0
        self.total_ops = 0

    def buf(self, name=""):
        b = Buf(name)
        self.bufs.append(b)
        return b

    def op(self, eng, fn, reads=(), writes=(), dma=False):
        ex = [b for b in reads if b.excl and b not in writes]
        if ex:
            writes = list(writes) + ex
            reads = [b for b in reads if not b.excl]
        idx = len(self.ops)
        deps = set()
        for b in reads:
            if b.writer is not None:
                deps.add(b.writer)
        for b in writes:
            if b.writer is not None:
                deps.add(b.writer)
            for r in b.readers.values():
                deps.add(r)
            for r in b.dma_readers:
                deps.add(r)
        self.ops.append(dict(eng=eng, fn=fn, deps=deps, dma=dma, signal=False))
        for b in writes:
            b.writer = idx
            b.readers = {}
            b.dma_readers = []
        for b in reads:
            if b in writes:
                continue
            if dma:
                b.dma_readers.append(idx)
            else:
                b.readers[eng] = idx
        return idx

    def dma(self, q, out, in_, reads=(), writes=(), **kw):
        return self.op(q, lambda e: e.dma_start(out=out, in_=in_, **kw), reads, writes, dma=True)

    def emit_phase(self):
        nc = self.nc
        ops = self.ops
        if not ops:
            return
        for o in ops:
            nd = set()
            for d in o["deps"]:
                od = ops[d]
                if (not od["dma"]) and (not o["dma"]) and od["eng"] == o["eng"] == "tensor":
                    continue
                nd.add(d)
            o["deps"] = nd
            for d in nd:
                ops[d]["signal"] = True
        last_c = {}
        for o in ops:
            e = o["eng"]
            if o["dma"]:
                k = self.dcnt[e]
                self.dcnt[e] = k + 1
                o["dslot"] = k % NDSEM
                o["dval"] = 16 * (k // NDSEM + 1)
            else:
                last_c[e] = o
        for e, o in last_c.items():
            o["signal"] = True
        for o in ops:
            if (not o["dma"]) and o["signal"]:
                e = o["eng"]
                self.cnt[e] += 1
                o["seq"] = self.cnt[e]
        csem, dsem, bar = self.csem, self.dsem, self.bar
        phase = self.nphase
        cnt_end = dict(self.cnt)
        dcnt_end = dict(self.dcnt)

        def body_for(ename):
            def body(eng):
                if phase > 0:
                    eng.wait_ge(bar, len(ALLENG) * phase)
                seen = {}
                for o in ops:
                    if o["eng"] != ename:
                        continue
                    waits = []
                    for d in o["deps"]:
                        od = ops[d]
                        if od["dma"]:
                            waits.append((("d", od["eng"], od["dslot"]), od["dval"]))
                        else:
                            waits.append((("c", od["eng"]), od["seq"]))
                    if o["dma"] and o["dval"] > 16:
                        waits.append((("d", ename, o["dslot"]), o["dval"] - 16))
                    best = {}
                    for k, v in waits:
                        if v > best.get(k, 0):
                            best[k] = v
                    for k, v in best.items():
                        if seen.get(k, 0) >= v:
                            continue
                        seen[k] = v
                        s = csem[k[1]] if k[0] == "c" else dsem[k[1]][k[2]]
                        eng.wait_ge(s, v)
                    ins = o["fn"](eng)
                    if o["dma"]:
                        ins.then_inc(dsem[ename][o["dslot"]], 16)
                    elif o["signal"]:
                        ins.then_inc(csem[ename], 1)
                if ename in COMPUTE and cnt_end[ename] > 0:
                    eng.wait_ge(csem[ename], cnt_end[ename])
                if ename in dsem:
                    k = dcnt_end[ename]
                    for sl in range(NDSEM):
                        n = (k - sl + NDSEM - 1) // NDSEM if k > sl else 0
                        if n > 0:
                            eng.wait_ge(dsem[ename][sl], 16 * n)
                eng.sem_inc(bar, 1)
            return body

        with nc.Block() as block:
            for e in ALLENG:
                getattr(block, e)(body_for(e))
        self.nphase += 1
        self.total_ops += len(ops)
        self.ops = []
        for b in self.bufs:
            b.reset()
        self.bufs = []


class Ring:
    def __init__(self, kb, shape, dt, n, psum=False):
        self.items = []
        for _ in range(n):
            if psum:
                ap = kb.PS([128, 512], F32) if dt == F32 else kb.PS([128, 1024], BF16)
            else:
                ap = kb.T(shape, dt)
            b = kb.P.buf()
            b.excl = psum
            self.items.append((ap, b))
        self.i = 0

    def next(self):
        it = self.items[self.i % len(self.items)]
        self.i += 1
        return it


def all_blocks(with_ctx=True):
    out = []
    for s in range(2):
        if with_ctx:
            out.append((s, "c", s * SEQT, LCTX, 2))
        for b in range(4):
            out.append((s, "l", s * SEQT + LCTX + 512 * b, 512, s))
    return out


def all_segments(with_ctx=True):
    out = []
    for s in range(2):
        if with_ctx:
            out.append((s, "c", s * SEQT, LCTX, 2))
        out.append((s, "l", s * SEQT + LCTX, LLAT, s))
    return out


class KB:
    def __init__(self, dbg=(), stop_after=None):
        self.nc = bass.Bass("TRN2", target_bir_lowering=False)
        self.dbg = set(dbg)
        self.stop_after = stop_after
        self.uid = 0
        self.inputs = {}

    def din(self, name, shape):
        order = ["setup", "l0mod", "l0proj", "na", "gla", "l0out", "l0ffn", "l1proj", "ssd", "l1out"]
        first = {"hy_in": 2, "hy_out": 5, "ffn_up": 6, "ffn_dn": 6, "ssd_in": 7, "ssd_out": 9}
        if self.stop_after is not None and name in first and order.index(self.stop_after) < first[name]:
            return self.nc.dram_tensor(name, list(shape), F32, kind="Internal").ap()
        ap = self.nc.dram_tensor(name, list(shape), F32, kind="ExternalInput").ap()
        self.inputs[name] = ap
        return ap

    def dscr(self, name, shape, dt):
        kind = "ExternalOutput" if name in self.dbg else "Internal"
        return self.nc.dram_tensor(name, list(shape), dt, kind=kind).ap()

    def T(self, shape, dt, persistent=False, stack=None):
        self.uid += 1
        st = stack if stack is not None else (self.gst if persistent else self.st)
        return st.enter_context(self.nc.sbuf_tensor("t%d" % self.uid, list(shape), dt)).ap()

    def PS(self, shape=(128, 512), dt=F32):
        self.uid += 1
        return self.st.enter_context(self.nc.psum_tensor("p%d" % self.uid, list(shape), dt)).ap()

    @contextlib.contextmanager
    def phase(self, name):
        with contextlib.ExitStack() as st:
            self.st = st
            yield
            self.P.emit_phase()

    def mm(self, out, lhsT, rhs, start, stop, reads, writes):
        self.P.op("tensor", lambda e: e.matmul(out, lhsT, rhs, start=start, stop=stop), reads, writes)

    def tr(self, out, in_, ident, reads, writes):
        self.P.op("tensor", lambda e: e.transpose(out, in_, ident), reads, writes)

    def act(self, out, in_, func, reads, writes, bias=None, scale=None):
        kw = {}
        if bias is not None:
            kw["bias"] = bias
        if scale is not None:
            kw["scale"] = scale
        self.P.op("scalar", lambda e: e.activation(out=out, in_=in_, func=func, **kw), reads, writes)

    def tt(self, eng, out, in0, in1, op, reads, writes):
        self.P.op(eng, lambda e: e.tensor_tensor(out, in0, in1, op), reads, writes)

    def ts(self, eng, out, in0, s1, s2, op0, op1, reads, writes):
        if s2 is None:
            self.P.op(eng, lambda e: e.tensor_scalar(out, in0, s1, None, op0), reads, writes)
        else:
            self.P.op(eng, lambda e: e.tensor_scalar(out, in0, s1, s2, op0, op1), reads, writes)

    def stt(self, eng, out, in0, scalar, in1, op0, op1, reads, writes):
        self.P.op(eng, lambda e: e.scalar_tensor_tensor(out, in0, scalar, in1, op0, op1), reads, writes)

    def cp(self, eng, out, in_, reads, writes):
        if eng == "scalar":
            self.P.op(eng, lambda e: e.copy(out, in_), reads, writes)
        else:
            self.P.op(eng, lambda e: e.tensor_copy(out, in_), reads, writes)

    def rsqrt_from(self, out, in_, scale, reads, writes_buf):
        self.act(out, in_, AF.Ln, reads + [self.b_cst], [writes_buf], bias=self.eps_t[0:out.shape[0], 0:1], scale=scale)
        self.act(out, out, AF.Exp, [writes_buf], [writes_buf], scale=-0.5)

    def build(self):
        nc = self.nc
        hin = self.din("hin", [KC, 128, NT])
        cT = self.din("cT", [128, KC, 4])
        wmod = self.din("wmod", [2, 128, KC, 6 * D])
        bmod = self.din("bmod", [128, 2, 48])
        nmix = self.din("nmix", [128, 2, KC])
        nffn = self.din("nffn", [128, 2, KC])
        ffn_up = self.din("ffn_up", [2, 128, KC, 2 * DFF])
        ffn_cw = self.din("ffn_cw", [128, 2, NFF, 4])
        ffn_dn = self.din("ffn_dn", [2, 128, NFF, D])
        hy_in = self.din("hy_in", [128, KC, 4096])
        hy_out = self.din("hy_out", [128, KC, D])
        na_g = self.din("na_g", [128, 2])
        na_bias = self.din("na_bias", [64, 8 * 15 * 64])
        na_mask = self.din("na_mask", [64, 64])
        hg_gain = self.din("hg_gain", [128, 1])
        hg_lb = self.din("hg_lb", [128, 2, 3, 4])
        ssd_in = self.din("ssd_in", [128, KC, 5184])
        ssd_cw = self.din("ssd_cw", [128, 24, 4])
        ssd_dtb = self.din("ssd_dtb", [128, 64])
        ssd_alog = self.din("ssd_alog", [128, 64])
        ssd_dsk = self.din("ssd_dsk", [128, 16])
        ssd_ng = self.din("ssd_ng", [128, 16])
        ssd_out = self.din("ssd_out", [128, 16, D])
        cst_ident = self.din("cst_ident", [128, 128])
        cst_bones = self.din("cst_bones", [128, 128])
        cst_masks = self.din("cst_masks", [64, 4, 64])
        cst_masks2 = self.din("cst_masks2", [128, 4, 64])
        cst_reset = self.din("cst_reset", [128, 512])
        outT = nc.dram_tensor("outT", [KC, 128, 2 * LLAT], F32, kind="ExternalOutput").ap()

        hT = self.dscr("hT", [KC, 128, NT], F32)
        qT = self.dscr("qT", [4, 128, NT], BF16)
        kT = self.dscr("kT", [4, 128, NT], BF16)
        vtok = self.dscr("vtok", [NROW, 64, 512], BF16)
        hqT = self.dscr("hqT", [4, 128, NT], BF16)
        kfT = self.dscr("kfT", [2, 4, 128, NT], BF16)
        gfT = self.dscr("gfT", [2, 4, 128, NT], F32)
        vitok = self.dscr("vitok", [NROW, 64, 512], BF16)
        gateT = self.dscr("gateT", [4, 128, NT], BF16)
        mixT = self.dscr("mixT", [KC, 128, NT], BF16)
        guT = self.dscr("guT", [NFF, 128, NT], BF16)
        zsT = self.dscr("zsT", [16, 128, NT], BF16)
        xsT = self.dscr("xsT", [16, 128, NT], BF16)
        xstok = self.dscr("xstok", [NROW, 64, 2048], BF16)
        bcT = self.dscr("bcT", [8, 128, NT], BF16)
        dtT = self.dscr("dtT", [64, NT], F32)
        btok = self.dscr("btok", [NROW, 64, 512], BF16)
        dltok = self.dscr("dltok", [NROW, 64, 4, 64], F32)
        yfb = self.dscr("yfb", [2, 64, 64, 2048], BF16)
        mix2T = self.dscr("mix2T", [16, 128, NT], BF16)

        with contextlib.ExitStack() as gst:
            self.gst = gst
            self.P = Prog(nc, gst)
            P = self.P
            self.modT = self.T([128, 2, 48, 4], F32, True)
            self.Am = self.T([128, 2, KC, 4], F32, True)
            self.Af = self.T([128, 2, KC, 4], F32, True)
            self.ident = self.T([128, 128], BF16, True)
            self.ones = self.T([128, 128], BF16, True)
            self.bones = self.T([128, 128], BF16, True)
            self.masks = self.T([64, 4, 64], F32, True)
            self.onesf = self.T([64, 128], F32, True)
            self.masks2 = self.T([128, 4, 64], F32, True)
            self.masks2b = self.T([128, 4, 64], BF16, True)
            self.resetm = self.T([128, 512], F32, True)
            self.eps_t = self.T([128, 1], F32, True)
            self.lbv = self.T([128, 2, 4], F32, True)
            self.omlv = self.T([128, 2, 4], F32, True)
            self.EB = self.T([64, 8 * 15 * 64], BF16, True)
            self.nag = self.T([128, 2], F32, True)
            self.hgg = self.T([128, 1], F32, True)
            self.ffcw = self.T([128, 2, NFF, 4], F32, True)
            self.sscw = self.T([128, 24, 4], F32, True)
            self.dtb = self.T([128, 64], F32, True)
            self.aneg = self.T([128, 64], F32, True)
            self.dsk = self.T([128, 16], F32, True)
            self.sng = self.T([128, 16], F32, True)
            self.one_t = self.T([128, 1], F32, True)
            self.identf = self.T([64, 64], F32, True)

            with self.phase("setup"):
                self.b_cst = P.buf("cst")
                bc = self.b_cst
                for dst, src in ((self.masks, cst_masks), (self.resetm, cst_reset), (self.nag, na_g), (self.hgg, hg_gain),
                                 (self.ffcw, ffn_cw), (self.sscw, ssd_cw), (self.dtb, ssd_dtb), (self.dsk, ssd_dsk),
                                 (self.sng, ssd_ng), (self.identf, cst_ident[0:64, 0:64]), (self.masks2, cst_masks2)):
                    P.dma("sync", dst, src, writes=[P.buf()])
                for dst, src in ((self.ident, cst_ident), (self.bones, cst_bones), (self.masks2b, cst_masks2)):
                    P.dma("gpsimd", dst, src, writes=[P.buf()])
                P.op("vector", lambda e: e.memset(self.ones, 1.0), writes=[P.buf()])
                P.op("vector", lambda e: e.memset(self.eps_t, EPS), writes=[P.buf()])
                P.op("vector", lambda e: e.memset(self.one_t, 1.0), writes=[P.buf()])
                P.op("vector", lambda e: e.memset(self.onesf, 1.0), writes=[P.buf()])
                b_al = P.buf()
                al = self.T([128, 64], F32)
                P.dma("sync", al, ssd_alog, writes=[b_al])
                self.act(al, al, AF.Exp, [b_al], [b_al])
                self.ts("vector", self.aneg, al, -1.0, None, ALU.mult, None, [b_al], [P.buf()])
                lbr = self.T([128, 2, 3, 4], F32)
                b_lb = P.buf()
                P.dma("sync", lbr, hg_lb, writes=[b_lb])
                self.act(lbr, lbr, AF.Exp, [b_lb], [b_lb])
                ssum = self.T([128, 2, 4], F32)
                b_ss = P.buf()
                self.tt("vector", ssum, lbr[:, :, 0, :], lbr[:, :, 1, :], ALU.add, [b_lb], [b_ss])
                self.tt("vector", ssum, ssum, lbr[:, :, 2, :], ALU.add, [b_lb, b_ss], [b_ss])
                P.op("vector", lambda e: e.reciprocal(ssum, ssum), [b_ss], [b_ss])
                b_lbv = P.buf()
                self.tt("vector", self.lbv, lbr[:, :, 0, :], ssum, ALU.mult, [b_lb, b_ss], [b_lbv])
                self.ts("vector", self.omlv, self.lbv, -1.0, 1.0, ALU.mult, ALU.add, [b_lbv], [P.buf()])
                bt = self.T([64, 120, 64], F32)
                mk = self.T([64, 64], F32)
                b_bt, b_mk = P.buf(), P.buf()
                P.dma("sync", bt, na_bias.rearrange("p (j c) -> p j c", c=64), writes=[b_bt])
                P.dma("sync", mk, na_mask, writes=[b_mk])
                self.tt("vector", bt, bt, mk.unsqueeze(1).broadcast_to([64, 120, 64]), ALU.add, [b_bt, b_mk], [b_bt])
                self.act(self.EB.rearrange("p (j c) -> p j c", c=64), bt, AF.Exp, [b_bt], [P.buf()])
                sc = self.T([128, KC, 4], F32)
                b_sc = P.buf()
                P.dma("sync", sc, cT, writes=[b_sc])
                self.act(sc, sc, AF.Silu, [b_sc], [b_sc])
                bm = self.T([128, 2, 48], F32)
                nm = self.T([128, 2, KC], F32)
                nf = self.T([128, 2, KC], F32)
                b_bm, b_nm, b_nf = P.buf(), P.buf(), P.buf()
                P.dma("sync", bm, bmod, writes=[b_bm])
                P.dma("sync", nm, nmix, writes=[b_nm])
                P.dma("sync", nf, nffn, writes=[b_nf])
                wring = Ring(self, [128, KC, 768], F32, 2)
                b_mod = P.buf()
                for l in range(2):
                    ps = self.PS()
                    b_ps = P.buf()
                    for cb in range(8):
                        wt, b_wt = wring.next()
                        P.dma("sync", wt, wmod[l, :, :, cb * 768:(cb + 1) * 768], writes=[b_wt])
                        for jj in range(6):
                            ch = cb * 6 + jj
                            for kc in range(KC):
                                self.mm(ps[:, ch * 4:(ch + 1) * 4], wt[:, kc, jj * 128:(jj + 1) * 128], sc[:, kc, :],
                                        kc == 0, kc == KC - 1, [b_wt, b_sc], [b_ps])
                    self.tt("vector", self.modT[:, l], ps[:, 0:192].rearrange("p (c j) -> p c j", j=4),
                            bm[:, l, :].unsqueeze(2).broadcast_to([128, 48, 4]), ALU.add, [b_ps, b_bm], [b_mod])
                    tmpa = self.T([128, KC, 4], F32)
                    b_ta = P.buf()
                    for (dst, lo, nv, b_nv) in ((self.Am, 8, nm, b_nm), (self.Af, 32, nf, b_nf)):
                        self.ts("vector", tmpa, self.modT[:, l, lo:lo + KC, :], 1.0, None, ALU.add, None, [b_mod], [b_ta])
                        self.tt("vector", dst[:, l], tmpa, nv[:, l, :].unsqueeze(2).broadcast_to([128, KC, 4]), ALU.mult,
                                [b_ta, b_nv], [P.buf()])
            if self.stop_after == "setup":
                return

            self.phase_l0_proj(hin, hy_in, qT, kT, vtok, hqT, kfT, gfT, vitok, gateT)
            if self.stop_after in ("l0proj", "l0mod"):
                return
            self.phase_na(qT, kT, vtok, mixT)
            if self.stop_after == "na":
                return
            self.phase_gla(hqT, kfT, gfT, vitok, gateT, mixT)
            if self.stop_after == "gla":
                return
            self.phase_outproj(0, hy_out, KC, mixT, hin, hT, True, None)
            if self.stop_after == "l0out":
                return
            self.phase_ffn(0, ffn_up, ffn_dn, guT, hT, hT, True, None)
            if self.stop_after == "l0ffn":
                return
            self.phase_l1_proj(hT, ssd_in, zsT, xsT, xstok, bcT, btok, dltok)
            if self.stop_after == "l1proj":
                return
            self.phase_ssd2(xstok, btok, bcT, dltok, yfb)
            if self.stop_after == "ssdscan":
                return
            self.phase_ssd_fin(zsT, xsT, yfb, mix2T)
            if self.stop_after == "ssd":
                return
            self.phase_outproj(1, ssd_out, 16, mix2T, hT, hT, False, None)
            if self.stop_after == "l1out":
                return
            self.phase_ffn(1, ffn_up, ffn_dn, guT, hT, None, False, outT)

    def modulate_all(self, ust, hsrc, l, A, shlo, blocks):
        U = self.T([128, KC, NT], BF16, stack=ust)
        with self.phase("modulate"):
            self._modulate(U, hsrc, l, A, shlo, blocks)
        return U

    def _modulate(self, U, hsrc, l, A, shlo, blocks):
        P = self.P
        hring = Ring(self, [128, KC, 512], F32, 3)
        sqring = Ring(self, [128, KC, 512], BF16, 2)
        rring = Ring(self, [128, 512], F32, 3)
        tring = Ring(self, [128, 512], F32, 3)
        psr = Ring(self, [128, 512], F32, 2, psum=True)
        hv = hsrc.rearrange("c p t -> p c t")

        def front_a(blk):
            (s, kind, off, n, mc) = blk
            hb, b_hb = hring.next()
            P.dma("sync", hb[:, :, 0:n], hv[:, :, off:off + n], writes=[b_hb])
            sq, b_sq = sqring.next()
            self.act(sq[:, :, 0:n], hb[:, :, 0:n], AF.Square, [b_hb], [b_sq])
            return dict(hb=hb, b_hb=b_hb, sq=sq, b_sq=b_sq)

        def front_b(blk, c_):
            (s, kind, off, n, mc) = blk
            sq, b_sq = c_["sq"], c_["b_sq"]
            ps, b_ps = psr.next()
            for c in range(KC):
                self.mm(ps[:, 0:n], self.ones, sq[:, c, 0:n], c == 0, c == KC - 1, [b_sq], [b_ps])
            rs, b_rs = rring.next()
            self.rsqrt_from(rs[:, 0:n], ps[:, 0:n], 1.0 / D, [b_ps], b_rs)
            c_["rs"], c_["b_rs"] = rs, b_rs

        def back(blk, c_):
            (s, kind, off, n, mc) = blk
            hb, b_hb, rs, b_rs = c_["hb"], c_["b_hb"], c_["rs"], c_["b_rs"]
            b_u = P.buf()
            for c in range(KC):
                tm, b_tm = tring.next()
                self.stt("vector", tm[:, 0:n], hb[:, c, 0:n], A[:, l, c, mc:mc + 1], rs[:, 0:n], ALU.mult, ALU.mult,
                         [b_hb, b_rs], [b_tm])
                self.act(U[:, c, off:off + n], tm[:, 0:n], AF.Identity, [b_tm], [b_u],
                         bias=self.modT[:, l, shlo + c, mc:mc + 1], scale=1.0)

        nb = len(blocks)
        ctxs = {}
        for i in range(min(2, nb)):
            ctxs[i] = front_a(blocks[i])
        front_b(blocks[0], ctxs[0])
        for i in range(nb):
            back(blocks[i], ctxs[i])
            if i + 1 < nb:
                front_b(blocks[i + 1], ctxs[i + 1])
            if i + 2 < nb:
                ctxs[i + 2] = front_a(blocks[i + 2])
            ctxs.pop(i)

    def load_w_chunk(self, dst, b_dst, src_ap):
        self.P.dma("gpsimd", dst, src_ap, writes=[b_dst])

    def phase_l0_proj(self, hsrc, w_in, qT, kT, vtok, hqT, kfT, gfT, vitok, gateT):
        P = self.P
        blocks = all_blocks(True)
        with contextlib.ExitStack() as ust:
          U = self.modulate_all(ust, hsrc, 0, self.Am, 0, blocks)
          if self.stop_after == "l0mod":
              return
          with self.phase("l0proj"):
            wring = Ring(self, [128, KC, 128], BF16, 2)
            psr = Ring(self, [128, 512], F32, 2, psum=True)
            ps2 = Ring(self, [128, 512], F32, 1, psum=True)
            oring = Ring(self, [128, 512], BF16, 3)
            o32 = Ring(self, [128, 512], F32, 2)
            f32r = Ring(self, [128, 512], F32, 2)
            sqr = Ring(self, [128, 512], BF16, 2)
            rr = Ring(self, [128, 512], F32, 2)
            fm = list(range(0, 8)) + list(range(12, 24)) + list(range(28, 32))
            import os as _os
            _skip = _os.environ.get("KSKIP", "")
            if "fm" in _skip:
                fm = []
            if "q" in _skip:
                fm = [j for j in fm if j >= 8]
            if "g" in _skip:
                fm = [j for j in fm if not (16 <= j < 24)]
            if "s" in _skip:
                fm = [j for j in fm if not (12 <= j < 16 or j >= 28)]
            for j in fm:
                wt, b_wt = wring.next()
                self.load_w_chunk(wt, b_wt, w_in[:, :, j * 128:(j + 1) * 128])
                for (s, kind, off, n, mc) in blocks:
                    ps, b_ps = psr.next()
                    for kc in range(KC):
                        self.mm(ps[:, 0:n], wt[:, kc, :], U[:, kc, off:off + n], kc == 0, kc == KC - 1, [b_wt], [b_ps])
                    if j < 8:
                        sq, b_sq = sqr.next()
                        self.act(sq[:, 0:n], ps[:, 0:n], AF.Square, [b_ps], [b_sq])
                        raw, b_raw = f32r.next()
                        self.cp("vector", raw[:, 0:n], ps[:, 0:n], [b_ps], [b_raw])
                        p2, b_p2 = ps2.next()
                        self.mm(p2[:, 0:n], self.bones, sq[:, 0:n], True, True, [b_sq], [b_p2])
                        r, b_r = rr.next()
                        self.rsqrt_from(r[:, 0:n], p2[:, 0:n], 1.0 / 64, [b_p2], b_r)
                        ot, b_ot = oring.next()
                        gi = 0 if j < 4 else 1
                        self.stt("vector", ot[:, 0:n], raw[:, 0:n], self.nag[:, gi:gi + 1], r[:, 0:n], ALU.mult, ALU.mult,
                                 [b_raw, b_r], [b_ot])
                        dst = qT if j < 4 else kT
                        P.dma("sync", dst[j % 4, :, off:off + n], ot[:, 0:n], reads=[b_ot])
                    elif 12 <= j < 16 or j >= 28:
                        ot, b_ot = oring.next()
                        self.act(ot[:, 0:n], ps[:, 0:n], AF.Silu, [b_ps], [b_ot])
                        dst = hqT if j < 16 else gateT
                        P.dma("sync", dst[j % 4, :, off:off + n], ot[:, 0:n], reads=[b_ot])
                    else:
                        dr = 0 if j < 20 else 1
                        jj = j % 4
                        f, b_f = f32r.next()
                        self.act(f[:, 0:n], ps[:, 0:n], AF.Sigmoid, [b_ps], [b_f])
                        self.ts("vector", f[:, 0:n], f[:, 0:n], self.omlv[:, dr, jj:jj + 1], self.lbv[:, dr, jj:jj + 1],
                                ALU.mult, ALU.add, [b_f], [b_f])
                        g, b_g = o32.next()
                        self.act(g[:, 0:n], f[:, 0:n], AF.Ln, [b_f], [b_g])
                        P.dma("sync", gfT[dr, jj, :, off:off + n], g[:, 0:n], reads=[b_g])
                        ot, b_ot = oring.next()
                        self.ts("vector", ot[:, 0:n], f[:, 0:n], -1.0, 1.0, ALU.mult, ALU.add, [b_f], [b_ot])
                        P.dma("sync", kfT[dr, jj, :, off:off + n], ot[:, 0:n], reads=[b_ot])
            if "tm" not in _skip:
                self.proj_tokmajor(U, blocks, [(w_in[:, :, 1024:1536], vtok, 0, 512), (w_in[:, :, 3072:3584], vitok, 0, 512)], psr)

    def proj_tokmajor(self, U, blocks, specs, psr):
        P = self.P
        st_ring = Ring(self, [64, 8, 512], BF16, 2)
        for (wsrc, dst, c0, ncols) in specs:
            wt = self.T([128, KC, 512], BF16)
            b_wt = P.buf()
            self.load_w_chunk(wt, b_wt, wsrc)
            for (s, kind, off, n, mc) in blocks:
                stg, b_stg = st_ring.next()
                nr = n // 64
                for r in range(nr):
                    ps, b_ps = psr.next()
                    for kc in range(KC):
                        self.mm(ps[0:64, 0:512], U[:, kc, off + r * 64:off + (r + 1) * 64], wt[:, kc, :], kc == 0, kc == KC - 1,
                                [b_wt], [b_ps])
                    self.cp("scalar" if r % 2 == 0 else "vector", stg[:, r, :], ps[0:64, 0:512], [b_ps], [b_stg])
                r0 = off // 64
                P.dma("sync", dst[r0:r0 + nr, :, c0:c0 + ncols].rearrange("r p f -> p r f"), stg[:, 0:nr, :], reads=[b_stg])

    def phase_na(self, qT, kT, vtok, mixT):
        P = self.P
        with self.phase("na"):
            qring = Ring(self, [128, SEQT], BF16, 2)
            kring = Ring(self, [128, SEQT], BF16, 2)
            vring = Ring(self, [64, 36, 128], BF16, 2)
            sps = Ring(self, [128, 512], F32, 4, psum=True)
            ops_ = Ring(self, [128, 512], F32, 2, psum=True)
            dps = Ring(self, [128, 512], F32, 2, psum=True)
            ering = Ring(self, [64, 512], BF16, 5)
            pring = Ring(self, [64, 512], BF16, 7)
            rdr = Ring(self, [64, 512], F32, 2)
            ostr = Ring(self, [64, 512], BF16, 2)
            EBv = self.EB.rearrange("p (h j c) -> p h j c", h=8, j=15)

            def r0(r):
                return min(max(r - 4, 0), 24)

            for s in range(2):
                for c in range(4):
                    q2, b_q = qring.next()
                    k2, b_k = kring.next()
                    v2, b_v = vring.next()
                    P.dma("sync", q2, qT[c, :, s * SEQT:(s + 1) * SEQT], writes=[b_q])
                    P.dma("sync", k2, kT[c, :, s * SEQT:(s + 1) * SEQT], writes=[b_k])
                    P.dma("sync", v2, vtok[s * 36:(s + 1) * 36, :, c * 128:(c + 1) * 128].rearrange("r p f -> p r f"), writes=[b_v])
                    items = []
                    for hh in range(2):
                        h = 2 * c + hh
                        pb = 64 * hh
                        groups = [("c", 0, LCTX, None)] + [("l", LCTX + 512 * g, 512, g) for g in range(4)]
                        for (gk, qoff, nq_g, g) in groups:
                            chunks = [("c", m, 0, nq_g, 0) for m in range(4)]
                            if gk == "l":
                                rows = list(range(8 * g, 8 * g + 8))
                                for rp in range(32):
                                    rin = [r for r in rows if r0(r) <= rp < r0(r) + 8]
                                    if rin:
                                        qlo, qhi = rin[0], rin[-1] + 1
                                        chunks.append(("l", rp, (qlo - 8 * g) * 64, (qhi - qlo) * 64, 7 - rp + qlo))
                            G = dict(hh=hh, h=h, pb=pb, qoff=qoff, nq_g=nq_g)
                            for ci_, ch_ in enumerate(chunks):
                                items.append(dict(G=G, chunk=ch_, first=(ci_ == 0), last=(ci_ == len(chunks) - 1)))

                    def front(it):
                        G = it["G"]
                        (ck, idx, c0, nq, jlo) = it["chunk"]
                        pb, h, qoff = G["pb"], G["h"], G["qoff"]
                        if it["first"]:
                            G["o"] = ops_.next()
                            G["d"] = dps.next()
                        koff = idx * 64 if ck == "c" else LCTX + idx * 64
                        s_ps, b_s = sps.next()
                        self.mm(s_ps[0:64, 0:nq], k2[pb:pb + 64, koff:koff + 64], q2[pb:pb + 64, qoff + c0:qoff + c0 + nq],
                                True, True, [b_k, b_q], [b_s])
                        pT, b_p = pring.next()
                        if ck == "c":
                            self.act(pT[:, 0:nq], s_ps[0:64, 0:nq], AF.Exp, [b_s], [b_p], scale=0.125)
                        else:
                            e_t, b_e = ering.next()
                            self.act(e_t[:, 0:nq], s_ps[0:64, 0:nq], AF.Exp, [b_s], [b_e], scale=0.125)
                            nj = nq // 64
                            self.tt("vector", pT[:, 0:nq], e_t[:, 0:nq],
                                    EBv[:, h, jlo:jlo + nj, :].rearrange("p j c -> p (j c)"), ALU.mult,
                                    [b_e, self.b_cst], [b_p])
                        it["pT"], it["b_p"] = pT, b_p

                    def back(it):
                        G = it["G"]
                        (ck, idx, c0, nq, jlo) = it["chunk"]
                        hh, pb, qoff, nq_g = G["hh"], G["pb"], G["qoff"], G["nq_g"]
                        o_ps, b_o = G["o"]
                        d_ps, b_d = G["d"]
                        pT, b_p = it["pT"], it["b_p"]
                        vrow = idx if ck == "c" else 4 + idx
                        self.mm(o_ps[0:64, c0:c0 + nq], v2[:, vrow, hh * 64:(hh + 1) * 64], pT[:, 0:nq], it["first"], it["last"],
                                [b_v, b_p], [b_o])
                        self.mm(d_ps[0:64, c0:c0 + nq], self.ones[0:64, 0:64], pT[:, 0:nq], it["first"], it["last"], [b_p], [b_d])
                        if it["last"]:
                            rd, b_rd = rdr.next()
                            P.op("vector", (lambda rd=rd, d_ps=d_ps, n=nq_g: lambda e: e.reciprocal(rd[:, 0:n], d_ps[0:64, 0:n]))(),
                                 [b_d], [b_rd])
                            ot, b_ot = ostr.next()
                            self.tt("vector", ot[:, 0:nq_g], o_ps[0:64, 0:nq_g], rd[:, 0:nq_g], ALU.mult, [b_o, b_rd], [b_ot])
                            P.dma("gpsimd", mixT[c, pb:pb + 64, s * SEQT + qoff:s * SEQT + qoff + nq_g], ot[:, 0:nq_g], reads=[b_ot])

                    LAG = 3
                    for t_ in range(len(items) + LAG):
                        if t_ < len(items):
                            front(items[t_])
                        if t_ >= LAG:
                            back(items[t_ - LAG])

    def scan_setup(self, nchains, dv):
        P = self.P
        W = {}
        W["G"] = Ring(self, [128, 512], F32, 2)
        W["X"] = Ring(self, [128, 512], F32, 2)
        W["Xc"] = Ring(self, [128, 512], F32, 2)
        W["qe"] = Ring(self, [128, 512], F32, 2)
        W["ke"] = Ring(self, [128, 512], F32, 2)
        W["qt"] = Ring(self, [128, 512], BF16, nchains + 1)
        W["kt"] = Ring(self, [128, 512], BF16, nchains + 1)
        W["q2"] = Ring(self, [128, 512], BF16, nchains + 1)
        W["sc3"] = Ring(self, [128, 3, 8], F32, nchains + 1)
        W["esc"] = Ring(self, [128, 3, 8], F32, nchains + 1)
        W["aT"] = Ring(self, [64, 64], BF16, 4)
        W["kT"] = Ring(self, [64, 128], BF16, 4)
        W["tmp"] = Ring(self, [128, dv], F32, 4)
        W["a_ps"] = Ring(self, [128, 512], F32, 2, psum=True)
        W["t_ps"] = Ring(self, [128, 512], BF16, 2, psum=True)
        W["kv_ps"] = Ring(self, [128, 512], F32, 2, psum=True)
        W["o_ps"] = Ring(self, [128, 512], F32, nchains, psum=True)
        W["LT"] = Ring(self, [64, 8, 64], BF16, nchains + 1)
        W["Dm"] = Ring(self, [64, 8, 64], F32, 2)
        W["xcol"] = Ring(self, [64, 8], F32, 2)
        W["kTall"] = Ring(self, [64, 8, 128], BF16, nchains + 1)
        W["chains"] = []
        for i in range(nchains):
            S = self.T([128, dv], F32)
            Sbf = self.T([128, dv], BF16)
            W["chains"].append(dict(S=S, Sbf=Sbf, bS=P.buf(), bSbf=P.buf()))
        return W

    def chain_reset(self, ch):
        S, Sbf = ch["S"], ch["Sbf"]
        self.P.op("vector", lambda e: e.memset(S, 0.0), writes=[ch["bS"]])
        ch["has_state"] = False

    def scan_prep(self, W, ch, q_ap, k_ap, g_ap, rd, vrows, n, dv, sigma, o_out, pbo=0, scalar=False):
        P = self.P
        nch = n // 64
        G, bG = W["G"].next()
        P.op("vector", lambda e: e.tensor_tensor_scan(G[:, 0:n], self.resetm[:, 0:n], g_ap, 0.0, ALU.mult, ALU.add),
             rd + [self.b_cst], [bG])
        Gv = G[:, 0:n].rearrange("p (c t) -> p c t", t=64)
        if sigma > 0:
            X, bX = G, bG
        else:
            X, bX = W["X"].next()
            self.tt("vector", X[:, 0:n], G[:, 0:n], g_ap, ALU.subtract, [bG] + rd, [bX])
        Xv = X[:, 0:n].rearrange("p (c t) -> p c t", t=64)
        sc3, b_sc3 = W["sc3"].next()
        esc, b_esc = W["esc"].next()
        qt, b_qt = W["qt"].next()
        kt, b_kt = W["kt"].next()
        q2, b_q2 = W["q2"].next()
        LT = b_LT = None
        if scalar:
            totb = sc3[:, 2, 0:nch].unsqueeze(2).broadcast_to([128, nch, 64])
            self.cp("vector", sc3[:, 2, 0:nch], Gv[:, :, 63], [bG], [b_sc3])
            self.act(esc[:, 2, 0:nch], sc3[:, 2, 0:nch], AF.Exp, [b_sc3], [b_esc])
            cD = esc[:, 2, :]
            cK = None
            Xc, bXc = W["Xc"].next()
            self.tt("vector", Xc[:, 0:n].rearrange("p (c t) -> p c t", t=64), totb, Xv, ALU.subtract, [bX, b_sc3], [bXc])
            qe, b_qe = W["qe"].next()
            ke, b_ke = W["ke"].next()
            if sigma > 0:
                self.act(qe[:, 0:n], X[:, 0:n], AF.Exp, [bX], [b_qe])
                self.act(ke[:, 0:n], Xc[:, 0:n], AF.Exp, [bXc], [b_ke])
            else:
                self.act(qe[:, 0:n], Xc[:, 0:n], AF.Exp, [bXc], [b_qe])
                self.act(ke[:, 0:n], X[:, 0:n], AF.Exp, [bX], [b_ke])
            self.tt("vector", q2[:, 0:n], q_ap, qe[:, 0:n], ALU.mult, rd + [b_qe], [b_q2])
            self.tt("vector", kt[:, 0:n], k_ap, ke[:, 0:n], ALU.mult, rd + [b_ke], [b_kt])
            self.cp("vector", qt[:, 0:n], k_ap, rd, [b_qt])
            Dm, b_Dm = W["Dm"].next()
            xcol, b_xc = W["xcol"].next()
            X64 = X[0:64, 0:n].rearrange("p (c t) -> p c t", t=64)
            self.tt("vector", Dm[:, 0:nch, :], X64, self.identf.unsqueeze(1).broadcast_to([64, nch, 64]), ALU.mult,
                    [bX, self.b_cst], [b_Dm])
            P.op("vector", (lambda xcol=xcol, Dm=Dm, nch=nch: lambda e: e.tensor_reduce(
                xcol[:, 0:nch], Dm[:, 0:nch, :], mybir.AxisListType.X, ALU.add))(), [b_Dm], [b_xc])
            self.tt("vector", Dm[:, 0:nch, :], X64, xcol[:, 0:nch].unsqueeze(2).broadcast_to([64, nch, 64]), ALU.subtract,
                    [bX, b_xc, b_Dm], [b_Dm])
            self.ts("vector", Dm[:, 0:nch, :], Dm[:, 0:nch, :], float(sigma), 0.0, ALU.mult, ALU.min, [b_Dm], [b_Dm])
            self.act(Dm[:, 0:nch, :], Dm[:, 0:nch, :], AF.Exp, [b_Dm], [b_Dm])
            LT, b_LT = W["LT"].next()
            mk = self.masks[:, 0, :] if sigma > 0 else self.masks[:, 1, :]
            self.tt("vector", LT[:, 0:nch, :], Dm[:, 0:nch, :], mk.unsqueeze(1).broadcast_to([64, nch, 64]), ALU.mult,
                    [b_Dm, self.b_cst], [b_LT])
        else:
            self.cp("vector", sc3[:, 0, 0:nch], Xv[:, :, 32], [bX], [b_sc3])
            self.cp("vector", sc3[:, 2, 0:nch], Gv[:, :, 63], [bG, b_sc3], [b_sc3])
            self.tt("vector", sc3[:, 1, 0:nch], sc3[:, 2, 0:nch], sc3[:, 0, 0:nch], ALU.subtract, [b_sc3], [b_sc3])
            self.act(esc[:, :, 0:nch], sc3[:, :, 0:nch], AF.Exp, [b_sc3], [b_esc])
            if sigma > 0:
                cS, cK = esc[:, 0, :], esc[:, 1, :]
            else:
                cS, cK = esc[:, 1, :], esc[:, 0, :]
            cD = esc[:, 2, :]
            Xc, bXc = W["Xc"].next()
            self.tt("vector", Xc[:, 0:n].rearrange("p (c t) -> p c t", t=64), Xv,
                    sc3[:, 0, 0:nch].unsqueeze(2).broadcast_to([128, nch, 64]), ALU.subtract, [bX, b_sc3], [bXc])
            qe, b_qe = W["qe"].next()
            ke, b_ke = W["ke"].next()
            self.act(qe[:, 0:n], Xc[:, 0:n], AF.Exp, [bXc], [b_qe], scale=float(sigma))
            self.act(ke[:, 0:n], Xc[:, 0:n], AF.Exp, [bXc], [b_ke], scale=float(-sigma))
            self.tt("vector", qt[:, 0:n], q_ap, qe[:, 0:n], ALU.mult, rd + [b_qe], [b_qt])
            self.tt("vector", kt[:, 0:n], k_ap, ke[:, 0:n], ALU.mult, rd + [b_ke], [b_kt])
            self.tt("vector", q2[:, 0:n].rearrange("p (c t) -> p c t", t=64), qt[:, 0:n].rearrange("p (c t) -> p c t", t=64),
                    cS[:, 0:nch].unsqueeze(2).broadcast_to([128, nch, 64]), ALU.mult, [b_qt, b_esc], [b_q2])
        tps_, b_tps = W["t_ps"].next()
        for i_ in range(nch):
            self.tr(tps_[0:64, i_ * 128:(i_ + 1) * 128], kt[:, i_ * 64:(i_ + 1) * 64], self.ident, [b_kt, self.b_cst], [b_tps])
        kTall, b_kTall = W["kTall"].next()
        self.cp("scalar", kTall[:, 0:nch, :], tps_[0:64, 0:nch * 128].rearrange("p (c f) -> p c f", f=128), [b_tps], [b_kTall])
        o_ps, b_o = W["o_ps"].next()
        order = list(range(nch)) if sigma > 0 else list(range(nch - 1, -1, -1))
        mask = self.masks[:, 0, :] if sigma > 0 else self.masks[:, 1, :]
        return dict(W=W, ch=ch, n=n, dv=dv, pbo=pbo, vrows=vrows, o_out=o_out, qt=qt, b_qt=b_qt, kt=kt, b_kt=b_kt, q2=q2,
                    b_q2=b_q2, esc=esc, b_esc=b_esc, cK=cK, cD=cD, o_ps=o_ps, b_o=b_o, order=order, mask=mask,
                    scalar=scalar, LT=LT, b_LT=b_LT, q_ap=q_ap, rd=rd, kTall=kTall, b_kTall=b_kTall)

    def scan_s1(self, C, k):
        W = C["W"]
        i = C["order"][k]
        cs = slice(i * 64, (i + 1) * 64)
        qt, kt = C["qt"], C["kt"]
        b_qt, b_kt = C["b_qt"], C["b_kt"]
        U = {}
        if C["o_out"] is not None:
            a_ps, b_a = W["a_ps"].next()
            if C["scalar"]:
                self.mm(a_ps[0:64, 0:64], qt[:, cs], C["q_ap"][:, cs], True, True, [b_qt] + C["rd"], [b_a])
            else:
                self.mm(a_ps[0:64, 0:64], kt[:, cs], qt[:, cs], True, True, [b_kt, b_qt], [b_a])
            U["a"] = (a_ps, b_a)
        C.setdefault("units", {})[k] = U

    def scan_s2(self, C, k):
        W = C["W"]
        i = C["order"][k]
        U = C["units"][k]
        if C["o_out"] is not None:
            a_ps, b_a = U["a"]
            aT, b_aT = W["aT"].next()
            if C["scalar"]:
                self.tt("vector", aT, a_ps[0:64, 0:64], C["LT"][:, i, :], ALU.mult, [b_a, C["b_LT"]], [b_aT])
            else:
                self.tt("vector", aT, a_ps[0:64, 0:64], C["mask"], ALU.mult, [b_a, self.b_cst], [b_aT])
            U["aT"] = (aT, b_aT)

    def scan_s3(self, C, k):
        W, ch = C["W"], C["ch"]
        dv, pbo = C["dv"], C["pbo"]
        i = C["order"][k]
        cs = slice(i * 64, (i + 1) * 64)
        U = C["units"].pop(k)
        q2, b_q2, b_esc = C["q2"], C["b_q2"], C["b_esc"]
        S, Sbf, bS, bSbf = ch["S"], ch["Sbf"], ch["bS"], ch["bSbf"]
        o_ps, b_o = C["o_ps"], C["b_o"]
        vr, b_vr = C["vrows"](i)
        if C["o_out"] is not None:
            aT, b_aT = U["aT"]
            self.mm(o_ps[pbo:pbo + dv, cs], vr, aT, True, not ch["has_state"], [b_vr, b_aT], [b_o])
            if ch["has_state"]:
                self.mm(o_ps[pbo:pbo + dv, cs], Sbf, q2[:, cs], False, True, [bSbf, b_q2], [b_o])
        kT, b_kT = C["kTall"][:, i, :], C["b_kTall"]
        kv_ps, b_kv = W["kv_ps"].next()
        self.mm(kv_ps[:, 0:dv], kT, vr, True, True, [b_kT, b_vr], [b_kv])
        if C["scalar"]:
            self.stt("vector", S, S, C["cD"][:, i:i + 1], kv_ps[:, 0:dv], ALU.mult, ALU.add, [bS, b_esc, b_kv], [bS])
        else:
            tmp, b_tmp = W["tmp"].next()
            self.ts("vector", tmp, kv_ps[:, 0:dv], C["cK"][:, i:i + 1], None, ALU.mult, None, [b_kv, b_esc], [b_tmp])
            self.stt("vector", S, S, C["cD"][:, i:i + 1], tmp, ALU.mult, ALU.add, [bS, b_esc, b_tmp], [bS])
        self.cp("scalar", Sbf, S, [bS], [bSbf])
        ch["has_state"] = True

    def scan_finish(self, C):
        if C["o_out"] is not None:
            oa, b_oa = C["o_out"]
            pbo, dv, n = C["pbo"], C["dv"], C["n"]
            self.tt("vector", oa, oa, C["o_ps"][pbo:pbo + dv, 0:n], ALU.add, [C["b_o"], b_oa], [b_oa])

    def scan_step(self, preps):
        nch = preps[0]["n"] // 64
        units = [(C, k) for k in range(nch) for C in preps]
        for idx, (C, k) in enumerate(units):
            if idx == 0:
                self.scan_s1(C, k)
            self.scan_s2(C, k)
            if idx + 1 < len(units):
                self.scan_s1(*units[idx + 1])
            self.scan_s3(C, k)
        for C in preps:
            self.scan_finish(C)

    def phase_gla(self, hqT, kfT, gfT, vitok, gateT, mixT):
        P = self.P
        with self.phase("gla"):
            NCH = 2
            W = self.scan_setup(NCH, 128)
            qb_r = Ring(self, [128, 512], BF16, 2 * NCH)
            kb_r = Ring(self, [128, 512], BF16, 2 * NCH)
            gb_r = Ring(self, [128, 512], F32, 2 * NCH)
            vb_r = Ring(self, [64, 8, 128], BF16, 2 * NCH)
            gtr = Ring(self, [128, 512], BF16, 3)
            oacc_r = Ring(self, [128, SEQT], F32, 4)
            sqr = Ring(self, [128, 512], BF16, 2)
            rr = Ring(self, [128, 512], F32, 2)
            t32 = Ring(self, [128, 512], F32, 2)
            yo = Ring(self, [128, 512], BF16, 2)
            nps = W["a_ps"]
            fwd = [(0, LCTX)] + [(LCTX + 512 * b, 512) for b in range(4)]
            bwd = [(0, LCTX)] + [(LCTX + 512 * b, 512) for b in (3, 2, 1, 0)]
            for s in range(2):
                for hp in range(4):
                    heads = (hp,)
                    oaccs = {}
                    for hd in heads:
                        oacc, b_oa = oacc_r.next()
                        P.op("gpsimd", (lambda oacc=oacc: lambda e: e.memset(oacc, 0.0))(), writes=[b_oa])
                        oaccs[hd] = (oacc, b_oa)
                    specs = []
                    for hd in heads:
                        specs.append((hd, 0, +1))
                        specs.append((hd, 1, -1))
                    for ci in range(NCH):
                        self.chain_reset(W["chains"][ci])
                    for step in range(5):
                        preps = []
                        for ci, (hd, dr, sg) in enumerate(specs):
                            ch = W["chains"][ci]
                            to, n = (fwd if sg > 0 else bwd)[step]
                            off = s * SEQT + to
                            nr = n // 64
                            q, b_q = qb_r.next()
                            k, b_k = kb_r.next()
                            g, b_g = gb_r.next()
                            v, b_v = vb_r.next()
                            P.dma("sync", q[:, 0:n], hqT[hd, :, off:off + n], writes=[b_q])
                            P.dma("sync", k[:, 0:n], kfT[dr, hd, :, off:off + n], writes=[b_k])
                            P.dma("sync", g[:, 0:n], gfT[dr, hd, :, off:off + n], writes=[b_g])
                            P.dma("sync", v[:, 0:nr, :], vitok[off // 64:off // 64 + nr, :, hd * 128:(hd + 1) * 128].rearrange("r p f -> p r f"),
                                  writes=[b_v])
                            oacc, b_oa = oaccs[hd]
                            preps.append(self.scan_prep(W, ch, q[:, 0:n], k[:, 0:n], g[:, 0:n], [b_q, b_k, b_g],
                                                        (lambda i, v=v, b_v=b_v: (v[:, i, :], b_v)), n, 128, sg,
                                                        (oacc[:, to:to + n], b_oa)))
                        self.scan_step(preps)
                    for hd in heads:
                        oacc, b_oa = oaccs[hd]
                        for (to, n) in fwd:
                            gt, b_gt = gtr.next()
                            P.dma("sync", gt[:, 0:n], gateT[hd, :, s * SEQT + to:s * SEQT + to + n], writes=[b_gt])
                            sq, b_sq = sqr.next()
                            self.act(sq[:, 0:n], oacc[:, to:to + n], AF.Square, [b_oa], [b_sq])
                            ps, b_ps = nps.next()
                            self.mm(ps[:, 0:n], self.ones, sq[:, 0:n], True, True, [b_sq], [b_ps])
                            r, b_r = rr.next()
                            self.rsqrt_from(r[:, 0:n], ps[:, 0:n], 1.0 / 128, [b_ps], b_r)
                            t, b_t = t32.next()
                            self.stt("vector", t[:, 0:n], oacc[:, to:to + n], self.hgg[:, 0:1], r[:, 0:n], ALU.mult, ALU.mult,
                                     [b_oa, b_r, self.b_cst], [b_t])
                            y, b_y = yo.next()
                            self.tt("vector", y[:, 0:n], t[:, 0:n], gt[:, 0:n], ALU.mult, [b_t, b_gt], [b_y])
                            P.dma("gpsimd", mixT[4 + hd, :, s * SEQT + to:s * SEQT + to + n], y[:, 0:n], reads=[b_y])

    def phase_outproj(self, l, w_out, nk, mixsrc, hsrc, hdst, with_ctx, outT):
        P = self.P
        with self.phase("outproj%d" % l):
            wt = self.T([128, nk, D], BF16)
            b_wt = P.buf()
            for kc in range(nk):
                P.dma("gpsimd", wt[:, kc, :], w_out[:, kc, :], writes=[b_wt])
            self.linear_residual(l, wt, b_wt, nk, mixsrc, hsrc, hdst, 16, with_ctx, outT)

    def linear_residual(self, l, wt, b_wt, nk, xsrc, hsrc, hdst, glo, with_ctx, outT):
        P = self.P
        xr = Ring(self, [128, nk, 512], BF16, 2)
        hr = Ring(self, [128, KC, 512], F32, 2)
        orr = Ring(self, [128, KC, 512], F32, 2)
        psr = Ring(self, [128, 512], F32, 2, psum=True)
        xv = xsrc.rearrange("c p t -> p c t")
        hv = hsrc.rearrange("c p t -> p c t")
        for (s, kind, off, n, mc) in all_blocks(with_ctx):
            xb, b_x = xr.next()
            P.dma("sync", xb[:, :, 0:n], xv[:, :, off:off + n], writes=[b_x])
            hb, b_h = hr.next()
            P.dma("sync", hb[:, :, 0:n], hv[:, :, off:off + n], writes=[b_h])
            ob, b_ob = orr.next()
            for i in range(KC):
                ps, b_ps = psr.next()
                for kc in range(nk):
                    self.mm(ps[:, 0:n], wt[:, kc, i * 128:(i + 1) * 128], xb[:, kc, 0:n], kc == 0, kc == nk - 1, [b_wt, b_x], [b_ps])
                self.stt("vector", ob[:, i, 0:n], ps[:, 0:n], self.modT[:, l, glo + i, mc:mc + 1], hb[:, i, 0:n], ALU.mult, ALU.add,
                         [b_ps, b_h], [b_ob])
            if outT is None:
                P.dma("gpsimd", hdst.rearrange("c p t -> p c t")[:, :, off:off + n], ob[:, :, 0:n], reads=[b_ob])
            else:
                oo = s * LLAT + (off - s * SEQT - LCTX)
                P.dma("gpsimd", outT.rearrange("c p t -> p c t")[:, :, oo:oo + n], ob[:, :, 0:n], reads=[b_ob])

    def conv3(self, strip, b_strip, ln, wv, out, b_out):
        self.ts("vector", out, strip[:, 1:ln + 1], wv[:, 1:2], None, ALU.mult, None, [b_strip, self.b_cst], [b_out])
        self.stt("vector", out, strip[:, 0:ln], wv[:, 0:1], out, ALU.mult, ALU.add, [b_strip, b_out, self.b_cst], [b_out])
        self.stt("vector", out, strip[:, 2:ln + 2], wv[:, 2:3], out, ALU.mult, ALU.add, [b_strip, b_out, self.b_cst], [b_out])

    def phase_ffn(self, l, ffn_up, ffn_dn, guT, hsrc, hdst, with_ctx, outT):
        P = self.P
        blocks = all_blocks(with_ctx)
        with contextlib.ExitStack() as ust:
          U = self.modulate_all(ust, hsrc, l, self.Af, 24, blocks)
          with self.phase("ffn_up%d" % l):
            war = Ring(self, [128, KC, 128], BF16, 2)
            wvr = Ring(self, [128, KC, 128], BF16, 2)
            psr = Ring(self, [128, 512], F32, 4, psum=True)
            a_l = Ring(self, [128, LLAT + 2], F32, 2)
            a_c = Ring(self, [128, LCTX + 2], F32, 2)
            v_r = Ring(self, [128, LLAT], F32, 2)
            c_r = Ring(self, [128, LLAT], F32, 2)
            g_r = Ring(self, [128, LLAT], F32, 2)
            gu_r = Ring(self, [128, LLAT], BF16, 2)
            for rg in (a_l, a_c):
                for (ap, b) in rg.items:
                    P.op("vector", (lambda ap=ap: lambda e: e.memset(ap, 0.0))(), writes=[b])
            for j in range(NFF):
                wa, b_wa = war.next()
                wv, b_wv = wvr.next()
                self.load_w_chunk(wa, b_wa, ffn_up[l, :, :, j * 128:(j + 1) * 128])
                self.load_w_chunk(wv, b_wv, ffn_up[l, :, :, DFF + j * 128:DFF + (j + 1) * 128])
                for (s, kind, soff, ln, mc) in all_segments(with_ctx):
                    strip, b_st = (a_l if kind == "l" else a_c).next()
                    vv, b_vv = v_r.next()
                    for bo in range(0, ln, 512):
                        n = min(512, ln - bo)
                        off = soff + bo
                        ps, b_ps = psr.next()
                        for kc in range(KC):
                            self.mm(ps[:, 0:n], wa[:, kc, :], U[:, kc, off:off + n], kc == 0, kc == KC - 1, [b_wa], [b_ps])
                        self.cp("scalar", strip[:, 1 + bo:1 + bo + n], ps[:, 0:n], [b_ps], [b_st])
                        ps2, b_ps2 = psr.next()
                        for kc in range(KC):
                            self.mm(ps2[:, 0:n], wv[:, kc, :], U[:, kc, off:off + n], kc == 0, kc == KC - 1, [b_wv], [b_ps2])
                        self.cp("vector", vv[:, bo:bo + n], ps2[:, 0:n], [b_ps2], [b_vv])
                    cv, b_cv = c_r.next()
                    self.conv3(strip, b_st, ln, self.ffcw[:, l, j, :], cv[:, 0:ln], b_cv)
                    gl, b_gl = g_r.next()
                    self.act(gl[:, 0:ln], cv[:, 0:ln], AF.Gelu_apprx_tanh, [b_cv, self.b_cst], [b_gl], bias=self.ffcw[:, l, j, 3:4], scale=1.0)
                    gu, b_gu = gu_r.next()
                    self.tt("vector", gu[:, 0:ln], gl[:, 0:ln], vv[:, 0:ln], ALU.mult, [b_gl, b_vv], [b_gu])
                    P.dma("sync", guT[j, :, soff:soff + ln], gu[:, 0:ln], reads=[b_gu])
        with self.phase("ffn_dn%d" % l):
            wt = self.T([128, NFF, D], BF16)
            b_wt = P.buf()
            for kc in range(NFF):
                P.dma("gpsimd", wt[:, kc, :], ffn_dn[l, :, kc, :], writes=[b_wt])
            self.linear_residual(l, wt, b_wt, NFF, guT, hsrc, hdst, 40, with_ctx, outT)

    def phase_l1_proj(self, hsrc, w_in, zsT, xsT, xstok, bcT, btok, dltok):
        P = self.P
        blocks = all_blocks(True)
        with contextlib.ExitStack() as ust:
          U = self.modulate_all(ust, hsrc, 1, self.Am, 0, blocks)
          with self.phase("l1proj"):
            wring = Ring(self, [128, KC, 128], BF16, 2)
            psr = Ring(self, [128, 512], F32, 3, psum=True)
            tpr = Ring(self, [128, 512], BF16, 2, psum=True)
            oring = Ring(self, [128, 512], BF16, 3)
            o32 = Ring(self, [128, 512], F32, 2)
            a_l = Ring(self, [128, LLAT + 2], F32, 2)
            a_c = Ring(self, [128, LCTX + 2], F32, 2)
            c_r = Ring(self, [128, LLAT], F32, 2)
            s_r = Ring(self, [128, LLAT], BF16, 2)
            stg_r = Ring(self, [64, 32, 128], BF16, 2)
            for rg in (a_l, a_c):
                for (ap, b) in rg.items:
                    P.op("vector", (lambda ap=ap: lambda e: e.memset(ap, 0.0))(), writes=[b])
            for j in range(16):
                wt, b_wt = wring.next()
                self.load_w_chunk(wt, b_wt, w_in[:, :, j * 128:(j + 1) * 128])
                for (s, kind, off, n, mc) in all_blocks(False):
                    ps, b_ps = psr.next()
                    for kc in range(KC):
                        self.mm(ps[:, 0:n], wt[:, kc, :], U[:, kc, off:off + n], kc == 0, kc == KC - 1, [b_wt], [b_ps])
                    ot, b_ot = oring.next()
                    self.act(ot[:, 0:n], ps[:, 0:n], AF.Silu, [b_ps], [b_ot])
                    P.dma("sync", zsT[j, :, off:off + n], ot[:, 0:n], reads=[b_ot])
            wd = self.T([128, KC, 64], BF16)
            b_wd = P.buf()
            self.load_w_chunk(wd, b_wd, w_in[:, :, 5120:5184])
            dstg_r = Ring(self, [64, 8, 4, 64], F32, 2)
            dhb_r = Ring(self, [64, 64], BF16, 2)
            dtmp_r = Ring(self, [64, 64], F32, 2)
            for (s, kind, off, n, mc) in blocks:
                dstg, b_dstg = dstg_r.next()
                nr = n // 64
                for r in range(nr):
                    ps, b_ps = psr.next()
                    for kc in range(KC):
                        self.mm(ps[0:64, 0:64], U[:, kc, off + r * 64:off + (r + 1) * 64], wd[:, kc, :], kc == 0, kc == KC - 1,
                                [b_wd], [b_ps])
                    dt_, b_dt_ = dtmp_r.next()
                    self.tt("vector", dt_, ps[0:64, 0:64], self.dtb[0:64, :], ALU.add, [b_ps, self.b_cst], [b_dt_])
                    self.act(dt_, dt_, AF.Exp, [b_dt_], [b_dt_])
                    self.act(dstg[:, r, 0, :], dt_, AF.Ln, [b_dt_, self.b_cst], [b_dstg], bias=self.one_t[0:64, 0:1], scale=1.0)
                    self.tt("gpsimd", dstg[:, r, 1, :], dstg[:, r, 0, :], self.aneg[0:64, :], ALU.mult, [b_dstg, self.b_cst], [b_dstg])
                    hb, b_hb = dhb_r.next()
                    self.cp("gpsimd", hb, dstg[:, r, 1, :], [b_dstg], [b_hb])
                    self.cp("gpsimd", dstg[:, r, 2, :], hb, [b_hb], [b_dstg])
                    self.tt("gpsimd", dstg[:, r, 3, :], dstg[:, r, 1, :], dstg[:, r, 2, :], ALU.subtract, [b_dstg], [b_dstg])
                r0 = off // 64
                P.dma("sync", dltok[r0:r0 + nr].rearrange("r p a f -> p r a f"), dstg[:, 0:nr], reads=[b_dstg])
            pending = []
            for j in range(24):
                wt, b_wt = wring.next()
                self.load_w_chunk(wt, b_wt, w_in[:, :, 2048 + j * 128:2048 + (j + 1) * 128])
                for (s, kind, soff, ln, mc) in all_segments(True):
                    strip, b_st = (a_l if kind == "l" else a_c).next()
                    for bo in range(0, ln, 512):
                        n = min(512, ln - bo)
                        off = soff + bo
                        ps, b_ps = psr.next()
                        for kc in range(KC):
                            self.mm(ps[:, 0:n], wt[:, kc, :], U[:, kc, off:off + n], kc == 0, kc == KC - 1, [b_wt], [b_ps])
                        self.cp("scalar", strip[:, 1 + bo:1 + bo + n], ps[:, 0:n], [b_ps], [b_st])
                    while pending:
                        pending.pop(0)()
                    cv, b_cv = c_r.next()
                    self.conv3(strip, b_st, ln, self.sscw[:, j, :], cv[:, 0:ln], b_cv)
                    so, b_so = s_r.next()
                    self.act(so[:, 0:ln], cv[:, 0:ln], AF.Silu, [b_cv, self.b_cst], [b_so], bias=self.sscw[:, j, 3:4], scale=1.0)
                    if j < 16:
                        P.dma("sync", xsT[j, :, soff:soff + ln], so[:, 0:ln], reads=[b_so])
                    else:
                        P.dma("sync", bcT[j - 16, :, soff:soff + ln], so[:, 0:ln], reads=[b_so])
                    if j < 20:
                        def tok_copy(so=so, b_so=b_so, ln=ln, soff=soff, j=j):
                            stg, b_stg = stg_r.next()
                            nr = ln // 64
                            for rb in range(0, nr, 8):
                                nb_ = min(8, nr - rb)
                                tp, b_tp = tpr.next()
                                for r in range(rb, rb + nb_):
                                    self.tr(tp[0:64, (r - rb) * 128:(r - rb + 1) * 128], so[:, r * 64:(r + 1) * 64], self.ident,
                                            [b_so, self.b_cst], [b_tp])
                                self.cp("scalar" if (rb // 8) % 2 else "vector", stg[:, rb:rb + nb_, :],
                                        tp[0:64, 0:nb_ * 128].rearrange("p (r f) -> p r f", f=128), [b_tp], [b_stg])
                            r0 = soff // 64
                            tdst = xstok[r0:r0 + nr, :, j * 128:(j + 1) * 128] if j < 16 else btok[r0:r0 + nr, :, (j - 16) * 128:(j - 15) * 128]
                            P.dma("gpsimd", tdst.rearrange("r p f -> p r f"), stg[:, 0:nr, :], reads=[b_stg])
                        pending.append(tok_copy)
            while pending:
                pending.pop(0)()

    def phase_ssd2(self, xstok, btok, bcT, dltok, yfb):
        P = self.P
        with self.phase("ssd2"):
            LE, GE, GT, LT = [self.masks[:, i, :] for i in range(4)]
            bc_r = Ring(self, [128, 8, SEQT], BF16, 1)
            x_r = Ring(self, [64, 2048], BF16, 3)
            bt_r = Ring(self, [64, 512], BF16, 4)
            dl_r = Ring(self, [64, 4, 64], F32, 3)
            lahl_r = Ring(self, [128, 32], F32, 3)
            ula_r = Ring(self, [128, 2048], BF16, 2)
            mtall_r = Ring(self, [64, 2048], BF16, 3)
            e12_r = Ring(self, [64, 64], F32, 4)
            etot_r = Ring(self, [128, 32], F32, 4)
            w_r = Ring(self, [64, 32], F32, 3)
            cbm_r = Ring(self, [64, 256], F32, 4)
            ed_r = Ring(self, [64, 512], F32, 2)
            mt_r = Ring(self, [64, 512], BF16, 3)
            xdt_r = Ring(self, [64, 2048], BF16, 3)
            xw_r = Ring(self, [64, 2048], BF16, 4)
            tmp_r = Ring(self, [64, 512], F32, 2)
            yrow_r = Ring(self, [64, 2048], BF16, 2)
            sm_ps = Ring(self, [128, 512], F32, 1, psum=True)
            D_ps = Ring(self, [128, 512], F32, 2, psum=True)
            y_ps = Ring(self, [128, 512], F32, 2, psum=True)
            in_ps = Ring(self, [128, 512], F32, 1, psum=True)
            kv_ps = Ring(self, [128, 512], F32, 2, psum=True)
            ST = [self.T([128, 2048], F32) for _ in range(2)]
            STbf = [self.T([128, 2048], BF16) for _ in range(2)]
            for s in range(2):
                sl = slice(s * SEQT, (s + 1) * SEQT)
                bc, b_bc = bc_r.next()
                P.dma("sync", bc, bcT[:, :, sl].rearrange("c p t -> p c t"), writes=[b_bc])
                bST = [[P.buf() for _ in range(4)] for _ in range(2)]
                bSTbf = [[P.buf() for _ in range(4)] for _ in range(2)]
                has_state = [False, False]
                order = [list(range(36)), [3, 2, 1, 0] + list(range(35, 3, -1))]
                seq_items = []
                for k in range(36):
                    for d in range(2):
                        seq_items.append((d, order[d][k]))

                def stageA(d, row):
                    A1 = LE if d == 0 else GE
                    A2 = GT if d == 0 else LT
                    need_out = row >= 4
                    grow = s * 36 + row
                    tok0 = row * 64
                    x, b_x = x_r.next()
                    bt, b_bt = bt_r.next()
                    dl, b_dl = dl_r.next()
                    P.dma("sync", x, xstok[grow], writes=[b_x])
                    P.dma("sync", bt, btok[grow], writes=[b_bt])
                    P.dma("sync", dl, dltok[grow], writes=[b_dl])
                    dt_d = dl[:, 0, d * 32:(d + 1) * 32]
                    la_d = dl[:, 1, d * 32:(d + 1) * 32]
                    sp, b_sp = sm_ps.next()
                    self.mm(sp[0:64, 0:32], A1, la_d, True, True, [b_dl, self.b_cst], [b_sp])
                    self.mm(sp[0:64, 32:64], A2, la_d, True, True, [b_dl, self.b_cst], [b_sp])
                    self.mm(sp[0:128, 64:96], self.onesf, la_d, True, True, [b_dl, self.b_cst], [b_sp])
                    e12, b_e12 = e12_r.next()
                    etot, b_et = etot_r.next()
                    self.act(e12, sp[0:64, 0:64], AF.Exp, [b_sp], [b_e12])
                    self.act(etot, sp[0:128, 64:96], AF.Exp, [b_sp], [b_et])
                    w, b_w = w_r.next()
                    self.tt("vector", w, e12[:, 32:64], dt_d, ALU.mult, [b_e12, b_dl], [b_w])
                    xv = x.rearrange("p (h q) -> p h q", q=64)
                    xw, b_xw = xw_r.next()
                    self.tt("gpsimd", xw.rearrange("p (h q) -> p h q", q=64), xv, w.unsqueeze(2).broadcast_to([64, 32, 64]),
                            ALU.mult, [b_x, b_w], [b_xw])
                    C = dict(d=d, row=row, A2=A2, need_out=need_out, tok0=tok0, bt=bt, b_bt=b_bt, e12=e12, b_e12=b_e12,
                             etot=etot, b_et=b_et, xw=xw, b_xw=b_xw)
                    if need_out:
                        lahl, b_lahl = lahl_r.next()
                        P.dma("sync", lahl[0:64, :], dltok[grow, :, 2, d * 32:(d + 1) * 32], writes=[b_lahl])
                        P.dma("sync", lahl[64:128, :], dltok[grow, :, 3, d * 32:(d + 1) * 32], writes=[b_lahl])
                        ula, b_ula = ula_r.next()
                        self.tt("gpsimd", ula.rearrange("p (h t) -> p h t", t=64), lahl.unsqueeze(2).broadcast_to([128, 32, 64]),
                                self.masks2[:, d, :].unsqueeze(1).broadcast_to([128, 32, 64]), ALU.mult, [b_lahl, self.b_cst], [b_ula])
                        xdt, b_xdt = xdt_r.next()
                        self.tt("vector", xdt.rearrange("p (h q) -> p h q", q=64), xv, dt_d.unsqueeze(2).broadcast_to([64, 32, 64]),
                                ALU.mult, [b_x, b_dl], [b_xdt])
                        cp_, b_cp = sp[:, 128:384], b_sp
                        for g in range(4):
                            self.mm(cp_[0:64, g * 64:(g + 1) * 64], bc[:, g, tok0:tok0 + 64], bc[:, 4 + g, tok0:tok0 + 64], True, True,
                                    [b_bc], [b_cp])
                        cbm, b_cbm = cbm_r.next()
                        self.tt("vector", cbm.rearrange("p (g t) -> p g t", t=64), cp_[0:64, 0:256].rearrange("p (g t) -> p g t", t=64),
                                A1.unsqueeze(1).broadcast_to([64, 4, 64]), ALU.mult, [b_cp, self.b_cst], [b_cbm])
                        mt, b_mt = mtall_r.next()
                        for g in range(4):
                            gs = slice(g * 512, (g + 1) * 512)
                            Dp, b_Dp = D_ps.next()
                            self.mm(Dp[0:64, 0:512], self.masks2b[:, 2 + d, :], ula[:, gs], True, True, [b_ula, self.b_cst], [b_Dp])
                            ed, b_ed = ed_r.next()
                            self.act(ed, Dp[0:64, 0:512], AF.Exp, [b_Dp], [b_ed])
                            self.tt("vector", mt[:, gs].rearrange("p (h t) -> p h t", t=64), ed.rearrange("p (h t) -> p h t", t=64),
                                    cbm[:, g * 64:(g + 1) * 64].unsqueeze(1).broadcast_to([64, 8, 64]), ALU.mult, [b_ed, b_cbm], [b_mt])
                        C.update(xdt=xdt, b_xdt=b_xdt, mt=mt, b_mt=b_mt)
                    return C

                def stageB(C):
                    d, row, A2, tok0 = C["d"], C["row"], C["A2"], C["tok0"]
                    store = None
                    bt, b_bt, e12, b_e12, etot, b_et, xw, b_xw = (C["bt"], C["b_bt"], C["e12"], C["b_e12"], C["etot"], C["b_et"],
                                                                  C["xw"], C["b_xw"])
                    if C["need_out"]:
                        xdt, b_xdt, mt, b_mt = C["xdt"], C["b_xdt"], C["mt"], C["b_mt"]
                        yrow, b_yr = yrow_r.next()
                        for g in range(4):
                            gs = slice(g * 512, (g + 1) * 512)
                            yp, b_yp = y_ps.next()
                            for hh in range(8):
                                h = g * 8 + hh
                                self.mm(yp[0:64, hh * 64:(hh + 1) * 64], mt[:, h * 64:(h + 1) * 64], xdt[:, h * 64:(h + 1) * 64],
                                        True, True, [b_mt, b_xdt], [b_yp])
                            if has_state[d]:
                                ip, b_ip = in_ps.next()
                                self.mm(ip[0:64, 0:512], bc[:, 4 + g, tok0:tok0 + 64], STbf[d][:, gs], True, True,
                                        [b_bc, bSTbf[d][g]], [b_ip])
                                tmp, b_tmp = tmp_r.next()
                                self.tt("vector", tmp.rearrange("p (h q) -> p h q", q=64), ip[0:64, 0:512].rearrange("p (h q) -> p h q", q=64),
                                        e12[:, g * 8:(g + 1) * 8].unsqueeze(2).broadcast_to([64, 8, 64]), ALU.mult, [b_ip, b_e12], [b_tmp])
                                self.tt("vector", yrow[:, gs], tmp, yp[0:64, 0:512], ALU.add, [b_tmp, b_yp], [b_yr])
                            else:
                                self.cp("scalar", yrow[:, gs], yp[0:64, 0:512], [b_yp], [b_yr])
                        lrow = s * 32 + (row - 4)
                        store = (yfb[d, lrow], yrow, b_yr)
                    for g in range(4):
                        gs = slice(g * 512, (g + 1) * 512)
                        kp, b_kp = kv_ps.next()
                        self.mm(kp[:, 0:512], bt[:, g * 128:(g + 1) * 128], xw[:, gs], True, True, [b_bt, b_xw], [b_kp])
                        if has_state[d]:
                            stv = ST[d][:, gs].rearrange("p (h q) -> p h q", q=64)
                            self.tt("gpsimd", stv, stv, etot[:, g * 8:(g + 1) * 8].unsqueeze(2).broadcast_to([128, 8, 64]), ALU.mult,
                                    [bST[d][g], b_et], [bST[d][g]])
                            self.tt("vector", ST[d][:, gs], ST[d][:, gs], kp[:, 0:512], ALU.add, [bST[d][g], b_kp], [bST[d][g]])
                        else:
                            self.cp("vector", ST[d][:, gs], kp[:, 0:512], [b_kp], [bST[d][g]])
                        self.cp("scalar", STbf[d][:, gs], ST[d][:, gs], [bST[d][g]], [bSTbf[d][g]])
                    if store is not None:
                        P.dma("gpsimd", store[0], store[1], reads=[store[2]])
                    has_state[d] = True

                LAG = 2
                ctxs = {}
                for t_ in range(len(seq_items) + LAG):
                    if t_ < len(seq_items):
                        ctxs[t_] = stageA(*seq_items[t_])
                    if t_ >= LAG:
                        stageB(ctxs.pop(t_ - LAG))

    def phase_ssd_fin(self, zsT, xsT, yfb, mix2T):
        P = self.P
        with self.phase("ssdfin"):
            yf_r = Ring(self, [64, 2048], BF16, 3)
            yb_r = Ring(self, [64, 2048], BF16, 3)
            yT_r = Ring(self, [128, 16, 512], F32, 1)
            tp_ps = Ring(self, [128, 512], F32, 3, psum=True)
            n_ps = Ring(self, [128, 512], F32, 1, psum=True)
            xf_r = Ring(self, [128, 4, 512], BF16, 2)
            zf_r = Ring(self, [128, 4, 512], BF16, 2)
            y_r = Ring(self, [128, 4, 512], F32, 1)
            sq_r = Ring(self, [128, 4, 512], BF16, 1)
            r_r = Ring(self, [128, 512], F32, 2)
            yo_r = Ring(self, [128, 4, 512], BF16, 2)
            for s in range(2):
                for b in range(4):
                    to = LCTX + 512 * b
                    off = s * SEQT + to
                    yT, b_yT = yT_r.next()
                    for rr in range(8):
                        lrow = s * 32 + b * 8 + rr
                        yf, b_yf = yf_r.next()
                        yb, b_yb = yb_r.next()
                        P.dma("sync", yf, yfb[0, lrow], writes=[b_yf])
                        P.dma("sync", yb, yfb[1, lrow], writes=[b_yb])
                        for half in range(2):
                            tp, b_tp = tp_ps.next()
                            for c8 in range(8):
                                c = half * 8 + c8
                                self.mm(tp[:, c8 * 64:(c8 + 1) * 64], yf[:, c * 128:(c + 1) * 128], self.ident[0:64, 0:64], True, False,
                                        [b_yf, self.b_cst], [b_tp])
                                self.mm(tp[:, c8 * 64:(c8 + 1) * 64], yb[:, c * 128:(c + 1) * 128], self.ident[0:64, 0:64], False, True,
                                        [b_yb, self.b_cst], [b_tp])
                            self.cp("scalar" if half == 0 else "vector", yT[:, half * 8:(half + 1) * 8, rr * 64:(rr + 1) * 64],
                                    tp[:, 0:512].rearrange("p (c t) -> p c t", t=64), [b_tp], [b_yT])
                    for g in range(4):
                        xf, b_xf = xf_r.next()
                        zf, b_zf = zf_r.next()
                        P.dma("sync", xf, xsT[4 * g:4 * g + 4, :, off:off + 512].rearrange("c p t -> p c t"), writes=[b_xf])
                        P.dma("sync", zf, zsT[4 * g:4 * g + 4, :, off:off + 512].rearrange("c p t -> p c t"), writes=[b_zf])
                        y, b_y = y_r.next()
                        sq, b_sq = sq_r.next()
                        ps, b_ps = n_ps.next()
                        for pr in range(4):
                            cix = 4 * g + pr
                            self.stt("vector", y[:, pr, :], xf[:, pr, :], self.dsk[:, cix:cix + 1], yT[:, cix, :],
                                     ALU.mult, ALU.add, [b_xf, b_yT, self.b_cst], [b_y])
                        self.tt("gpsimd", y, y, zf, ALU.mult, [b_y, b_zf], [b_y])
                        self.act(sq, y, AF.Square, [b_y], [b_sq])
                        for pr in range(4):
                            self.mm(ps[:, 0:512], self.ones, sq[:, pr, :], pr == 0, pr == 3, [b_sq], [b_ps])
                        r, b_r = r_r.next()
                        self.rsqrt_from(r, ps[:, 0:512], 1.0 / 512, [b_ps], b_r)
                        yo, b_yo = yo_r.next()
                        for pr in range(4):
                            cix = 4 * g + pr
                            self.stt("vector", yo[:, pr, :], y[:, pr, :], self.sng[:, cix:cix + 1], r, ALU.mult, ALU.mult,
                                     [b_y, b_r, self.b_cst], [b_yo])
                        P.dma("gpsimd", mix2T[4 * g:4 * g + 4, :, off:off + 512].rearrange("c p t -> p c t"), yo, reads=[b_yo])

    def phase_ssd(self, zsT, xsT, xstok, bcT, dtT, mix2T):
        P = self.P
        with self.phase("ssd"):
            NCH = 4
            W = self.scan_setup(NCH, 64)
            Br = Ring(self, [128, SEQT], BF16, 1)
            Cr = Ring(self, [128, SEQT], BF16, 1)
            dtr = Ring(self, [64, SEQT], F32, 1)
            xtr = Ring(self, [64, 36, 512], BF16, 1)
            oaccs = [(self.T([128, SEQT], F32), None) for _ in range(4)]
            dps = W["a_ps"]
            dtb_r = Ring(self, [128, 512], F32, 2)
            la_r = Ring(self, [128, 512], F32, 2)
            kk_r = Ring(self, [128, 512], F32, 2)
            xf_r = Ring(self, [128, 4, 512], BF16, 1)
            zf_r = Ring(self, [128, 4, 512], BF16, 1)
            y_r = Ring(self, [128, 4, 512], F32, 1)
            sq_r = Ring(self, [128, 4, 512], BF16, 1)
            r_r = Ring(self, [128, 512], F32, 1)
            yo_r = Ring(self, [128, 4, 512], BF16, 1)
            fwd = [(0, LCTX)] + [(LCTX + 512 * b, 512) for b in range(4)]
            bwd = [(0, LCTX)] + [(LCTX + 512 * b, 512) for b in (3, 2, 1, 0)]
            for s in range(2):
                sl = slice(s * SEQT, (s + 1) * SEQT)
                dt, b_dt = dtr.next()
                P.dma("sync", dt, dtT[:, sl], writes=[b_dt])
                for g in range(4):
                    Bt, b_B = Br.next()
                    Ct, b_C = Cr.next()
                    xt, b_xt = xtr.next()
                    P.dma("sync", Bt, bcT[g, :, sl], writes=[b_B])
                    P.dma("sync", Ct, bcT[4 + g, :, sl], writes=[b_C])
                    P.dma("sync", xt, xstok[s * 36:(s + 1) * 36, :, g * 512:(g + 1) * 512].rearrange("r p f -> p r f"), writes=[b_xt])
                    b_oas = [P.buf() for _ in range(4)]
                    for pr in range(4):
                        P.op("gpsimd", (lambda t=oaccs[pr][0]: lambda e: e.memset(t, 0.0))(), writes=[b_oas[pr]])
                    jobs = []
                    for hh in range(8):
                        for (dr, sg) in ((0, +1), (1, -1)):
                            jobs.append((hh, dr, sg))
                    for j0 in range(0, len(jobs), NCH):
                        batch = jobs[j0:j0 + NCH]
                        for ci in range(len(batch)):
                            self.chain_reset(W["chains"][ci])
                        for step in range(5):
                            preps = []
                            for ci, (hh, dr, sg) in enumerate(batch):
                                ch = W["chains"][ci]
                                to, n = (fwd if sg > 0 else bwd)[step]
                                hidx = g * 8 + hh
                                col = dr * 32 + hidx
                                dtv, b_dtv = dtb_r.next()
                                P.dma("sync", dtv[:, 0:n], dtT[col:col + 1, s * SEQT + to:s * SEQT + to + n].partition_broadcast(128),
                                      writes=[b_dtv])
                                self.act(dtv[:, 0:n], dtv[:, 0:n], AF.Exp, [b_dtv, self.b_cst], [b_dtv], bias=self.dtb[:, col:col + 1], scale=1.0)
                                self.act(dtv[:, 0:n], dtv[:, 0:n], AF.Ln, [b_dtv, self.b_cst], [b_dtv], bias=self.one_t[:, 0:1], scale=1.0)
                                la, b_la = la_r.next()
                                self.ts("vector", la[:, 0:n], dtv[:, 0:n], self.aneg[:, col:col + 1], None, ALU.mult, None,
                                        [b_dtv, self.b_cst], [b_la])
                                kk, b_kk = kk_r.next()
                                self.tt("vector", kk[:, 0:n], Bt[:, to:to + n], dtv[:, 0:n], ALU.mult, [b_B, b_dtv], [b_kk])
                                pair = hh // 2
                                pbo = 64 * (hh % 2)
                                oacc = oaccs[pair][0]
                                o_out = None
                                if to >= LCTX:
                                    o_out = (oacc[pbo:pbo + 64, to:to + n], b_oas[pair])
                                preps.append(self.scan_prep(W, ch, Ct[:, to:to + n], kk[:, 0:n], la[:, 0:n], [b_C, b_kk, b_la],
                                                            (lambda i, to=to, xt=xt, b_xt=b_xt, hh=hh: (xt[:, to // 64 + i, hh * 64:(hh + 1) * 64], b_xt)),
                                                            n, 64, sg, o_out, pbo, scalar=True))
                            self.scan_step(preps)
                    for b in range(4):
                        to = LCTX + 512 * b
                        off = s * SEQT + to
                        xf, b_xf = xf_r.next()
                        zf, b_zf = zf_r.next()
                        P.dma("sync", xf, xsT[4 * g:4 * g + 4, :, off:off + 512].rearrange("c p t -> p c t"), writes=[b_xf])
                        P.dma("sync", zf, zsT[4 * g:4 * g + 4, :, off:off + 512].rearrange("c p t -> p c t"), writes=[b_zf])
                        y, b_y = y_r.next()
                        sq, b_sq = sq_r.next()
                        ps, b_ps = dps.next()
                        for pr in range(4):
                            cix = 4 * g + pr
                            self.stt("vector", y[:, pr, :], xf[:, pr, :], self.dsk[:, cix:cix + 1], oaccs[pr][0][:, to:to + 512],
                                     ALU.mult, ALU.add, [b_xf, b_oas[pr], self.b_cst], [b_y])
                        self.tt("vector", y, y, zf, ALU.mult, [b_y, b_zf], [b_y])
                        self.act(sq, y, AF.Square, [b_y], [b_sq])
                        for pr in range(4):
                            self.mm(ps[:, 0:512], self.ones, sq[:, pr, :], pr == 0, pr == 3, [b_sq], [b_ps])
                        r, b_r = r_r.next()
                        self.rsqrt_from(r, ps[:, 0:512], 1.0 / 512, [b_ps], b_r)
                        yo, b_yo = yo_r.next()
                        for pr in range(4):
                            cix = 4 * g + pr
                            self.stt("vector", yo[:, pr, :], y[:, pr, :], self.sng[:, cix:cix + 1], r, ALU.mult, ALU.mult,
                                     [b_y, b_r, self.b_cst], [b_yo])
                        P.dma("gpsimd", mix2T[4 * g:4 * g + 4, :, off:off + 512].rearrange("c p t -> p c t"), yo, reads=[b_yo])


def _fm(v, nchunk):
    return np.ascontiguousarray(np.asarray(v, np.float32).reshape(nchunk, 128).T)


def _wt(w):
    K, N = w.shape
    return np.ascontiguousarray(np.asarray(w, np.float32).reshape(K // 128, 128, N).transpose(1, 0, 2))


def host_inputs(I):
    f32 = np.float32
    sh = {}
    sh["wmod"] = np.stack([_wt(I["w_mod"][l]) for l in range(2)])
    sh["bmod"] = np.ascontiguousarray(np.asarray(I["b_mod"], f32).reshape(2, 48, 128).transpose(2, 0, 1))
    sh["nmix"] = np.ascontiguousarray(np.asarray(I["norm_mix"], f32).reshape(2, KC, 128).transpose(2, 0, 1))
    sh["nffn"] = np.ascontiguousarray(np.asarray(I["norm_ffn"], f32).reshape(2, KC, 128).transpose(2, 0, 1))
    sh["ffn_up"] = np.stack([_wt(I["ffn_w_up"][l]) for l in range(2)])
    cw = np.concatenate([np.asarray(I["ffn_conv_w"], f32), np.asarray(I["ffn_conv_b"], f32)[:, None, :]], axis=1)
    sh["ffn_cw"] = np.ascontiguousarray(cw.reshape(2, 4, NFF, 128).transpose(3, 0, 2, 1))
    sh["ffn_dn"] = np.stack([_wt(I["ffn_w_down"][l]) for l in range(2)])
    sh["hy_in"] = _wt(I["hy_w_in"][0])
    sh["hy_out"] = _wt(I["hy_w_out"][0])
    sh["na_g"] = np.ascontiguousarray(np.stack([np.tile(np.asarray(I["na_q_gain"][0], f32), 2),
                                                np.tile(np.asarray(I["na_k_gain"][0], f32), 2)], axis=1))
    rpb = np.asarray(I["na_rpb"][0], f32)
    cp = np.arange(64)[:, None]
    cq = np.arange(64)[None, :]
    dc = np.clip(cp - cq + 15, 0, 30)
    tab = rpb[:, ::-1, :][:, :, dc]
    sh["na_bias"] = np.ascontiguousarray(tab.transpose(2, 0, 1, 3).reshape(64, 8 * 15 * 64))
    ws = np.clip(np.arange(64) - 8, 0, 48)[None, :]
    inwin = (cp >= ws) & (cp < ws + 16)
    sh["na_mask"] = np.where(inwin, 0.0, -30000.0).astype(f32)
    sh["hg_gain"] = np.asarray(I["hg_out_gain"][0], f32).reshape(128, 1).copy()
    lb = np.stack([np.asarray(I["hg_lb_fwd"], f32), np.asarray(I["hg_lb_bwd"], f32)])
    sh["hg_lb"] = np.ascontiguousarray(lb.reshape(2, 3, 4, 128).transpose(3, 0, 1, 2))
    sh["ssd_in"] = _wt(I["ssd_w_in"][0])
    scw = np.concatenate([np.asarray(I["ssd_conv_w"][0], f32), np.asarray(I["ssd_conv_b"][0], f32)[None, :]], axis=0)
    sh["ssd_cw"] = np.ascontiguousarray(scw.reshape(4, 24, 128).transpose(2, 1, 0))
    dtb = np.concatenate([np.asarray(I["ssd_dt_bias_fwd"][0], f32), np.asarray(I["ssd_dt_bias_bwd"][0], f32)])
    sh["ssd_dtb"] = np.ascontiguousarray(np.broadcast_to(dtb[None, :], (128, 64)))
    al = np.concatenate([np.asarray(I["ssd_a_log_fwd"][0], f32), np.asarray(I["ssd_a_log_bwd"][0], f32)])
    sh["ssd_alog"] = np.ascontiguousarray(np.broadcast_to(al[None, :], (128, 64)))
    dsk = np.repeat(np.asarray(I["ssd_d"][0], f32), 64)
    sh["ssd_dsk"] = _fm(dsk, 16)
    sh["ssd_ng"] = _fm(I["ssd_norm_gain"][0], 16)
    sh["ssd_out"] = _wt(I["ssd_w_out"][0])
    sh["cst_ident"] = np.eye(128, dtype=f32)
    bo = np.zeros((128, 128), f32)
    bo[0:64, 0:64] = 1.0
    bo[64:128, 64:128] = 1.0
    sh["cst_bones"] = bo
    si = np.arange(64)[:, None]
    ti = np.arange(64)[None, :]
    sh["cst_masks"] = np.ascontiguousarray(np.stack([(si <= ti), (si >= ti), (si > ti), (si < ti)], axis=1).astype(f32))
    sh["cst_masks2"] = np.ascontiguousarray(np.concatenate([sh["cst_masks"], sh["cst_masks"]], axis=0))
    rm = np.ones((128, 512), f32)
    rm[:, ::64] = 0.0
    sh["cst_reset"] = rm
    per_core = []
    x = np.asarray(I["x"], f32)
    ctx = np.asarray(I["ctx"], f32)
    c = np.asarray(I["c"], f32)
    cc = np.asarray(I["c_ctx"], f32)
    for i in range(8):
        toks = np.concatenate([ctx[2 * i], x[2 * i], ctx[2 * i + 1], x[2 * i + 1]], axis=0)
        hin = np.ascontiguousarray(toks.T.reshape(KC, 128, NT))
        cm = np.stack([c[2 * i], c[2 * i + 1], cc, cc], axis=1)
        cTt = np.ascontiguousarray(cm.reshape(KC, 128, 4).transpose(1, 0, 2))
        d = dict(sh)
        d["hin"] = hin
        d["cT"] = cTt
        per_core.append(d)
    return per_core


_CACHE = {}


def build_program(dbg=(), stop_after=None):
    key = (tuple(sorted(dbg)), stop_after)
    if key not in _CACHE:
        kb = KB(dbg, stop_after)
        kb.build()
        _CACHE[key] = kb
    return _CACHE[key]


def kernel(**inputs):
    kb = build_program()
    in_maps = host_inputs(inputs)
    names = set(kb.inputs.keys())
    in_maps = [{k: v for k, v in m.items() if k in names} for m in in_maps]
    res = run_bass_kernel_spmd(kb.nc, in_maps, core_ids=list(range(8)))
    out = np.empty((16, LLAT, D), np.float32)
    for i in range(8):
        o = np.asarray(res.results[i]["outT"], np.float32).reshape(D, 2 * LLAT)
        out[2 * i] = o[:, 0:LLAT].T
        out[2 * i + 1] = o[:, LLAT:2 * LLAT].T
    return out
```

```python
import contextlib
import numpy as np
import concourse.bass as bass
import concourse.mybir as mybir
from concourse.bass_utils import run_bass_kernel_spmd

F32 = mybir.dt.float32
BF16 = mybir.dt.bfloat16
AF = mybir.ActivationFunctionType
ALU = mybir.AluOpType

COMPUTE = ("tensor", "vector", "scalar", "gpsimd")
ALLENG = ("tensor", "vector", "scalar", "gpsimd", "sync")
NDSEM = 8

D = 1024
KC = 8
LCTX = 256
LLAT = 2048
SEQT = LCTX + LLAT
NT = 2 * SEQT
NROW = NT // 64
DFF = 2816
NFF = DFF // 128
EPS = 1e-6


class Buf:
    __slots__ = ("name", "writer", "readers", "dma_readers", "excl")

    def __init__(self, name=""):
        self.name = name
        self.excl = False
        self.writer = None
        self.readers = {}
        self.dma_readers = []

    def reset(self):
        self.writer = None
        self.readers = {}
        self.dma_readers = []


class Prog:
    def __init__(self, nc, stack):
        self.nc = nc
        self.ops = []
        self.bufs = []
        self.csem = {e: stack.enter_context(nc.semaphore("s_" + e)) for e in COMPUTE}
        self.dsem = {e: [stack.enter_context(nc.semaphore("d_%s_%d" % (e, j))) for j in range(NDSEM)]
                     for e in ("sync", "gpsimd", "scalar")}
        self.bar = stack.enter_context(nc.semaphore("bar"))
        self.cnt = {e: 0 for e in COMPUTE}
        self.dcnt = {e: 0 for e in self.dsem}
        self.nphase = 0
        self.total_ops = 0

    def buf(self, name=""):
        b = Buf(name)
        self.bufs.append(b)
        return b

    def op(self, eng, fn, reads=(), writes=(), dma=False):
        ex = [b for b in reads if b.excl and b not in writes]
        if ex:
            writes = list(writes) + ex
            reads = [b for b in reads if not b.excl]
        idx = len(self.ops)
        deps = set()
        for b in reads:
            if b.writer is not None:
                deps.add(b.writer)
        for b in writes:
            if b.writer is not None:
                deps.add(b.writer)
            for r in b.readers.values():
                deps.add(r)
            for r in b.dma_readers:
                deps.add(r)
        self.ops.append(dict(eng=eng, fn=fn, deps=deps, dma=dma, signal=False))
        for b in writes:
            b.writer = idx
            b.readers = {}
            b.dma_readers = []
        for b in reads:
            if b in writes:
                continue
            if dma:
                b.dma_readers.append(idx)
            else:
                b.readers[eng] = idx
        return idx

    def dma(self, q, out, in_, reads=(), writes=(), **kw):
        return self.op(q, lambda e: e.dma_start(out=out, in_=in_, **kw), reads, writes, dma=True)

    def emit_phase(self):
        nc = self.nc
        ops = self.ops
        if not ops:
            return
        for o in ops:
            nd = set()
            for d in o["deps"]:
                od = ops[d]
                if (not od["dma"]) and (not o["dma"]) and od["eng"] == o["eng"] == "tensor":
                    continue
                nd.add(d)
            o["deps"] = nd
            for d in nd:
                ops[d]["signal"] = True
        last_c = {}
        for o in ops:
            e = o["eng"]
            if o["dma"]:
                k = self.dcnt[e]
                self.dcnt[e] = k + 1
                o["dslot"] = k % NDSEM
                o["dval"] = 16 * (k // NDSEM + 1)
            else:
                last_c[e] = o
        for e, o in last_c.items():
            o["signal"] = True
        for o in ops:
            if (not o["dma"]) and o["signal"]:
                e = o["eng"]
                self.cnt[e] += 1
                o["seq"] = self.cnt[e]
        csem, dsem, bar = self.csem, self.dsem, self.bar
        phase = self.nphase
        cnt_end = dict(self.cnt)
        dcnt_end = dict(self.dcnt)

        def body_for(ename):
            def body(eng):
                if phase > 0:
                    eng.wait_ge(bar, len(ALLENG) * phase)
                seen = {}
                for o in ops:
                    if o["eng"] != ename:
                        continue
                    waits = []
                    for d in o["deps"]:
                        od = ops[d]
                        if od["dma"]:
                            waits.append((("d", od["eng"], od["dslot"]), od["dval"]))
                        else:
                            waits.append((("c", od["eng"]), od["seq"]))
                    if o["dma"] and o["dval"] > 16:
                        waits.append((("d", ename, o["dslot"]), o["dval"] - 16))
                    best = {}
                    for k, v in waits:
                        if v > best.get(k, 0):
                            best[k] = v
                    for k, v in best.items():
                        if seen.get(k, 0) >= v:
                            continue
                        seen[k] = v
                        s = csem[k[1]] if k[0] == "c" else dsem[k[1]][k[2]]
                        eng.wait_ge(s, v)
                    ins = o["fn"](eng)
                    if o["dma"]:
                        ins.then_inc(dsem[ename][o["dslot"]], 16)
                    elif o["signal"]:
                        ins.then_inc(csem[ename], 1)
                if ename in COMPUTE and cnt_end[ename] > 0:
                    eng.wait_ge(csem[ename], cnt_end[ename])
                if ename in dsem:
                    k = dcnt_end[ename]
                    for sl in range(NDSEM):
                        n = (k - sl + NDSEM - 1) // NDSEM if k > sl else 0
                        if n > 0:
                            eng.wait_ge(dsem[ename][sl], 16 * n)
                eng.sem_inc(bar, 1)
            return body

        with nc.Block() as block:
            for e in ALLENG:
                getattr(block, e)(body_for(e))
        self.nphase += 1
        self.total_ops += len(ops)
        self.ops = []
        for b in self.bufs:
            b.reset()
        self.bufs = []


class Ring:
    def __init__(self, kb, shape, dt, n, psum=False):
        self.items = []
        for _ in range(n):
            if psum:
                ap = kb.PS([128, 512], F32) if dt == F32 else kb.PS([128, 1024], BF16)
            else:
                ap = kb.T(shape, dt)
            b = kb.P.buf()
            b.excl = psum
            self.items.append((ap, b))
        self.i = 0

    def next(self):
        it = self.items[self.i % len(self.items)]
        self.i += 1
        return it


def all_blocks(with_ctx=True):
    out = []
    for s in range(2):
        if with_ctx:
            out.append((s, "c", s * SEQT, LCTX, 2))
        for b in range(4):
            out.append((s, "l", s * SEQT + LCTX + 512 * b, 512, s))
    return out


def all_segments(with_ctx=True):
    out = []
    for s in range(2):
        if with_ctx:
            out.append((s, "c", s * SEQT, LCTX, 2))
        out.append((s, "l", s * SEQT + LCTX, LLAT, s))
    return out


class KB:
    def __init__(self, dbg=(), stop_after=None):
        self.nc = bass.Bass("TRN2", target_bir_lowering=False)
        self.dbg = set(dbg)
        self.stop_after = stop_after
        self.uid = 0
        self.inputs = {}

    def din(self, name, shape):
        order = ["setup", "l0mod", "l0proj", "na", "gla", "l0out", "l0ffn", "l1proj", "ssd", "l1out"]
        first = {"hy_in": 2, "hy_out": 5, "ffn_up": 6, "ffn_dn": 6, "ssd_in": 7, "ssd_out": 9}
        if self.stop_after is not None and name in first and order.index(self.stop_after) < first[name]:
            return self.nc.dram_tensor(name, list(shape), F32, kind="Internal").ap()
        ap = self.nc.dram_tensor(name, list(shape), F32, kind="ExternalInput").ap()
        self.inputs[name] = ap
        return ap

    def dscr(self, name, shape, dt):
        kind = "ExternalOutput" if name in self.dbg else "Internal"
        return self.nc.dram_tensor(name, list(shape), dt, kind=kind).ap()

    def T(self, shape, dt, persistent=False, stack=None):
        self.uid += 1
        st = stack if stack is not None else (self.gst if persistent else self.st)
        return st.enter_context(self.nc.sbuf_tensor("t%d" % self.uid, list(shape), dt)).ap()

    def PS(self, shape=(128, 512), dt=F32):
        self.uid += 1
        return self.st.enter_context(self.nc.psum_tensor("p%d" % self.uid, list(shape), dt)).ap()

    @contextlib.contextmanager
    def phase(self, name):
        with contextlib.ExitStack() as st:
            self.st = st
            yield
            self.P.emit_phase()

    def mm(self, out, lhsT, rhs, start, stop, reads, writes):
        self.P.op("tensor", lambda e: e.matmul(out, lhsT, rhs, start=start, stop=stop), reads, writes)

    def tr(self, out, in_, ident, reads, writes):
        self.P.op("tensor", lambda e: e.transpose(out, in_, ident), reads, writes)

    def act(self, out, in_, func, reads, writes, bias=None, scale=None):
        kw = {}
        if bias is not None:
            kw["bias"] = bias
        if scale is not None:
            kw["scale"] = scale
        self.P.op("scalar", lambda e: e.activation(out=out, in_=in_, func=func, **kw), reads, writes)

    def tt(self, eng, out, in0, in1, op, reads, writes):
        self.P.op(eng, lambda e: e.tensor_tensor(out, in0, in1, op), reads, writes)

    def ts(self, eng, out, in0, s1, s2, op0, op1, reads, writes):
        if s2 is None:
            self.P.op(eng, lambda e: e.tensor_scalar(out, in0, s1, None, op0), reads, writes)
        else:
            self.P.op(eng, lambda e: e.tensor_scalar(out, in0, s1, s2, op0, op1), reads, writes)

    def stt(self, eng, out, in0, scalar, in1, op0, op1, reads, writes):
        self.P.op(eng, lambda e: e.scalar_tensor_tensor(out, in0, scalar, in1, op0, op1), reads, writes)

    def cp(self, eng, out, in_, reads, writes):
        if eng == "scalar":
            self.P.op(eng, lambda e: e.copy(out, in_), reads, writes)
        else:
            self.P.op(eng, lambda e: e.tensor_copy(out, in_), reads, writes)

    def rsqrt_from(self, out, in_, scale, reads, writes_buf):
        self.act(out, in_, AF.Ln, reads + [self.b_cst], [writes_buf], bias=self.eps_t[0:out.shape[0], 0:1], scale=scale)
        self.act(out, out, AF.Exp, [writes_buf], [writes_buf], scale=-0.5)

    def build(self):
        nc = self.nc
        hin = self.din("hin", [KC, 128, NT])
        cT = self.din("cT", [128, KC, 4])
        wmod = self.din("wmod", [2, 128, KC, 6 * D])
        bmod = self.din("bmod", [128, 2, 48])
        nmix = self.din("nmix", [128, 2, KC])
        nffn = self.din("nffn", [128, 2, KC])
        ffn_up = self.din("ffn_up", [2, 128, KC, 2 * DFF])
        ffn_cw = self.din("ffn_cw", [128, 2, NFF, 4])
        ffn_dn = self.din("ffn_dn", [2, 128, NFF, D])
        hy_in = self.din("hy_in", [128, KC, 4096])
        hy_out = self.din("hy_out", [128, KC, D])
        na_g = self.din("na_g", [128, 2])
        na_bias = self.din("na_bias", [64, 8 * 15 * 64])
        na_mask = self.din("na_mask", [64, 64])
        hg_gain = self.din("hg_gain", [128, 1])
        hg_lb = self.din("hg_lb", [128, 2, 3, 4])
        ssd_in = self.din("ssd_in", [128, KC, 5184])
        ssd_cw = self.din("ssd_cw", [128, 24, 4])
        ssd_dtb = self.din("ssd_dtb", [128, 64])
        ssd_alog = self.din("ssd_alog", [128, 64])
        ssd_dsk = self.din("ssd_dsk", [128, 16])
        ssd_ng = self.din("ssd_ng", [128, 16])
        ssd_out = self.din("ssd_out", [128, 16, D])
        cst_ident = self.din("cst_ident", [128, 128])
        cst_bones = self.din("cst_bones", [128, 128])
        cst_masks = self.din("cst_masks", [64, 4, 64])
        cst_masks2 = self.din("cst_masks2", [128, 4, 64])
        cst_reset = self.din("cst_reset", [128, 512])
        outT = nc.dram_tensor("outT", [KC, 128, 2 * LLAT], F32, kind="ExternalOutput").ap()

        hT = self.dscr("hT", [KC, 128, NT], F32)
        qT = self.dscr("qT", [4, 128, NT], BF16)
        kT = self.dscr("kT", [4, 128, NT], BF16)
        vtok = self.dscr("vtok", [NROW, 64, 512], BF16)
        hqT = self.dscr("hqT", [4, 128, NT], BF16)
        kfT = self.dscr("kfT", [2, 4, 128, NT], BF16)
        gfT = self.dscr("gfT", [2, 4, 128, NT], F32)
        vitok = self.dscr("vitok", [NROW, 64, 512], BF16)
        gateT = self.dscr("gateT", [4, 128, NT], BF16)
        mixT = self.dscr("mixT", [KC, 128, NT], BF16)
        guT = self.dscr("guT", [NFF, 128, NT], BF16)
        zsT = self.dscr("zsT", [16, 128, NT], BF16)
        xsT = self.dscr("xsT", [16, 128, NT], BF16)
        xstok = self.dscr("xstok", [NROW, 64, 2048], BF16)
        bcT = self.dscr("bcT", [8, 128, NT], BF16)
        dtT = self.dscr("dtT", [64, NT], F32)
        btok = self.dscr("btok", [NROW, 64, 512], BF16)
        dltok = self.dscr("dltok", [NROW, 64, 4, 64], F32)
        yfb = self.dscr("yfb", [2, 64, 64, 2048], BF16)
        mix2T = self.dscr("mix2T", [16, 128, NT], BF16)

        with contextlib.ExitStack() as gst:
            self.gst = gst
            self.P = Prog(nc, gst)
            P = self.P
            self.modT = self.T([128, 2, 48, 4], F32, True)
            self.Am = self.T([128, 2, KC, 4], F32, True)
            self.Af = self.T([128, 2, KC, 4], F32, True)
            self.ident = self.T([128, 128], BF16, True)
            self.ones = self.T([128, 128], BF16, True)
            self.bones = self.T([128, 128], BF16, True)
            self.masks = self.T([64, 4, 64], F32, True)
            self.onesf = self.T([64, 128], F32, True)
            self.masks2 = self.T([128, 4, 64], F32, True)
            self.masks2b = self.T([128, 4, 64], BF16, True)
            self.resetm = self.T([128, 512], F32, True)
            self.eps_t = self.T([128, 1], F32, True)
            self.lbv = self.T([128, 2, 4], F32, True)
            self.omlv = self.T([128, 2, 4], F32, True)
            self.EB = self.T([64, 8 * 15 * 64], BF16, True)
            self.nag = self.T([128, 2], F32, True)
            self.hgg = self.T([128, 1], F32, True)
            self.ffcw = self.T([128, 2, NFF, 4], F32, True)
            self.sscw = self.T([128, 24, 4], F32, True)
            self.dtb = self.T([128, 64], F32, True)
            self.aneg = self.T([128, 64], F32, True)
            self.dsk = self.T([128, 16], F32, True)
            self.sng = self.T([128, 16], F32, True)
            self.one_t = self.T([128, 1], F32, True)
            self.identf = self.T([64, 64], F32, True)

            with self.phase("setup"):
                self.b_cst = P.buf("cst")
                bc = self.b_cst
                for dst, src in ((self.masks, cst_masks), (self.resetm, cst_reset), (self.nag, na_g), (self.hgg, hg_gain),
                                 (self.ffcw, ffn_cw), (self.sscw, ssd_cw), (self.dtb, ssd_dtb), (self.dsk, ssd_dsk),
                                 (self.sng, ssd_ng), (self.identf, cst_ident[0:64, 0:64]), (self.masks2, cst_masks2)):
                    P.dma("sync", dst, src, writes=[P.buf()])
                for dst, src in ((self.ident, cst_ident), (self.bones, cst_bones), (self.masks2b, cst_masks2)):
                    P.dma("gpsimd", dst, src, writes=[P.buf()])
                P.op("vector", lambda e: e.memset(self.ones, 1.0), writes=[P.buf()])
                P.op("vector", lambda e: e.memset(self.eps_t, EPS), writes=[P.buf()])
                P.op("vector", lambda e: e.memset(self.one_t, 1.0), writes=[P.buf()])
                P.op("vector", lambda e: e.memset(self.onesf, 1.0), writes=[P.buf()])
                b_al = P.buf()
                al = self.T([128, 64], F32)
                P.dma("sync", al, ssd_alog, writes=[b_al])
                self.act(al, al, AF.Exp, [b_al], [b_al])
                self.ts("vector", self.aneg, al, -1.0, None, ALU.mult, None, [b_al], [P.buf()])
                lbr = self.T([128, 2, 3, 4], F32)
                b_lb = P.buf()
                P.dma("sync", lbr, hg_lb, writes=[b_lb])
                self.act(lbr, lbr, AF.Exp, [b_lb], [b_lb])
                ssum = self.T([128, 2, 4], F32)
                b_ss = P.buf()
                self.tt("vector", ssum, lbr[:, :, 0, :], lbr[:, :, 1, :], ALU.add, [b_lb], [b_ss])
                self.tt("vector", ssum, ssum, lbr[:, :, 2, :], ALU.add, [b_lb, b_ss], [b_ss])
                P.op("vector", lambda e: e.reciprocal(ssum, ssum), [b_ss], [b_ss])
                b_lbv = P.buf()
                self.tt("vector", self.lbv, lbr[:, :, 0, :], ssum, ALU.mult, [b_lb, b_ss], [b_lbv])
                self.ts("vector", self.omlv, self.lbv, -1.0, 1.0, ALU.mult, ALU.add, [b_lbv], [P.buf()])
                bt = self.T([64, 120, 64], F32)
                mk = self.T([64, 64], F32)
                b_bt, b_mk = P.buf(), P.buf()
                P.dma("sync", bt, na_bias.rearrange("p (j c) -> p j c", c=64), writes=[b_bt])
                P.dma("sync", mk, na_mask, writes=[b_mk])
                self.tt("vector", bt, bt, mk.unsqueeze(1).broadcast_to([64, 120, 64]), ALU.add, [b_bt, b_mk], [b_bt])
                self.act(self.EB.rearrange("p (j c) -> p j c", c=64), bt, AF.Exp, [b_bt], [P.buf()])
                sc = self.T([128, KC, 4], F32)
                b_sc = P.buf()
                P.dma("sync", sc, cT, writes=[b_sc])
                self.act(sc, sc, AF.Silu, [b_sc], [b_sc])
                bm = self.T([128, 2, 48], F32)
                nm = self.T([128, 2, KC], F32)
                nf = self.T([128, 2, KC], F32)
                b_bm, b_nm, b_nf = P.buf(), P.buf(), P.buf()
                P.dma("sync", bm, bmod, writes=[b_bm])
                P.dma("sync", nm, nmix, writes=[b_nm])
                P.dma("sync", nf, nffn, writes=[b_nf])
                wring = Ring(self, [128, KC, 768], F32, 2)
                b_mod = P.buf()
                for l in range(2):
                    ps = self.PS()
                    b_ps = P.buf()
                    for cb in range(8):
                        wt, b_wt = wring.next()
                        P.dma("sync", wt, wmod[l, :, :, cb * 768:(cb + 1) * 768], writes=[b_wt])
                        for jj in range(6):
                            ch = cb * 6 + jj
                            for kc in range(KC):
                                self.mm(ps[:, ch * 4:(ch + 1) * 4], wt[:, kc, jj * 128:(jj + 1) * 128], sc[:, kc, :],
                                        kc == 0, kc == KC - 1, [b_wt, b_sc], [b_ps])
                    self.tt("vector", self.modT[:, l], ps[:, 0:192].rearrange("p (c j) -> p c j", j=4),
                            bm[:, l, :].unsqueeze(2).broadcast_to([128, 48, 4]), ALU.add, [b_ps, b_bm], [b_mod])
                    tmpa = self.T([128, KC, 4], F32)
                    b_ta = P.buf()
                    for (dst, lo, nv, b_nv) in ((self.Am, 8, nm, b_nm), (self.Af, 32, nf, b_nf)):
                        self.ts("vector", tmpa, self.modT[:, l, lo:lo + KC, :], 1.0, None, ALU.add, None, [b_mod], [b_ta])
                        self.tt("vector", dst[:, l], tmpa, nv[:, l, :].unsqueeze(2).broadcast_to([128, KC, 4]), ALU.mult,
                                [b_ta, b_nv], [P.buf()])
            if self.stop_after == "setup":
                return

            self.phase_l0_proj(hin, hy_in, qT, kT, vtok, hqT, kfT, gfT, vitok, gateT)
            if self.stop_after in ("l0proj", "l0mod"):
                return
            self.phase_na(qT, kT, vtok, mixT)
            if self.stop_after == "na":
                return
            self.phase_gla(hqT, kfT, gfT, vitok, gateT, mixT)
            if self.stop_after == "gla":
                return
            self.phase_outproj(0, hy_out, KC, mixT, hin, hT, True, None)
            if self.stop_after == "l0out":
                return
            self.phase_ffn(0, ffn_up, ffn_dn, guT, hT, hT, True, None)
            if self.stop_after == "l0ffn":
                return
            self.phase_l1_proj(hT, ssd_in, zsT, xsT, xstok, bcT, btok, dltok)
            if self.stop_after == "l1proj":
                return
            self.phase_ssd2(xstok, btok, bcT, dltok, yfb)
            if self.stop_after == "ssdscan":
                return
            self.phase_ssd_fin(zsT, xsT, yfb, mix2T)
            if self.stop_after == "ssd":
                return
            self.phase_outproj(1, ssd_out, 16, mix2T, hT, hT, False, None)
            if self.stop_after == "l1out":
                return
            self.phase_ffn(1, ffn_up, ffn_dn, guT, hT, None, False, outT)

    def modulate_all(self, ust, hsrc, l, A, shlo, blocks):
        U = self.T([128, KC, NT], BF16, stack=ust)
        with self.phase("modulate"):
            self._modulate(U, hsrc, l, A, shlo, blocks)
        return U

    def _modulate(self, U, hsrc, l, A, shlo, blocks):
        P = self.P
        hring = Ring(self, [128, KC, 512], F32, 3)
        sqring = Ring(self, [128, KC, 512], BF16, 2)
        rring = Ring(self, [128, 512], F32, 3)
        tring = Ring(self, [128, 512], F32, 3)
        psr = Ring(self, [128, 512], F32, 2, psum=True)
        hv = hsrc.rearrange("c p t -> p c t")

        def front_a(blk):
            (s, kind, off, n, mc) = blk
            hb, b_hb = hring.next()
            P.dma("sync", hb[:, :, 0:n], hv[:, :, off:off + n], writes=[b_hb])
            sq, b_sq = sqring.next()
            self.act(sq[:, :, 0:n], hb[:, :, 0:n], AF.Square, [b_hb], [b_sq])
            return dict(hb=hb, b_hb=b_hb, sq=sq, b_sq=b_sq)

        def front_b(blk, c_):
            (s, kind, off, n, mc) = blk
            sq, b_sq = c_["sq"], c_["b_sq"]
            ps, b_ps = psr.next()
            for c in range(KC):
                self.mm(ps[:, 0:n], self.ones, sq[:, c, 0:n], c == 0, c == KC - 1, [b_sq], [b_ps])
            rs, b_rs = rring.next()
            self.rsqrt_from(rs[:, 0:n], ps[:, 0:n], 1.0 / D, [b_ps], b_rs)
            c_["rs"], c_["b_rs"] = rs, b_rs

        def back(blk, c_):
            (s, kind, off, n, mc) = blk
            hb, b_hb, rs, b_rs = c_["hb"], c_["b_hb"], c_["rs"], c_["b_rs"]
            b_u = P.buf()
            for c in range(KC):
                tm, b_tm = tring.next()
                self.stt("vector", tm[:, 0:n], hb[:, c, 0:n], A[:, l, c, mc:mc + 1], rs[:, 0:n], ALU.mult, ALU.mult,
                         [b_hb, b_rs], [b_tm])
                self.act(U[:, c, off:off + n], tm[:, 0:n], AF.Identity, [b_tm], [b_u],
                         bias=self.modT[:, l, shlo + c, mc:mc + 1], scale=1.0)

        nb = len(blocks)
        ctxs = {}
        for i in range(min(2, nb)):
            ctxs[i] = front_a(blocks[i])
        front_b(blocks[0], ctxs[0])
        for i in range(nb):
            back(blocks[i], ctxs[i])
            if i + 1 < nb:
                front_b(blocks[i + 1], ctxs[i + 1])
            if i + 2 < nb:
                ctxs[i + 2] = front_a(blocks[i + 2])
            ctxs.pop(i)

    def load_w_chunk(self, dst, b_dst, src_ap):
        self.P.dma("gpsimd", dst, src_ap, writes=[b_dst])

    def phase_l0_proj(self, hsrc, w_in, qT, kT, vtok, hqT, kfT, gfT, vitok, gateT):
        P = self.P
        blocks = all_blocks(True)
        with contextlib.ExitStack() as ust:
          U = self.modulate_all(ust, hsrc, 0, self.Am, 0, blocks)
          if self.stop_after == "l0mod":
              return
          with self.phase("l0proj"):
            wring = Ring(self, [128, KC, 128], BF16, 2)
            psr = Ring(self, [128, 512], F32, 2, psum=True)
            ps2 = Ring(self, [128, 512], F32, 1, psum=True)
            oring = Ring(self, [128, 512], BF16, 3)
            o32 = Ring(self, [128, 512], F32, 2)
            f32r = Ring(self, [128, 512], F32, 2)
            sqr = Ring(self, [128, 512], BF16, 2)
            rr = Ring(self, [128, 512], F32, 2)
            fm = list(range(0, 8)) + list(range(12, 24)) + list(range(28, 32))
            import os as _os
            _skip = _os.environ.get("KSKIP", "")
            if "fm" in _skip:
                fm = []
            if "q" in _skip:
                fm = [j for j in fm if j >= 8]
            if "g" in _skip:
                fm = [j for j in fm if not (16 <= j < 24)]
            if "s" in _skip:
                fm = [j for j in fm if not (12 <= j < 16 or j >= 28)]
            for j in fm:
                wt, b_wt = wring.next()
                self.load_w_chunk(wt, b_wt, w_in[:, :, j * 128:(j + 1) * 128])
                for (s, kind, off, n, mc) in blocks:
                    ps, b_ps = psr.next()
                    for kc in range(KC):
                        self.mm(ps[:, 0:n], wt[:, kc, :], U[:, kc, off:off + n], kc == 0, kc == KC - 1, [b_wt], [b_ps])
                    if j < 8:
                        sq, b_sq = sqr.next()
                        self.act(sq[:, 0:n], ps[:, 0:n], AF.Square, [b_ps], [b_sq])
                        raw, b_raw = f32r.next()
                        self.cp("vector", raw[:, 0:n], ps[:, 0:n], [b_ps], [b_raw])
                        p2, b_p2 = ps2.next()
                        self.mm(p2[:, 0:n], self.bones, sq[:, 0:n], True, True, [b_sq], [b_p2])
                        r, b_r = rr.next()
                        self.rsqrt_from(r[:, 0:n], p2[:, 0:n], 1.0 / 64, [b_p2], b_r)
                        ot, b_ot = oring.next()
                        gi = 0 if j < 4 else 1
                        self.stt("vector", ot[:, 0:n], raw[:, 0:n], self.nag[:, gi:gi + 1], r[:, 0:n], ALU.mult, ALU.mult,
                                 [b_raw, b_r], [b_ot])
                        dst = qT if j < 4 else kT
                        P.dma("sync", dst[j % 4, :, off:off + n], ot[:, 0:n], reads=[b_ot])
                    elif 12 <= j < 16 or j >= 28:
                        ot, b_ot = oring.next()
                        self.act(ot[:, 0:n], ps[:, 0:n], AF.Silu, [b_ps], [b_ot])
                        dst = hqT if j < 16 else gateT
                        P.dma("sync", dst[j % 4, :, off:off + n], ot[:, 0:n], reads=[b_ot])
                    else:
                        dr = 0 if j < 20 else 1
                        jj = j % 4
                        f, b_f = f32r.next()
                        self.act(f[:, 0:n], ps[:, 0:n], AF.Sigmoid, [b_ps], [b_f])
                        self.ts("vector", f[:, 0:n], f[:, 0:n], self.omlv[:, dr, jj:jj + 1], self.lbv[:, dr, jj:jj + 1],
                                ALU.mult, ALU.add, [b_f], [b_f])
                        g, b_g = o32.next()
                        self.act(g[:, 0:n], f[:, 0:n], AF.Ln, [b_f], [b_g])
                        P.dma("sync", gfT[dr, jj, :, off:off + n], g[:, 0:n], reads=[b_g])
                        ot, b_ot = oring.next()
                        self.ts("vector", ot[:, 0:n], f[:, 0:n], -1.0, 1.0, ALU.mult, ALU.add, [b_f], [b_ot])
                        P.dma("sync", kfT[dr, jj, :, off:off + n], ot[:, 0:n], reads=[b_ot])
            if "tm" not in _skip:
                self.proj_tokmajor(U, blocks, [(w_in[:, :, 1024:1536], vtok, 0, 512), (w_in[:, :, 3072:3584], vitok, 0, 512)], psr)

    def proj_tokmajor(self, U, blocks, specs, psr):
        P = self.P
        st_ring = Ring(self, [64, 8, 512], BF16, 2)
        for (wsrc, dst, c0, ncols) in specs:
            wt = self.T([128, KC, 512], BF16)
            b_wt = P.buf()
            self.load_w_chunk(wt, b_wt, wsrc)
            for (s, kind, off, n, mc) in blocks:
                stg, b_stg = st_ring.next()
                nr = n // 64
                for r in range(nr):
                    ps, b_ps = psr.next()
                    for kc in range(KC):
                        self.mm(ps[0:64, 0:512], U[:, kc, off + r * 64:off + (r + 1) * 64], wt[:, kc, :], kc == 0, kc == KC - 1,
                                [b_wt], [b_ps])
                    self.cp("scalar" if r % 2 == 0 else "vector", stg[:, r, :], ps[0:64, 0:512], [b_ps], [b_stg])
                r0 = off // 64
                P.dma("sync", dst[r0:r0 + nr, :, c0:c0 + ncols].rearrange("r p f -> p r f"), stg[:, 0:nr, :], reads=[b_stg])

    def phase_na(self, qT, kT, vtok, mixT):
        P = self.P
        with self.phase("na"):
            qring = Ring(self, [128, SEQT], BF16, 2)
            kring = Ring(self, [128, SEQT], BF16, 2)
            vring = Ring(self, [64, 36, 128], BF16, 2)
            sps = Ring(self, [128, 512], F32, 4, psum=True)
            ops_ = Ring(self, [128, 512], F32, 2, psum=True)
            dps = Ring(self, [128, 512], F32, 2, psum=True)
            ering = Ring(self, [64, 512], BF16, 5)
            pring = Ring(self, [64, 512], BF16, 7)
            rdr = Ring(self, [64, 512], F32, 2)
            ostr = Ring(self, [64, 512], BF16, 2)
            EBv = self.EB.rearrange("p (h j c) -> p h j c", h=8, j=15)

            def r0(r):
                return min(max(r - 4, 0), 24)

            for s in range(2):
                for c in range(4):
                    q2, b_q = qring.next()
                    k2, b_k = kring.next()
                    v2, b_v = vring.next()
                    P.dma("sync", q2, qT[c, :, s * SEQT:(s + 1) * SEQT], writes=[b_q])
                    P.dma("sync", k2, kT[c, :, s * SEQT:(s + 1) * SEQT], writes=[b_k])
                    P.dma("sync", v2, vtok[s * 36:(s + 1) * 36, :, c * 128:(c + 1) * 128].rearrange("r p f -> p r f"), writes=[b_v])
                    items = []
                    for hh in range(2):
                        h = 2 * c + hh
                        pb = 64 * hh
                        groups = [("c", 0, LCTX, None)] + [("l", LCTX + 512 * g, 512, g) for g in range(4)]
                        for (gk, qoff, nq_g, g) in groups:
                            chunks = [("c", m, 0, nq_g, 0) for m in range(4)]
                            if gk == "l":
                                rows = list(range(8 * g, 8 * g + 8))
                                for rp in range(32):
                                    rin = [r for r in rows if r0(r) <= rp < r0(r) + 8]
                                    if rin:
                                        qlo, qhi = rin[0], rin[-1] + 1
                                        chunks.append(("l", rp, (qlo - 8 * g) * 64, (qhi - qlo) * 64, 7 - rp + qlo))
                            G = dict(hh=hh, h=h, pb=pb, qoff=qoff, nq_g=nq_g)
                            for ci_, ch_ in enumerate(chunks):
                                items.append(dict(G=G, chunk=ch_, first=(ci_ == 0), last=(ci_ == len(chunks) - 1)))

                    def front(it):
                        G = it["G"]
                        (ck, idx, c0, nq, jlo) = it["chunk"]
                        pb, h, qoff = G["pb"], G["h"], G["qoff"]
                        if it["first"]:
                            G["o"] = ops_.next()
                            G["d"] = dps.next()
                        koff = idx * 64 if ck == "c" else LCTX + idx * 64
                        s_ps, b_s = sps.next()
                        self.mm(s_ps[0:64, 0:nq], k2[pb:pb + 64, koff:koff + 64], q2[pb:pb + 64, qoff + c0:qoff + c0 + nq],
                                True, True, [b_k, b_q], [b_s])
                        pT, b_p = pring.next()
                        if ck == "c":
                            self.act(pT[:, 0:nq], s_ps[0:64, 0:nq], AF.Exp, [b_s], [b_p], scale=0.125)
                        else:
                            e_t, b_e = ering.next()
                            self.act(e_t[:, 0:nq], s_ps[0:64, 0:nq], AF.Exp, [b_s], [b_e], scale=0.125)
                            nj = nq // 64
                            self.tt("vector", pT[:, 0:nq], e_t[:, 0:nq],
                                    EBv[:, h, jlo:jlo + nj, :].rearrange("p j c -> p (j c)"), ALU.mult,
                                    [b_e, self.b_cst], [b_p])
                        it["pT"], it["b_p"] = pT, b_p

                    def back(it):
                        G = it["G"]
                        (ck, idx, c0, nq, jlo) = it["chunk"]
                        hh, pb, qoff, nq_g = G["hh"], G["pb"], G["qoff"], G["nq_g"]
                        o_ps, b_o = G["o"]
                        d_ps, b_d = G["d"]
                        pT, b_p = it["pT"], it["b_p"]
                        vrow = idx if ck == "c" else 4 + idx
                        self.mm(o_ps[0:64, c0:c0 + nq], v2[:, vrow, hh * 64:(hh + 1) * 64], pT[:, 0:nq], it["first"], it["last"],
                                [b_v, b_p], [b_o])
                        self.mm(d_ps[0:64, c0:c0 + nq], self.ones[0:64, 0:64], pT[:, 0:nq], it["first"], it["last"], [b_p], [b_d])
                        if it["last"]:
                            rd, b_rd = rdr.next()
                            P.op("vector", (lambda rd=rd, d_ps=d_ps, n=nq_g: lambda e: e.reciprocal(rd[:, 0:n], d_ps[0:64, 0:n]))(),
                                 [b_d], [b_rd])
                            ot, b_ot = ostr.next()
                            self.tt("vector", ot[:, 0:nq_g], o_ps[0:64, 0:nq_g], rd[:, 0:nq_g], ALU.mult, [b_o, b_rd], [b_ot])
                            P.dma("gpsimd", mixT[c, pb:pb + 64, s * SEQT + qoff:s * SEQT + qoff + nq_g], ot[:, 0:nq_g], reads=[b_ot])

                    LAG = 3
                    for t_ in range(len(items) + LAG):
                        if t_ < len(items):
                            front(items[t_])
                        if t_ >= LAG:
                            back(items[t_ - LAG])

    def scan_setup(self, nchains, dv):
        P = self.P
        W = {}
        W["G"] = Ring(self, [128, 512], F32, 2)
        W["X"] = Ring(self, [128, 512], F32, 2)
        W["Xc"] = Ring(self, [128, 512], F32, 2)
        W["qe"] = Ring(self, [128, 512], F32, 2)
        W["ke"] = Ring(self, [128, 512], F32, 2)
        W["qt"] = Ring(self, [128, 512], BF16, nchains + 1)
        W["kt"] = Ring(self, [128, 512], BF16, nchains + 1)
        W["q2"] = Ring(self, [128, 512], BF16, nchains + 1)
        W["sc3"] = Ring(self, [128, 3, 8], F32, nchains + 1)
        W["esc"] = Ring(self, [128, 3, 8], F32, nchains + 1)
        W["aT"] = Ring(self, [64, 64], BF16, 4)
        W["kT"] = Ring(self, [64, 128], BF16, 4)
        W["tmp"] = Ring(self, [128, dv], F32, 4)
        W["a_ps"] = Ring(self, [128, 512], F32, 2, psum=True)
        W["t_ps"] = Ring(self, [128, 512], BF16, 2, psum=True)
        W["kv_ps"] = Ring(self, [128, 512], F32, 2, psum=True)
        W["o_ps"] = Ring(self, [128, 512], F32, nchains, psum=True)
        W["LT"] = Ring(self, [64, 8, 64], BF16, nchains + 1)
        W["Dm"] = Ring(self, [64, 8, 64], F32, 2)
        W["xcol"] = Ring(self, [64, 8], F32, 2)
        W["kTall"] = Ring(self, [64, 8, 128], BF16, nchains + 1)
        W["aTall"] = Ring(self, [64, 8, 64], BF16, nchains + 1)
        W["chains"] = []
        for i in range(nchains):
            S = self.T([128, dv], F32)
            Sbf = self.T([128, dv], BF16)
            W["chains"].append(dict(S=S, Sbf=Sbf, bS=P.buf(), bSbf=P.buf()))
        return W

    def chain_reset(self, ch):
        S, Sbf = ch["S"], ch["Sbf"]
        self.P.op("vector", lambda e: e.memset(S, 0.0), writes=[ch["bS"]])
        ch["has_state"] = False

    def scan_prep(self, W, ch, q_ap, k_ap, g_ap, rd, vrows, n, dv, sigma, o_out, pbo=0, scalar=False):
        P = self.P
        nch = n // 64
        G, bG = W["G"].next()
        P.op("vector", lambda e: e.tensor_tensor_scan(G[:, 0:n], self.resetm[:, 0:n], g_ap, 0.0, ALU.mult, ALU.add),
             rd + [self.b_cst], [bG])
        Gv = G[:, 0:n].rearrange("p (c t) -> p c t", t=64)
        if sigma > 0:
            X, bX = G, bG
        else:
            X, bX = W["X"].next()
            self.tt("vector", X[:, 0:n], G[:, 0:n], g_ap, ALU.subtract, [bG] + rd, [bX])
        Xv = X[:, 0:n].rearrange("p (c t) -> p c t", t=64)
        sc3, b_sc3 = W["sc3"].next()
        esc, b_esc = W["esc"].next()
        qt, b_qt = W["qt"].next()
        kt, b_kt = W["kt"].next()
        q2, b_q2 = W["q2"].next()
        LT = b_LT = None
        if scalar:
            totb = sc3[:, 2, 0:nch].unsqueeze(2).broadcast_to([128, nch, 64])
            self.cp("vector", sc3[:, 2, 0:nch], Gv[:, :, 63], [bG], [b_sc3])
            self.act(esc[:, 2, 0:nch], sc3[:, 2, 0:nch], AF.Exp, [b_sc3], [b_esc])
            cD = esc[:, 2, :]
            cK = None
            Xc, bXc = W["Xc"].next()
            self.tt("vector", Xc[:, 0:n].rearrange("p (c t) -> p c t", t=64), totb, Xv, ALU.subtract, [bX, b_sc3], [bXc])
            qe, b_qe = W["qe"].next()
            ke, b_ke = W["ke"].next()
            if sigma > 0:
                self.act(qe[:, 0:n], X[:, 0:n], AF.Exp, [bX], [b_qe])
                self.act(ke[:, 0:n], Xc[:, 0:n], AF.Exp, [bXc], [b_ke])
            else:
                self.act(qe[:, 0:n], Xc[:, 0:n], AF.Exp, [bXc], [b_qe])
                self.act(ke[:, 0:n], X[:, 0:n], AF.Exp, [bX], [b_ke])
            self.tt("vector", q2[:, 0:n], q_ap, qe[:, 0:n], ALU.mult, rd + [b_qe], [b_q2])
            self.tt("vector", kt[:, 0:n], k_ap, ke[:, 0:n], ALU.mult, rd + [b_ke], [b_kt])
            self.cp("vector", qt[:, 0:n], k_ap, rd, [b_qt])
            Dm, b_Dm = W["Dm"].next()
            xcol, b_xc = W["xcol"].next()
            X64 = X[0:64, 0:n].rearrange("p (c t) -> p c t", t=64)
            self.tt("vector", Dm[:, 0:nch, :], X64, self.identf.unsqueeze(1).broadcast_to([64, nch, 64]), ALU.mult,
                    [bX, self.b_cst], [b_Dm])
            P.op("vector", (lambda xcol=xcol, Dm=Dm, nch=nch: lambda e: e.tensor_reduce(
                xcol[:, 0:nch], Dm[:, 0:nch, :], mybir.AxisListType.X, ALU.add))(), [b_Dm], [b_xc])
            self.tt("vector", Dm[:, 0:nch, :], X64, xcol[:, 0:nch].unsqueeze(2).broadcast_to([64, nch, 64]), ALU.subtract,
                    [bX, b_xc, b_Dm], [b_Dm])
            self.ts("vector", Dm[:, 0:nch, :], Dm[:, 0:nch, :], float(sigma), 0.0, ALU.mult, ALU.min, [b_Dm], [b_Dm])
            self.act(Dm[:, 0:nch, :], Dm[:, 0:nch, :], AF.Exp, [b_Dm], [b_Dm])
            LT, b_LT = W["LT"].next()
            mk = self.masks[:, 0, :] if sigma > 0 else self.masks[:, 1, :]
            self.tt("vector", LT[:, 0:nch, :], Dm[:, 0:nch, :], mk.unsqueeze(1).broadcast_to([64, nch, 64]), ALU.mult,
                    [b_Dm, self.b_cst], [b_LT])
        else:
            self.cp("vector", sc3[:, 0, 0:nch], Xv[:, :, 32], [bX], [b_sc3])
            self.cp("vector", sc3[:, 2, 0:nch], Gv[:, :, 63], [bG, b_sc3], [b_sc3])
            self.tt("vector", sc3[:, 1, 0:nch], sc3[:, 2, 0:nch], sc3[:, 0, 0:nch], ALU.subtract, [b_sc3], [b_sc3])
            self.act(esc[:, :, 0:nch], sc3[:, :, 0:nch], AF.Exp, [b_sc3], [b_esc])
            if sigma > 0:
                cS, cK = esc[:, 0, :], esc[:, 1, :]
            else:
                cS, cK = esc[:, 1, :], esc[:, 0, :]
            cD = esc[:, 2, :]
            Xc, bXc = W["Xc"].next()
            self.tt("vector", Xc[:, 0:n].rearrange("p (c t) -> p c t", t=64), Xv,
                    sc3[:, 0, 0:nch].unsqueeze(2).broadcast_to([128, nch, 64]), ALU.subtract, [bX, b_sc3], [bXc])
            qe, b_qe = W["qe"].next()
            ke, b_ke = W["ke"].next()
            self.act(qe[:, 0:n], Xc[:, 0:n], AF.Exp, [bXc], [b_qe], scale=float(sigma))
            self.act(ke[:, 0:n], Xc[:, 0:n], AF.Exp, [bXc], [b_ke], scale=float(-sigma))
            self.tt("vector", qt[:, 0:n], q_ap, qe[:, 0:n], ALU.mult, rd + [b_qe], [b_qt])
            self.tt("vector", kt[:, 0:n], k_ap, ke[:, 0:n], ALU.mult, rd + [b_ke], [b_kt])
            self.tt("vector", q2[:, 0:n].rearrange("p (c t) -> p c t", t=64), qt[:, 0:n].rearrange("p (c t) -> p c t", t=64),
                    cS[:, 0:nch].unsqueeze(2).broadcast_to([128, nch, 64]), ALU.mult, [b_qt, b_esc], [b_q2])
        tps_, b_tps = W["t_ps"].next()
        for i_ in range(nch):
            self.tr(tps_[0:64, i_ * 128:(i_ + 1) * 128], kt[:, i_ * 64:(i_ + 1) * 64], self.ident, [b_kt, self.b_cst], [b_tps])
        kTall, b_kTall = W["kTall"].next()
        self.cp("scalar", kTall[:, 0:nch, :], tps_[0:64, 0:nch * 128].rearrange("p (c f) -> p c f", f=128), [b_tps], [b_kTall])
        aTall = b_aTall = None
        if (not scalar) and o_out is not None:
            aps_, b_aps = W["a_ps"].next()
            for i_ in range(nch):
                cs_ = slice(i_ * 64, (i_ + 1) * 64)
                self.mm(aps_[0:64, cs_], kt[:, cs_], qt[:, cs_], True, True, [b_kt, b_qt], [b_aps])
            aTall, b_aTall = W["aTall"].next()
            mk_ = self.masks[:, 0, :] if sigma > 0 else self.masks[:, 1, :]
            self.tt("vector", aTall[:, 0:nch, :], aps_[0:64, 0:nch * 64].rearrange("p (c t) -> p c t", t=64),
                    mk_.unsqueeze(1).broadcast_to([64, nch, 64]), ALU.mult, [b_aps, self.b_cst], [b_aTall])
        o_ps, b_o = W["o_ps"].next()
        order = list(range(nch)) if sigma > 0 else list(range(nch - 1, -1, -1))
        mask = self.masks[:, 0, :] if sigma > 0 else self.masks[:, 1, :]
        return dict(W=W, ch=ch, n=n, dv=dv, pbo=pbo, vrows=vrows, o_out=o_out, qt=qt, b_qt=b_qt, kt=kt, b_kt=b_kt, q2=q2,
                    b_q2=b_q2, esc=esc, b_esc=b_esc, cK=cK, cD=cD, o_ps=o_ps, b_o=b_o, order=order, mask=mask,
                    scalar=scalar, LT=LT, b_LT=b_LT, q_ap=q_ap, rd=rd, kTall=kTall, b_kTall=b_kTall, aTall=aTall, b_aTall=b_aTall)

    def scan_s1(self, C, k):
        W = C["W"]
        i = C["order"][k]
        cs = slice(i * 64, (i + 1) * 64)
        qt, kt = C["qt"], C["kt"]
        b_qt, b_kt = C["b_qt"], C["b_kt"]
        U = {}
        if C["o_out"] is not None and C["aTall"] is None:
            a_ps, b_a = W["a_ps"].next()
            if C["scalar"]:
                self.mm(a_ps[0:64, 0:64], qt[:, cs], C["q_ap"][:, cs], True, True, [b_qt] + C["rd"], [b_a])
            else:
                self.mm(a_ps[0:64, 0:64], kt[:, cs], qt[:, cs], True, True, [b_kt, b_qt], [b_a])
            U["a"] = (a_ps, b_a)
        C.setdefault("units", {})[k] = U

    def scan_s2(self, C, k):
        W = C["W"]
        i = C["order"][k]
        U = C["units"][k]
        if C["o_out"] is not None and C["aTall"] is None:
            a_ps, b_a = U["a"]
            aT, b_aT = W["aT"].next()
            if C["scalar"]:
                self.tt("vector", aT, a_ps[0:64, 0:64], C["LT"][:, i, :], ALU.mult, [b_a, C["b_LT"]], [b_aT])
            else:
                self.tt("vector", aT, a_ps[0:64, 0:64], C["mask"], ALU.mult, [b_a, self.b_cst], [b_aT])
            U["aT"] = (aT, b_aT)

    def scan_s3(self, C, k):
        W, ch = C["W"], C["ch"]
        dv, pbo = C["dv"], C["pbo"]
        i = C["order"][k]
        cs = slice(i * 64, (i + 1) * 64)
        U = C["units"].pop(k)
        q2, b_q2, b_esc = C["q2"], C["b_q2"], C["b_esc"]
        S, Sbf, bS, bSbf = ch["S"], ch["Sbf"], ch["bS"], ch["bSbf"]
        o_ps, b_o = C["o_ps"], C["b_o"]
        vr, b_vr = C["vrows"](i)
        if C["o_out"] is not None:
            if C["aTall"] is not None:
                aT, b_aT = C["aTall"][:, i, :], C["b_aTall"]
            else:
                aT, b_aT = U["aT"]
            self.mm(o_ps[pbo:pbo + dv, cs], vr, aT, True, not ch["has_state"], [b_vr, b_aT], [b_o])
            if ch["has_state"]:
                self.mm(o_ps[pbo:pbo + dv, cs], Sbf, q2[:, cs], False, True, [bSbf, b_q2], [b_o])
        kT, b_kT = C["kTall"][:, i, :], C["b_kTall"]
        kv_ps, b_kv = W["kv_ps"].next()
        self.mm(kv_ps[:, 0:dv], kT, vr, True, True, [b_kT, b_vr], [b_kv])
        if C["scalar"]:
            self.stt("vector", S, S, C["cD"][:, i:i + 1], kv_ps[:, 0:dv], ALU.mult, ALU.add, [bS, b_esc, b_kv], [bS])
        else:
            tmp, b_tmp = W["tmp"].next()
            self.ts("vector", tmp, kv_ps[:, 0:dv], C["cK"][:, i:i + 1], None, ALU.mult, None, [b_kv, b_esc], [b_tmp])
            self.stt("vector", S, S, C["cD"][:, i:i + 1], tmp, ALU.mult, ALU.add, [bS, b_esc, b_tmp], [bS])
        self.cp("scalar", Sbf, S, [bS], [bSbf])
        ch["has_state"] = True

    def scan_finish(self, C):
        if C["o_out"] is not None:
            oa, b_oa = C["o_out"]
            pbo, dv, n = C["pbo"], C["dv"], C["n"]
            self.tt("vector", oa, oa, C["o_ps"][pbo:pbo + dv, 0:n], ALU.add, [C["b_o"], b_oa], [b_oa])

    def scan_step(self, preps):
        nch = preps[0]["n"] // 64
        units = [(C, k) for k in range(nch) for C in preps]
        for idx, (C, k) in enumerate(units):
            if idx == 0:
                self.scan_s1(C, k)
            self.scan_s2(C, k)
            if idx + 1 < len(units):
                self.scan_s1(*units[idx + 1])
            self.scan_s3(C, k)
        for C in preps:
            self.scan_finish(C)

    def phase_gla(self, hqT, kfT, gfT, vitok, gateT, mixT):
        P = self.P
        with self.phase("gla"):
            NCH = 2
            W = self.scan_setup(NCH, 128)
            qb_r = Ring(self, [128, 512], BF16, 2 * NCH)
            kb_r = Ring(self, [128, 512], BF16, 2 * NCH)
            gb_r = Ring(self, [128, 512], F32, 2 * NCH)
            vb_r = Ring(self, [64, 8, 128], BF16, 2 * NCH)
            gtr = Ring(self, [128, 512], BF16, 3)
            oacc_r = Ring(self, [128, SEQT], F32, 4)
            sqr = Ring(self, [128, 512], BF16, 2)
            rr = Ring(self, [128, 512], F32, 2)
            t32 = Ring(self, [128, 512], F32, 2)
            yo = Ring(self, [128, 512], BF16, 2)
            nps = W["a_ps"]
            fwd = [(0, LCTX)] + [(LCTX + 512 * b, 512) for b in range(4)]
            bwd = [(0, LCTX)] + [(LCTX + 512 * b, 512) for b in (3, 2, 1, 0)]
            for s in range(2):
                for hp in range(4):
                    heads = (hp,)
                    oaccs = {}
                    for hd in heads:
                        oacc, b_oa = oacc_r.next()
                        P.op("gpsimd", (lambda oacc=oacc: lambda e: e.memset(oacc, 0.0))(), writes=[b_oa])
                        oaccs[hd] = (oacc, b_oa)
                    specs = []
                    for hd in heads:
                        specs.append((hd, 0, +1))
                        specs.append((hd, 1, -1))
                    for ci in range(NCH):
                        self.chain_reset(W["chains"][ci])
                    for step in range(5):
                        preps = []
                        for ci, (hd, dr, sg) in enumerate(specs):
                            ch = W["chains"][ci]
                            to, n = (fwd if sg > 0 else bwd)[step]
                            off = s * SEQT + to
                            nr = n // 64
                            q, b_q = qb_r.next()
                            k, b_k = kb_r.next()
                            g, b_g = gb_r.next()
                            v, b_v = vb_r.next()
                            P.dma("sync", q[:, 0:n], hqT[hd, :, off:off + n], writes=[b_q])
                            P.dma("sync", k[:, 0:n], kfT[dr, hd, :, off:off + n], writes=[b_k])
                            P.dma("sync", g[:, 0:n], gfT[dr, hd, :, off:off + n], writes=[b_g])
                            P.dma("sync", v[:, 0:nr, :], vitok[off // 64:off // 64 + nr, :, hd * 128:(hd + 1) * 128].rearrange("r p f -> p r f"),
                                  writes=[b_v])
                            oacc, b_oa = oaccs[hd]
                            preps.append(self.scan_prep(W, ch, q[:, 0:n], k[:, 0:n], g[:, 0:n], [b_q, b_k, b_g],
                                                        (lambda i, v=v, b_v=b_v: (v[:, i, :], b_v)), n, 128, sg,
                                                        (oacc[:, to:to + n], b_oa)))
                        self.scan_step(preps)
                    for hd in heads:
                        oacc, b_oa = oaccs[hd]
                        for (to, n) in fwd:
                            gt, b_gt = gtr.next()
                            P.dma("sync", gt[:, 0:n], gateT[hd, :, s * SEQT + to:s * SEQT + to + n], writes=[b_gt])
                            sq, b_sq = sqr.next()
                            self.act(sq[:, 0:n], oacc[:, to:to + n], AF.Square, [b_oa], [b_sq])
                            ps, b_ps = nps.next()
                            self.mm(ps[:, 0:n], self.ones, sq[:, 0:n], True, True, [b_sq], [b_ps])
                            r, b_r = rr.next()
                            self.rsqrt_from(r[:, 0:n], ps[:, 0:n], 1.0 / 128, [b_ps], b_r)
                            t, b_t = t32.next()
                            self.stt("vector", t[:, 0:n], oacc[:, to:to + n], self.hgg[:, 0:1], r[:, 0:n], ALU.mult, ALU.mult,
                                     [b_oa, b_r, self.b_cst], [b_t])
                            y, b_y = yo.next()
                            self.tt("vector", y[:, 0:n], t[:, 0:n], gt[:, 0:n], ALU.mult, [b_t, b_gt], [b_y])
                            P.dma("gpsimd", mixT[4 + hd, :, s * SEQT + to:s * SEQT + to + n], y[:, 0:n], reads=[b_y])

    def phase_outproj(self, l, w_out, nk, mixsrc, hsrc, hdst, with_ctx, outT):
        P = self.P
        with self.phase("outproj%d" % l):
            wt = self.T([128, nk, D], BF16)
            b_wt = P.buf()
            for kc in range(nk):
                P.dma("gpsimd", wt[:, kc, :], w_out[:, kc, :], writes=[b_wt])
            self.linear_residual(l, wt, b_wt, nk, mixsrc, hsrc, hdst, 16, with_ctx, outT)

    def linear_residual(self, l, wt, b_wt, nk, xsrc, hsrc, hdst, glo, with_ctx, outT):
        P = self.P
        xr = Ring(self, [128, nk, 512], BF16, 2)
        hr = Ring(self, [128, KC, 512], F32, 2)
        orr = Ring(self, [128, KC, 512], F32, 2)
        psr = Ring(self, [128, 512], F32, 2, psum=True)
        xv = xsrc.rearrange("c p t -> p c t")
        hv = hsrc.rearrange("c p t -> p c t")
        for (s, kind, off, n, mc) in all_blocks(with_ctx):
            xb, b_x = xr.next()
            P.dma("sync", xb[:, :, 0:n], xv[:, :, off:off + n], writes=[b_x])
            hb, b_h = hr.next()
            P.dma("sync", hb[:, :, 0:n], hv[:, :, off:off + n], writes=[b_h])
            ob, b_ob = orr.next()
            for i in range(KC):
                ps, b_ps = psr.next()
                for kc in range(nk):
                    self.mm(ps[:, 0:n], wt[:, kc, i * 128:(i + 1) * 128], xb[:, kc, 0:n], kc == 0, kc == nk - 1, [b_wt, b_x], [b_ps])
                self.stt("vector", ob[:, i, 0:n], ps[:, 0:n], self.modT[:, l, glo + i, mc:mc + 1], hb[:, i, 0:n], ALU.mult, ALU.add,
                         [b_ps, b_h], [b_ob])
            if outT is None:
                P.dma("gpsimd", hdst.rearrange("c p t -> p c t")[:, :, off:off + n], ob[:, :, 0:n], reads=[b_ob])
            else:
                oo = s * LLAT + (off - s * SEQT - LCTX)
                P.dma("gpsimd", outT.rearrange("c p t -> p c t")[:, :, oo:oo + n], ob[:, :, 0:n], reads=[b_ob])

    def conv3(self, strip, b_strip, ln, wv, out, b_out):
        self.ts("vector", out, strip[:, 1:ln + 1], wv[:, 1:2], None, ALU.mult, None, [b_strip, self.b_cst], [b_out])
        self.stt("vector", out, strip[:, 0:ln], wv[:, 0:1], out, ALU.mult, ALU.add, [b_strip, b_out, self.b_cst], [b_out])
        self.stt("vector", out, strip[:, 2:ln + 2], wv[:, 2:3], out, ALU.mult, ALU.add, [b_strip, b_out, self.b_cst], [b_out])

    def phase_ffn(self, l, ffn_up, ffn_dn, guT, hsrc, hdst, with_ctx, outT):
        P = self.P
        blocks = all_blocks(with_ctx)
        with contextlib.ExitStack() as ust:
          U = self.modulate_all(ust, hsrc, l, self.Af, 24, blocks)
          with self.phase("ffn_up%d" % l):
            war = Ring(self, [128, KC, 128], BF16, 2)
            wvr = Ring(self, [128, KC, 128], BF16, 2)
            psr = Ring(self, [128, 512], F32, 4, psum=True)
            a_l = Ring(self, [128, LLAT + 2], F32, 2)
            a_c = Ring(self, [128, LCTX + 2], F32, 2)
            v_r = Ring(self, [128, LLAT], F32, 2)
            c_r = Ring(self, [128, LLAT], F32, 2)
            g_r = Ring(self, [128, LLAT], F32, 2)
            gu_r = Ring(self, [128, LLAT], BF16, 2)
            for rg in (a_l, a_c):
                for (ap, b) in rg.items:
                    P.op("vector", (lambda ap=ap: lambda e: e.memset(ap, 0.0))(), writes=[b])
            for j in range(NFF):
                wa, b_wa = war.next()
                wv, b_wv = wvr.next()
                self.load_w_chunk(wa, b_wa, ffn_up[l, :, :, j * 128:(j + 1) * 128])
                self.load_w_chunk(wv, b_wv, ffn_up[l, :, :, DFF + j * 128:DFF + (j + 1) * 128])
                for (s, kind, soff, ln, mc) in all_segments(with_ctx):
                    strip, b_st = (a_l if kind == "l" else a_c).next()
                    vv, b_vv = v_r.next()
                    for bo in range(0, ln, 512):
                        n = min(512, ln - bo)
                        off = soff + bo
                        ps, b_ps = psr.next()
                        for kc in range(KC):
                            self.mm(ps[:, 0:n], wa[:, kc, :], U[:, kc, off:off + n], kc == 0, kc == KC - 1, [b_wa], [b_ps])
                        self.cp("scalar", strip[:, 1 + bo:1 + bo + n], ps[:, 0:n], [b_ps], [b_st])
                        ps2, b_ps2 = psr.next()
                        for kc in range(KC):
                            self.mm(ps2[:, 0:n], wv[:, kc, :], U[:, kc, off:off + n], kc == 0, kc == KC - 1, [b_wv], [b_ps2])
                        self.cp("vector", vv[:, bo:bo + n], ps2[:, 0:n], [b_ps2], [b_vv])
                    cv, b_cv = c_r.next()
                    self.conv3(strip, b_st, ln, self.ffcw[:, l, j, :], cv[:, 0:ln], b_cv)
                    gl, b_gl = g_r.next()
                    self.act(gl[:, 0:ln], cv[:, 0:ln], AF.Gelu_apprx_tanh, [b_cv, self.b_cst], [b_gl], bias=self.ffcw[:, l, j, 3:4], scale=1.0)
                    gu, b_gu = gu_r.next()
                    self.tt("vector", gu[:, 0:ln], gl[:, 0:ln], vv[:, 0:ln], ALU.mult, [b_gl, b_vv], [b_gu])
                    P.dma("sync", guT[j, :, soff:soff + ln], gu[:, 0:ln], reads=[b_gu])
        with self.phase("ffn_dn%d" % l):
            wt = self.T([128, NFF, D], BF16)
            b_wt = P.buf()
            for kc in range(NFF):
                P.dma("gpsimd", wt[:, kc, :], ffn_dn[l, :, kc, :], writes=[b_wt])
            self.linear_residual(l, wt, b_wt, NFF, guT, hsrc, hdst, 40, with_ctx, outT)

    def phase_l1_proj(self, hsrc, w_in, zsT, xsT, xstok, bcT, btok, dltok):
        P = self.P
        blocks = all_blocks(True)
        with contextlib.ExitStack() as ust:
          U = self.modulate_all(ust, hsrc, 1, self.Am, 0, blocks)
          with self.phase("l1proj"):
            wring = Ring(self, [128, KC, 128], BF16, 2)
            psr = Ring(self, [128, 512], F32, 3, psum=True)
            tpr = Ring(self, [128, 512], BF16, 2, psum=True)
            oring = Ring(self, [128, 512], BF16, 3)
            o32 = Ring(self, [128, 512], F32, 2)
            a_l = Ring(self, [128, LLAT + 2], F32, 2)
            a_c = Ring(self, [128, LCTX + 2], F32, 2)
            c_r = Ring(self, [128, LLAT], F32, 2)
            s_r = Ring(self, [128, LLAT], BF16, 2)
            stg_r = Ring(self, [64, 32, 128], BF16, 2)
            for rg in (a_l, a_c):
                for (ap, b) in rg.items:
                    P.op("vector", (lambda ap=ap: lambda e: e.memset(ap, 0.0))(), writes=[b])
            for j in range(16):
                wt, b_wt = wring.next()
                self.load_w_chunk(wt, b_wt, w_in[:, :, j * 128:(j + 1) * 128])
                for (s, kind, off, n, mc) in all_blocks(False):
                    ps, b_ps = psr.next()
                    for kc in range(KC):
                        self.mm(ps[:, 0:n], wt[:, kc, :], U[:, kc, off:off + n], kc == 0, kc == KC - 1, [b_wt], [b_ps])
                    ot, b_ot = oring.next()
                    self.act(ot[:, 0:n], ps[:, 0:n], AF.Silu, [b_ps], [b_ot])
                    P.dma("sync", zsT[j, :, off:off + n], ot[:, 0:n], reads=[b_ot])
            wd = self.T([128, KC, 64], BF16)
            b_wd = P.buf()
            self.load_w_chunk(wd, b_wd, w_in[:, :, 5120:5184])
            dstg_r = Ring(self, [64, 8, 4, 64], F32, 2)
            dhb_r = Ring(self, [64, 64], BF16, 2)
            dtmp_r = Ring(self, [64, 64], F32, 2)
            dt8_r = Ring(self, [64, 8, 64], F32, 2)
            dhb8_r = Ring(self, [64, 8, 64], BF16, 2)
            for (s, kind, off, n, mc) in blocks:
                dstg, b_dstg = dstg_r.next()
                nr = n // 64
                ps, b_ps = psr.next()
                for r in range(nr):
                    for kc in range(KC):
                        self.mm(ps[0:64, r * 64:(r + 1) * 64], U[:, kc, off + r * 64:off + (r + 1) * 64], wd[:, kc, :], kc == 0, kc == KC - 1,
                                [b_wd], [b_ps])
                dt_, b_dt_ = dt8_r.next()
                self.tt("vector", dt_[:, 0:nr, :], ps[0:64, 0:nr * 64].rearrange("p (r f) -> p r f", f=64),
                        self.dtb[0:64, :].unsqueeze(1).broadcast_to([64, nr, 64]), ALU.add, [b_ps, self.b_cst], [b_dt_])
                self.act(dt_[:, 0:nr, :], dt_[:, 0:nr, :], AF.Exp, [b_dt_], [b_dt_])
                self.act(dstg[:, 0:nr, 0, :], dt_[:, 0:nr, :], AF.Ln, [b_dt_, self.b_cst], [b_dstg], bias=self.one_t[0:64, 0:1], scale=1.0)
                self.tt("gpsimd", dstg[:, 0:nr, 1, :], dstg[:, 0:nr, 0, :], self.aneg[0:64, :].unsqueeze(1).broadcast_to([64, nr, 64]),
                        ALU.mult, [b_dstg, self.b_cst], [b_dstg])
                hb, b_hb = dhb8_r.next()
                self.cp("gpsimd", hb[:, 0:nr, :], dstg[:, 0:nr, 1, :], [b_dstg], [b_hb])
                self.cp("gpsimd", dstg[:, 0:nr, 2, :], hb[:, 0:nr, :], [b_hb], [b_dstg])
                self.tt("gpsimd", dstg[:, 0:nr, 3, :], dstg[:, 0:nr, 1, :], dstg[:, 0:nr, 2, :], ALU.subtract, [b_dstg], [b_dstg])
                r0 = off // 64
                P.dma("sync", dltok[r0:r0 + nr].rearrange("r p a f -> p r a f"), dstg[:, 0:nr], reads=[b_dstg])
            pending = []
            for j in range(24):
                wt, b_wt = wring.next()
                self.load_w_chunk(wt, b_wt, w_in[:, :, 2048 + j * 128:2048 + (j + 1) * 128])
                for (s, kind, soff, ln, mc) in all_segments(True):
                    strip, b_st = (a_l if kind == "l" else a_c).next()
                    for bo in range(0, ln, 512):
                        n = min(512, ln - bo)
                        off = soff + bo
                        ps, b_ps = psr.next()
                        for kc in range(KC):
                            self.mm(ps[:, 0:n], wt[:, kc, :], U[:, kc, off:off + n], kc == 0, kc == KC - 1, [b_wt], [b_ps])
                        self.cp("scalar", strip[:, 1 + bo:1 + bo + n], ps[:, 0:n], [b_ps], [b_st])
                    while pending:
                        pending.pop(0)()
                    cv, b_cv = c_r.next()
                    self.conv3(strip, b_st, ln, self.sscw[:, j, :], cv[:, 0:ln], b_cv)
                    so, b_so = s_r.next()
                    self.act(so[:, 0:ln], cv[:, 0:ln], AF.Silu, [b_cv, self.b_cst], [b_so], bias=self.sscw[:, j, 3:4], scale=1.0)
                    if j < 16:
                        P.dma("sync", xsT[j, :, soff:soff + ln], so[:, 0:ln], reads=[b_so])
                    else:
                        P.dma("sync", bcT[j - 16, :, soff:soff + ln], so[:, 0:ln], reads=[b_so])
                    if j < 20:
                        def tok_copy(so=so, b_so=b_so, ln=ln, soff=soff, j=j):
                            stg, b_stg = stg_r.next()
                            nr = ln // 64
                            for rb in range(0, nr, 8):
                                nb_ = min(8, nr - rb)
                                tp, b_tp = tpr.next()
                                for r in range(rb, rb + nb_):
                                    self.tr(tp[0:64, (r - rb) * 128:(r - rb + 1) * 128], so[:, r * 64:(r + 1) * 64], self.ident,
                                            [b_so, self.b_cst], [b_tp])
                                self.cp("scalar" if (rb // 8) % 2 else "vector", stg[:, rb:rb + nb_, :],
                                        tp[0:64, 0:nb_ * 128].rearrange("p (r f) -> p r f", f=128), [b_tp], [b_stg])
                            r0 = soff // 64
                            tdst = xstok[r0:r0 + nr, :, j * 128:(j + 1) * 128] if j < 16 else btok[r0:r0 + nr, :, (j - 16) * 128:(j - 15) * 128]
                            P.dma("gpsimd", tdst.rearrange("r p f -> p r f"), stg[:, 0:nr, :], reads=[b_stg])
                        pending.append(tok_copy)
            while pending:
                pending.pop(0)()

    def phase_ssd2(self, xstok, btok, bcT, dltok, yfb):
        P = self.P
        with self.phase("ssd2"):
            LE, GE, GT, LT = [self.masks[:, i, :] for i in range(4)]
            bc_r = Ring(self, [128, 8, SEQT], BF16, 1)
            x_r = Ring(self, [64, 2048], BF16, 3)
            bt_r = Ring(self, [64, 512], BF16, 4)
            dl_r = Ring(self, [64, 4, 64], F32, 3)
            lahl_r = Ring(self, [128, 32], F32, 3)
            ula_r = Ring(self, [128, 2048], BF16, 2)
            mtall_r = Ring(self, [64, 2048], BF16, 3)
            e12_r = Ring(self, [64, 64], F32, 4)
            etot_r = Ring(self, [128, 32], F32, 4)
            w_r = Ring(self, [64, 32], F32, 3)
            cbm_r = Ring(self, [64, 256], F32, 4)
            ed_r = Ring(self, [64, 512], F32, 2)
            mt_r = Ring(self, [64, 512], BF16, 3)
            xdt_r = Ring(self, [64, 2048], BF16, 3)
            xw_r = Ring(self, [64, 2048], BF16, 4)
            tmp_r = Ring(self, [64, 512], F32, 2)
            yrow_r = Ring(self, [64, 2048], BF16, 2)
            sm_ps = Ring(self, [128, 512], F32, 1, psum=True)
            D_ps = Ring(self, [128, 512], F32, 2, psum=True)
            y_ps = Ring(self, [128, 512], F32, 2, psum=True)
            in_ps = Ring(self, [128, 512], F32, 1, psum=True)
            kv_ps = Ring(self, [128, 512], F32, 2, psum=True)
            ST = [self.T([128, 2048], F32) for _ in range(2)]
            STbf = [self.T([128, 2048], BF16) for _ in range(2)]
            for s in range(2):
                sl = slice(s * SEQT, (s + 1) * SEQT)
                bc, b_bc = bc_r.next()
                P.dma("sync", bc, bcT[:, :, sl].rearrange("c p t -> p c t"), writes=[b_bc])
                bST = [[P.buf() for _ in range(4)] for _ in range(2)]
                bSTbf = [[P.buf() for _ in range(4)] for _ in range(2)]
                has_state = [False, False]
                order = [list(range(36)), [3, 2, 1, 0] + list(range(35, 3, -1))]
                seq_items = []
                for k in range(36):
                    for d in range(2):
                        seq_items.append((d, order[d][k]))

                def stageA(d, row):
                    A1 = LE if d == 0 else GE
                    A2 = GT if d == 0 else LT
                    need_out = row >= 4
                    grow = s * 36 + row
                    tok0 = row * 64
                    x, b_x = x_r.next()
                    bt, b_bt = bt_r.next()
                    dl, b_dl = dl_r.next()
                    P.dma("sync", x, xstok[grow], writes=[b_x])
                    P.dma("sync", bt, btok[grow], writes=[b_bt])
                    P.dma("sync", dl, dltok[grow], writes=[b_dl])
                    dt_d = dl[:, 0, d * 32:(d + 1) * 32]
                    la_d = dl[:, 1, d * 32:(d + 1) * 32]
                    sp, b_sp = sm_ps.next()
                    self.mm(sp[0:64, 0:32], A1, la_d, True, True, [b_dl, self.b_cst], [b_sp])
                    self.mm(sp[0:64, 32:64], A2, la_d, True, True, [b_dl, self.b_cst], [b_sp])
                    self.mm(sp[0:128, 64:96], self.onesf, la_d, True, True, [b_dl, self.b_cst], [b_sp])
                    e12, b_e12 = e12_r.next()
                    etot, b_et = etot_r.next()
                    self.act(e12, sp[0:64, 0:64], AF.Exp, [b_sp], [b_e12])
                    self.act(etot, sp[0:128, 64:96], AF.Exp, [b_sp], [b_et])
                    w, b_w = w_r.next()
                    self.tt("vector", w, e12[:, 32:64], dt_d, ALU.mult, [b_e12, b_dl], [b_w])
                    xv = x.rearrange("p (h q) -> p h q", q=64)
                    xw, b_xw = xw_r.next()
                    self.tt("gpsimd", xw.rearrange("p (h q) -> p h q", q=64), xv, w.unsqueeze(2).broadcast_to([64, 32, 64]),
                            ALU.mult, [b_x, b_w], [b_xw])
                    C = dict(d=d, row=row, A2=A2, need_out=need_out, tok0=tok0, bt=bt, b_bt=b_bt, e12=e12, b_e12=b_e12,
                             etot=etot, b_et=b_et, xw=xw, b_xw=b_xw)
                    if need_out:
                        lahl, b_lahl = lahl_r.next()
                        P.dma("sync", lahl[0:64, :], dltok[grow, :, 2, d * 32:(d + 1) * 32], writes=[b_lahl])
                        P.dma("sync", lahl[64:128, :], dltok[grow, :, 3, d * 32:(d + 1) * 32], writes=[b_lahl])
                        ula, b_ula = ula_r.next()
                        self.tt("gpsimd", ula.rearrange("p (h t) -> p h t", t=64), lahl.unsqueeze(2).broadcast_to([128, 32, 64]),
                                self.masks2[:, d, :].unsqueeze(1).broadcast_to([128, 32, 64]), ALU.mult, [b_lahl, self.b_cst], [b_ula])
                        xdt, b_xdt = xdt_r.next()
                        self.tt("vector", xdt.rearrange("p (h q) -> p h q", q=64), xv, dt_d.unsqueeze(2).broadcast_to([64, 32, 64]),
                                ALU.mult, [b_x, b_dl], [b_xdt])
                        cp_, b_cp = sp[:, 128:384], b_sp
                        for g in range(4):
                            self.mm(cp_[0:64, g * 64:(g + 1) * 64], bc[:, g, tok0:tok0 + 64], bc[:, 4 + g, tok0:tok0 + 64], True, True,
                                    [b_bc], [b_cp])
                        cbm, b_cbm = cbm_r.next()
                        self.tt("vector", cbm.rearrange("p (g t) -> p g t", t=64), cp_[0:64, 0:256].rearrange("p (g t) -> p g t", t=64),
                                A1.unsqueeze(1).broadcast_to([64, 4, 64]), ALU.mult, [b_cp, self.b_cst], [b_cbm])
                        mt, b_mt = mtall_r.next()
                        for g in range(4):
                            gs = slice(g * 512, (g + 1) * 512)
                            Dp, b_Dp = D_ps.next()
                            self.mm(Dp[0:64, 0:512], self.masks2b[:, 2 + d, :], ula[:, gs], True, True, [b_ula, self.b_cst], [b_Dp])
                            ed, b_ed = ed_r.next()
                            self.act(ed, Dp[0:64, 0:512], AF.Exp, [b_Dp], [b_ed])
                            self.tt("vector", mt[:, gs].rearrange("p (h t) -> p h t", t=64), ed.rearrange("p (h t) -> p h t", t=64),
                                    cbm[:, g * 64:(g + 1) * 64].unsqueeze(1).broadcast_to([64, 8, 64]), ALU.mult, [b_ed, b_cbm], [b_mt])
                        C.update(xdt=xdt, b_xdt=b_xdt, mt=mt, b_mt=b_mt)
                    return C

                def stageB(C):
                    d, row, A2, tok0 = C["d"], C["row"], C["A2"], C["tok0"]
                    store = None
                    bt, b_bt, e12, b_e12, etot, b_et, xw, b_xw = (C["bt"], C["b_bt"], C["e12"], C["b_e12"], C["etot"], C["b_et"],
                                                                  C["xw"], C["b_xw"])
                    if C["need_out"]:
                        xdt, b_xdt, mt, b_mt = C["xdt"], C["b_xdt"], C["mt"], C["b_mt"]
                        yrow, b_yr = yrow_r.next()
                        for g in range(4):
                            gs = slice(g * 512, (g + 1) * 512)
                            yp, b_yp = y_ps.next()
                            for hh in range(8):
                                h = g * 8 + hh
                                self.mm(yp[0:64, hh * 64:(hh + 1) * 64], mt[:, h * 64:(h + 1) * 64], xdt[:, h * 64:(h + 1) * 64],
                                        True, True, [b_mt, b_xdt], [b_yp])
                            if has_state[d]:
                                ip, b_ip = in_ps.next()
                                self.mm(ip[0:64, 0:512], bc[:, 4 + g, tok0:tok0 + 64], STbf[d][:, gs], True, True,
                                        [b_bc, bSTbf[d][g]], [b_ip])
                                tmp, b_tmp = tmp_r.next()
                                self.tt("vector", tmp.rearrange("p (h q) -> p h q", q=64), ip[0:64, 0:512].rearrange("p (h q) -> p h q", q=64),
                                        e12[:, g * 8:(g + 1) * 8].unsqueeze(2).broadcast_to([64, 8, 64]), ALU.mult, [b_ip, b_e12], [b_tmp])
                                self.tt("vector", yrow[:, gs], tmp, yp[0:64, 0:512], ALU.add, [b_tmp, b_yp], [b_yr])
                            else:
                                self.cp("scalar", yrow[:, gs], yp[0:64, 0:512], [b_yp], [b_yr])
                        lrow = s * 32 + (row - 4)
                        store = (yfb[d, lrow], yrow, b_yr)
                    for g in range(4):
                        gs = slice(g * 512, (g + 1) * 512)
                        kp, b_kp = kv_ps.next()
                        self.mm(kp[:, 0:512], bt[:, g * 128:(g + 1) * 128], xw[:, gs], True, True, [b_bt, b_xw], [b_kp])
                        if has_state[d]:
                            stv = ST[d][:, gs].rearrange("p (h q) -> p h q", q=64)
                            self.tt("gpsimd", stv, stv, etot[:, g * 8:(g + 1) * 8].unsqueeze(2).broadcast_to([128, 8, 64]), ALU.mult,
                                    [bST[d][g], b_et], [bST[d][g]])
                            self.tt("vector", ST[d][:, gs], ST[d][:, gs], kp[:, 0:512], ALU.add, [bST[d][g], b_kp], [bST[d][g]])
                        else:
                            self.cp("vector", ST[d][:, gs], kp[:, 0:512], [b_kp], [bST[d][g]])
                        self.cp("scalar", STbf[d][:, gs], ST[d][:, gs], [bST[d][g]], [bSTbf[d][g]])
                    if store is not None:
                        P.dma("gpsimd", store[0], store[1], reads=[store[2]])
                    has_state[d] = True

                LAG = 2
                ctxs = {}
                for t_ in range(len(seq_items) + LAG):
                    if t_ < len(seq_items):
                        ctxs[t_] = stageA(*seq_items[t_])
                    if t_ >= LAG:
                        stageB(ctxs.pop(t_ - LAG))

    def phase_ssd_fin(self, zsT, xsT, yfb, mix2T):
        P = self.P
        with self.phase("ssdfin"):
            yf_r = Ring(self, [64, 2048], BF16, 3)
            yb_r = Ring(self, [64, 2048], BF16, 3)
            yT_r = Ring(self, [128, 16, 512], F32, 1)
            tp_ps = Ring(self, [128, 512], F32, 3, psum=True)
            n_ps = Ring(self, [128, 512], F32, 1, psum=True)
            xf_r = Ring(self, [128, 4, 512], BF16, 2)
            zf_r = Ring(self, [128, 4, 512], BF16, 2)
            y_r = Ring(self, [128, 4, 512], F32, 1)
            sq_r = Ring(self, [128, 4, 512], BF16, 1)
            r_r = Ring(self, [128, 512], F32, 2)
            yo_r = Ring(self, [128, 4, 512], BF16, 2)
            for s in range(2):
                for b in range(4):
                    to = LCTX + 512 * b
                    off = s * SEQT + to
                    yT, b_yT = yT_r.next()
                    for rr in range(8):
                        lrow = s * 32 + b * 8 + rr
                        yf, b_yf = yf_r.next()
                        yb, b_yb = yb_r.next()
                        P.dma("sync", yf, yfb[0, lrow], writes=[b_yf])
                        P.dma("sync", yb, yfb[1, lrow], writes=[b_yb])
                        for half in range(2):
                            tp, b_tp = tp_ps.next()
                            for c8 in range(8):
                                c = half * 8 + c8
                                self.mm(tp[:, c8 * 64:(c8 + 1) * 64], yf[:, c * 128:(c + 1) * 128], self.ident[0:64, 0:64], True, False,
                                        [b_yf, self.b_cst], [b_tp])
                                self.mm(tp[:, c8 * 64:(c8 + 1) * 64], yb[:, c * 128:(c + 1) * 128], self.ident[0:64, 0:64], False, True,
                                        [b_yb, self.b_cst], [b_tp])
                            self.cp("scalar" if half == 0 else "vector", yT[:, half * 8:(half + 1) * 8, rr * 64:(rr + 1) * 64],
                                    tp[:, 0:512].rearrange("p (c t) -> p c t", t=64), [b_tp], [b_yT])
                    for g in range(4):
                        xf, b_xf = xf_r.next()
                        zf, b_zf = zf_r.next()
                        P.dma("sync", xf, xsT[4 * g:4 * g + 4, :, off:off + 512].rearrange("c p t -> p c t"), writes=[b_xf])
                        P.dma("sync", zf, zsT[4 * g:4 * g + 4, :, off:off + 512].rearrange("c p t -> p c t"), writes=[b_zf])
                        y, b_y = y_r.next()
                        sq, b_sq = sq_r.next()
                        ps, b_ps = n_ps.next()
                        for pr in range(4):
                            cix = 4 * g + pr
                            self.stt("vector", y[:, pr, :], xf[:, pr, :], self.dsk[:, cix:cix + 1], yT[:, cix, :],
                                     ALU.mult, ALU.add, [b_xf, b_yT, self.b_cst], [b_y])
                        self.tt("gpsimd", y, y, zf, ALU.mult, [b_y, b_zf], [b_y])
                        self.act(sq, y, AF.Square, [b_y], [b_sq])
                        for pr in range(4):
                            self.mm(ps[:, 0:512], self.ones, sq[:, pr, :], pr == 0, pr == 3, [b_sq], [b_ps])
                        r, b_r = r_r.next()
                        self.rsqrt_from(r, ps[:, 0:512], 1.0 / 512, [b_ps], b_r)
                        yo, b_yo = yo_r.next()
                        for pr in range(4):
                            cix = 4 * g + pr
                            self.stt("vector", yo[:, pr, :], y[:, pr, :], self.sng[:, cix:cix + 1], r, ALU.mult, ALU.mult,
                                     [b_y, b_r, self.b_cst], [b_yo])
                        P.dma("gpsimd", mix2T[4 * g:4 * g + 4, :, off:off + 512].rearrange("c p t -> p c t"), yo, reads=[b_yo])

    def phase_ssd(self, zsT, xsT, xstok, bcT, dtT, mix2T):
        P = self.P
        with self.phase("ssd"):
            NCH = 4
            W = self.scan_setup(NCH, 64)
            Br = Ring(self, [128, SEQT], BF16, 1)
            Cr = Ring(self, [128, SEQT], BF16, 1)
            dtr = Ring(self, [64, SEQT], F32, 1)
            xtr = Ring(self, [64, 36, 512], BF16, 1)
            oaccs = [(self.T([128, SEQT], F32), None) for _ in range(4)]
            dps = W["a_ps"]
            dtb_r = Ring(self, [128, 512], F32, 2)
            la_r = Ring(self, [128, 512], F32, 2)
            kk_r = Ring(self, [128, 512], F32, 2)
            xf_r = Ring(self, [128, 4, 512], BF16, 1)
            zf_r = Ring(self, [128, 4, 512], BF16, 1)
            y_r = Ring(self, [128, 4, 512], F32, 1)
            sq_r = Ring(self, [128, 4, 512], BF16, 1)
            r_r = Ring(self, [128, 512], F32, 1)
            yo_r = Ring(self, [128, 4, 512], BF16, 1)
            fwd = [(0, LCTX)] + [(LCTX + 512 * b, 512) for b in range(4)]
            bwd = [(0, LCTX)] + [(LCTX + 512 * b, 512) for b in (3, 2, 1, 0)]
            for s in range(2):
                sl = slice(s * SEQT, (s + 1) * SEQT)
                dt, b_dt = dtr.next()
                P.dma("sync", dt, dtT[:, sl], writes=[b_dt])
                for g in range(4):
                    Bt, b_B = Br.next()
                    Ct, b_C = Cr.next()
                    xt, b_xt = xtr.next()
                    P.dma("sync", Bt, bcT[g, :, sl], writes=[b_B])
                    P.dma("sync", Ct, bcT[4 + g, :, sl], writes=[b_C])
                    P.dma("sync", xt, xstok[s * 36:(s + 1) * 36, :, g * 512:(g + 1) * 512].rearrange("r p f -> p r f"), writes=[b_xt])
                    b_oas = [P.buf() for _ in range(4)]
                    for pr in range(4):
                        P.op("gpsimd", (lambda t=oaccs[pr][0]: lambda e: e.memset(t, 0.0))(), writes=[b_oas[pr]])
                    jobs = []
                    for hh in range(8):
                        for (dr, sg) in ((0, +1), (1, -1)):
                            jobs.append((hh, dr, sg))
                    for j0 in range(0, len(jobs), NCH):
                        batch = jobs[j0:j0 + NCH]
                        for ci in range(len(batch)):
                            self.chain_reset(W["chains"][ci])
                        for step in range(5):
                            preps = []
                            for ci, (hh, dr, sg) in enumerate(batch):
                                ch = W["chains"][ci]
                                to, n = (fwd if sg > 0 else bwd)[step]
                                hidx = g * 8 + hh
                                col = dr * 32 + hidx
                                dtv, b_dtv = dtb_r.next()
                                P.dma("sync", dtv[:, 0:n], dtT[col:col + 1, s * SEQT + to:s * SEQT + to + n].partition_broadcast(128),
                                      writes=[b_dtv])
                                self.act(dtv[:, 0:n], dtv[:, 0:n], AF.Exp, [b_dtv, self.b_cst], [b_dtv], bias=self.dtb[:, col:col + 1], scale=1.0)
                                self.act(dtv[:, 0:n], dtv[:, 0:n], AF.Ln, [b_dtv, self.b_cst], [b_dtv], bias=self.one_t[:, 0:1], scale=1.0)
                                la, b_la = la_r.next()
                                self.ts("vector", la[:, 0:n], dtv[:, 0:n], self.aneg[:, col:col + 1], None, ALU.mult, None,
                                        [b_dtv, self.b_cst], [b_la])
                                kk, b_kk = kk_r.next()
                                self.tt("vector", kk[:, 0:n], Bt[:, to:to + n], dtv[:, 0:n], ALU.mult, [b_B, b_dtv], [b_kk])
                                pair = hh // 2
                                pbo = 64 * (hh % 2)
                                oacc = oaccs[pair][0]
                                o_out = None
                                if to >= LCTX:
                                    o_out = (oacc[pbo:pbo + 64, to:to + n], b_oas[pair])
                                preps.append(self.scan_prep(W, ch, Ct[:, to:to + n], kk[:, 0:n], la[:, 0:n], [b_C, b_kk, b_la],
                                                            (lambda i, to=to, xt=xt, b_xt=b_xt, hh=hh: (xt[:, to // 64 + i, hh * 64:(hh + 1) * 64], b_xt)),
                                                            n, 64, sg, o_out, pbo, scalar=True))
                            self.scan_step(preps)
                    for b in range(4):
                        to = LCTX + 512 * b
                        off = s * SEQT + to
                        xf, b_xf = xf_r.next()
                        zf, b_zf = zf_r.next()
                        P.dma("sync", xf, xsT[4 * g:4 * g + 4, :, off:off + 512].rearrange("c p t -> p c t"), writes=[b_xf])
                        P.dma("sync", zf, zsT[4 * g:4 * g + 4, :, off:off + 512].rearrange("c p t -> p c t"), writes=[b_zf])
                        y, b_y = y_r.next()
                        sq, b_sq = sq_r.next()
                        ps, b_ps = dps.next()
                        for pr in range(4):
                            cix = 4 * g + pr
                            self.stt("vector", y[:, pr, :], xf[:, pr, :], self.dsk[:, cix:cix + 1], oaccs[pr][0][:, to:to + 512],
                                     ALU.mult, ALU.add, [b_xf, b_oas[pr], self.b_cst], [b_y])
                        self.tt("vector", y, y, zf, ALU.mult, [b_y, b_zf], [b_y])
                        self.act(sq, y, AF.Square, [b_y], [b_sq])
                        for pr in range(4):
                            self.mm(ps[:, 0:512], self.ones, sq[:, pr, :], pr == 0, pr == 3, [b_sq], [b_ps])
                        r, b_r = r_r.next()
                        self.rsqrt_from(r, ps[:, 0:512], 1.0 / 512, [b_ps], b_r)
                        yo, b_yo = yo_r.next()
                        for pr in range(4):
                            cix = 4 * g + pr
                            self.stt("vector", yo[:, pr, :], y[:, pr, :], self.sng[:, cix:cix + 1], r, ALU.mult, ALU.mult,
                                     [b_y, b_r, self.b_cst], [b_yo])
                        P.dma("gpsimd", mix2T[4 * g:4 * g + 4, :, off:off + 512].rearrange("c p t -> p c t"), yo, reads=[b_yo])


def _fm(v, nchunk):
    return np.ascontiguousarray(np.asarray(v, np.float32).reshape(nchunk, 128).T)


def _wt(w):
    K, N = w.shape
    return np.ascontiguousarray(np.asarray(w, np.float32).reshape(K // 128, 128, N).transpose(1, 0, 2))


def host_inputs(I):
    f32 = np.float32
    sh = {}
    sh["wmod"] = np.stack([_wt(I["w_mod"][l]) for l in range(2)])
    sh["bmod"] = np.ascontiguousarray(np.asarray(I["b_mod"], f32).reshape(2, 48, 128).transpose(2, 0, 1))
    sh["nmix"] = np.ascontiguousarray(np.asarray(I["norm_mix"], f32).reshape(2, KC, 128).transpose(2, 0, 1))
    sh["nffn"] = np.ascontiguousarray(np.asarray(I["norm_ffn"], f32).reshape(2, KC, 128).transpose(2, 0, 1))
    sh["ffn_up"] = np.stack([_wt(I["ffn_w_up"][l]) for l in range(2)])
    cw = np.concatenate([np.asarray(I["ffn_conv_w"], f32), np.asarray(I["ffn_conv_b"], f32)[:, None, :]], axis=1)
    sh["ffn_cw"] = np.ascontiguousarray(cw.reshape(2, 4, NFF, 128).transpose(3, 0, 2, 1))
    sh["ffn_dn"] = np.stack([_wt(I["ffn_w_down"][l]) for l in range(2)])
    sh["hy_in"] = _wt(I["hy_w_in"][0])
    sh["hy_out"] = _wt(I["hy_w_out"][0])
    sh["na_g"] = np.ascontiguousarray(np.stack([np.tile(np.asarray(I["na_q_gain"][0], f32), 2),
                                                np.tile(np.asarray(I["na_k_gain"][0], f32), 2)], axis=1))
    rpb = np.asarray(I["na_rpb"][0], f32)
    cp = np.arange(64)[:, None]
    cq = np.arange(64)[None, :]
    dc = np.clip(cp - cq + 15, 0, 30)
    tab = rpb[:, ::-1, :][:, :, dc]
    sh["na_bias"] = np.ascontiguousarray(tab.transpose(2, 0, 1, 3).reshape(64, 8 * 15 * 64))
    ws = np.clip(np.arange(64) - 8, 0, 48)[None, :]
    inwin = (cp >= ws) & (cp < ws + 16)
    sh["na_mask"] = np.where(inwin, 0.0, -30000.0).astype(f32)
    sh["hg_gain"] = np.asarray(I["hg_out_gain"][0], f32).reshape(128, 1).copy()
    lb = np.stack([np.asarray(I["hg_lb_fwd"], f32), np.asarray(I["hg_lb_bwd"], f32)])
    sh["hg_lb"] = np.ascontiguousarray(lb.reshape(2, 3, 4, 128).transpose(3, 0, 1, 2))
    sh["ssd_in"] = _wt(I["ssd_w_in"][0])
    scw = np.concatenate([np.asarray(I["ssd_conv_w"][0], f32), np.asarray(I["ssd_conv_b"][0], f32)[None, :]], axis=0)
    sh["ssd_cw"] = np.ascontiguousarray(scw.reshape(4, 24, 128).transpose(2, 1, 0))
    dtb = np.concatenate([np.asarray(I["ssd_dt_bias_fwd"][0], f32), np.asarray(I["ssd_dt_bias_bwd"][0], f32)])
    sh["ssd_dtb"] = np.ascontiguousarray(np.broadcast_to(dtb[None, :], (128, 64)))
    al = np.concatenate([np.asarray(I["ssd_a_log_fwd"][0], f32), np.asarray(I["ssd_a_log_bwd"][0], f32)])
    sh["ssd_alog"] = np.ascontiguousarray(np.broadcast_to(al[None, :], (128, 64)))
    dsk = np.repeat(np.asarray(I["ssd_d"][0], f32), 64)
    sh["ssd_dsk"] = _fm(dsk, 16)
    sh["ssd_ng"] = _fm(I["ssd_norm_gain"][0], 16)
    sh["ssd_out"] = _wt(I["ssd_w_out"][0])
    sh["cst_ident"] = np.eye(128, dtype=f32)
    bo = np.zeros((128, 128), f32)
    bo[0:64, 0:64] = 1.0
    bo[64:128, 64:128] = 1.0
    sh["cst_bones"] = bo
    si = np.arange(64)[:, None]
    ti = np.arange(64)[None, :]
    sh["cst_masks"] = np.ascontiguousarray(np.stack([(si <= ti), (si >= ti), (si > ti), (si < ti)], axis=1).astype(f32))
    sh["cst_masks2"] = np.ascontiguousarray(np.concatenate([sh["cst_masks"], sh["cst_masks"]], axis=0))
    rm = np.ones((128, 512), f32)
    rm[:, ::64] = 0.0
    sh["cst_reset"] = rm
    per_core = []
    x = np.asarray(I["x"], f32)
    ctx = np.asarray(I["ctx"], f32)
    c = np.asarray(I["c"], f32)
    cc = np.asarray(I["c_ctx"], f32)
    for i in range(8):
        toks = np.concatenate([ctx[2 * i], x[2 * i], ctx[2 * i + 1], x[2 * i + 1]], axis=0)
        hin = np.ascontiguousarray(toks.T.reshape(KC, 128, NT))
        cm = np.stack([c[2 * i], c[2 * i + 1], cc, cc], axis=1)
        cTt = np.ascontiguousarray(cm.reshape(KC, 128, 4).transpose(1, 0, 2))
        d = dict(sh)
        d["hin"] = hin
        d["cT"] = cTt
        per_core.append(d)
    return per_core


_CACHE = {}


def build_program(dbg=(), stop_after=None):
    key = (tuple(sorted(dbg)), stop_after)
    if key not in _CACHE:
        kb = KB(dbg, stop_after)
        kb.build()
        _CACHE[key] = kb
    return _CACHE[key]


def kernel(**inputs):
    kb = build_program()
    in_maps = host_inputs(inputs)
    names = set(kb.inputs.keys())
    in_maps = [{k: v for k, v in m.items() if k in names} for m in in_maps]
    res = run_bass_kernel_spmd(kb.nc, in_maps, core_ids=list(range(8)))
    out = np.empty((16, LLAT, D), np.float32)
    for i in range(8):
        o = np.asarray(res.results[i]["outT"], np.float32).reshape(D, 2 * LLAT)
        out[2 * i] = o[:, 0:LLAT].T
        out[2 * i + 1] = o[:, LLAT:2 * LLAT].T
    return out
```
